# Optimizing a Trainium2 kernel written in Bass

```python
import jax, jax.numpy as jnp
from jax import lax
import numpy as np

D_MODEL = 1024
BATCH = 16
SEQ = 256
DEPTH = 4
DEC_BATCH = 4
DEC_SEQ = 4096
PAST_LEN = 512

GRID_W = 64
D_MIX = D_MODEL
N_DIR = 2
RW_HEADS = 4
RW_HEAD = 64
RW_W = RW_HEADS * RW_HEAD
RW_DECAY_RANK = 64
RW_A_RANK = 64
RW_GATE_RANK = 128
RW_GN_EPS = 64e-5
RW_IN = 3 * RW_W + N_DIR * RW_DECAY_RANK + N_DIR * RW_A_RANK + RW_GATE_RANK
MLA_HEADS = 8
MLA_NOPE = 64
MLA_ROPE = 32
MLA_V = 64
MLA_Q_RANK = 256
MLA_KV_RANK = 128
MLA_W = MLA_HEADS * MLA_V
MLA_SCALE = (MLA_NOPE + MLA_ROPE) ** -0.5
ROPE_BASE = 10000.0
ROPE_AXIS_PAIRS = MLA_ROPE // 4
Q_BLOCK = 128
GM_GROUPS = 4
GM_GROUP_W = 64
GM_W = GM_GROUPS * GM_GROUP_W
CHUNK = 128
N_IN = RW_IN + MLA_Q_RANK + MLA_KV_RANK + MLA_ROPE + 2 * GM_W
D_FF = -(-8 * D_MODEL // (3 * 256)) * 256
ALPHA = (2 * DEPTH) ** 0.25
BETA = (8 * DEPTH) ** -0.25
F32 = jnp.float32

kernel_name = 'hybrid_rwkv7_mla_gmlp_diffusion_step'


def _layer_norm(x, g, b, eps=1e-5):
    xf = x.astype(F32)
    mu = jnp.mean(xf, -1, keepdims=True)
    var = jnp.mean(jnp.square(xf - mu), -1, keepdims=True)
    return ((xf - mu) * lax.rsqrt(var + eps) * g + b).astype(x.dtype)


def _rms_norm(x, g, eps=1e-6):
    xf = x.astype(F32)
    return (xf * lax.rsqrt(jnp.mean(xf * xf, -1, keepdims=True) + eps) * g).astype(x.dtype)


def _axial_rope_tables(n_tok):
    rows = n_tok // GRID_W
    rr, cc = jnp.meshgrid(jnp.arange(rows), jnp.arange(GRID_W), indexing='ij')
    rr = rr.reshape(-1).astype(F32)
    cc = cc.reshape(-1).astype(F32)
    inv = ROPE_BASE ** (-jnp.arange(ROPE_AXIS_PAIRS, dtype=F32) / ROPE_AXIS_PAIRS)
    ang = jnp.concatenate([rr[:, None] * inv, cc[:, None] * inv], -1)
    return jnp.cos(ang), jnp.sin(ang)


def _apply_rope(x, cos, sin):
    half = x.shape[-1] // 2
    x1 = x[..., :half].astype(F32)
    x2 = x[..., half:].astype(F32)
    return jnp.concatenate([x1 * cos - x2 * sin, x1 * sin + x2 * cos], -1).astype(x.dtype)


def _centred_conv3(z, w):
    zp = jnp.pad(z, ((0, 0), (1, 1), (0, 0)))
    return zp[:, :-2] * w[0] + zp[:, 1:-1] * w[1] + zp[:, 2:] * w[2]


def _adaln(cond, w, b):
    return jnp.einsum('bd,de->be', jax.nn.silu(cond), w) + b


def _split_proj(h, w_in, rw_conv):
    z = jnp.einsum('btd,dn->btn', h, w_in)
    z_rw = _centred_conv3(z[..., :RW_IN], rw_conv)
    o = RW_IN
    zq = z[..., o:o + MLA_Q_RANK]
    o += MLA_Q_RANK
    zkv = z[..., o:o + MLA_KV_RANK]
    o += MLA_KV_RANK
    zkr = z[..., o:o + MLA_ROPE]
    o += MLA_ROPE
    zu = z[..., o:o + GM_W]
    zv = z[..., o + GM_W:o + 2 * GM_W]
    return z_rw, zq, zkv, zkr, zu, zv


def _rwkv7_bidir(z, s0, p):
    B, T, _ = z.shape
    H, N = RW_HEADS, RW_HEAD
    r = z[..., :RW_W]
    k = z[..., RW_W:2 * RW_W]
    v = z[..., 2 * RW_W:3 * RW_W]
    o = 3 * RW_W
    xw = z[..., o:o + N_DIR * RW_DECAY_RANK].reshape(B, T, N_DIR, RW_DECAY_RANK)
    o += N_DIR * RW_DECAY_RANK
    xa = z[..., o:o + N_DIR * RW_A_RANK].reshape(B, T, N_DIR, RW_A_RANK)
    o += N_DIR * RW_A_RANK
    xg = z[..., o:o + RW_GATE_RANK]
    w_log = -jax.nn.softplus(-(p['rw_w0'] + jnp.einsum('btdr,drc->btdc', jnp.tanh(xw), p['rw_w2']))) - 0.5
    decay = jnp.exp(-jnp.exp(w_log.astype(F32)))
    a = jax.nn.sigmoid(p['rw_a0'] + jnp.einsum('btdr,drc->btdc', xa, p['rw_a2']))
    g = jnp.einsum('btr,rc->btc', jax.nn.sigmoid(xg), p['rw_g2'])
    kk = (k * p['rw_kk']).reshape(B, T, H, N).astype(F32)
    kk = kk * lax.rsqrt(jnp.maximum(jnp.sum(kk * kk, -1, keepdims=True), 1e-24))
    k_d = k[:, :, None, :] * (1 + (a - 1) * p['rw_ka'])

    def per_dir(t):
        return jnp.moveaxis(t.reshape(B, T, N_DIR, H, N).astype(F32), 2, 0)

    def shared(t):
        return jnp.broadcast_to(t.reshape(B, T, H, N).astype(F32), (N_DIR, B, T, H, N))

    def time_major(t):
        return jnp.moveaxis(jnp.stack([t[0], t[1][:, ::-1]]), 2, 0)

    xs = tuple(time_major(t) for t in (shared(r), per_dir(decay), per_dir(k_d), shared(v), shared(kk), per_dir(a)))

    def step(S, inp):
        r_t, w_t, k_t, v_t, kk_t, a_t = inp
        s_kk = jnp.einsum('dbhvk,dbhk->dbhv', S, -kk_t)
        S = S * w_t[..., None, :] + s_kk[..., :, None] * (kk_t * a_t)[..., None, :] + v_t[..., :, None] * k_t[..., None, :]
        return S, jnp.einsum('dbhvk,dbhk->dbhv', S, r_t)

    s_fin, out = lax.scan(step, s0.astype(F32), xs)
    out = jnp.moveaxis(out, 0, 2)
    y = out[0] + out[1][:, ::-1]
    mu = jnp.mean(y, -1, keepdims=True)
    var = jnp.mean(jnp.square(y - mu), -1, keepdims=True)
    y = ((y - mu) * lax.rsqrt(var + RW_GN_EPS)).reshape(B, T, RW_W) * p['rw_gn_g'] + p['rw_gn_b']
    rk = jnp.sum(r.reshape(B, T, 1, H, N) * k_d.reshape(B, T, N_DIR, H, N) * p['rw_rk'], axis=(2, 4))[..., None]
    bonus = (rk * v.reshape(B, T, H, N)).reshape(B, T, RW_W).astype(F32)
    return ((y + bonus) * g).astype(z.dtype), s_fin


def _mla_qk(zq, zkv, zkr, p):
    B, T, _ = zq.shape
    q = jnp.einsum('btr,rn->btn', _rms_norm(zq, p['mla_q_norm']), p['mla_q_up'])
    q = q.reshape(B, T, MLA_HEADS, MLA_NOPE + MLA_ROPE)
    ckv = _rms_norm(zkv, p['mla_kv_norm'])
    return q[..., :MLA_NOPE], q[..., MLA_NOPE:], ckv, zkr


def _mla_expand(ckv, kv_up):
    B, T, _ = ckv.shape
    kv = jnp.einsum('btr,rn->btn', ckv, kv_up).reshape(B, T, MLA_HEADS, MLA_NOPE + MLA_V)
    return kv[..., :MLA_NOPE], kv[..., MLA_NOPE:]


def _mla_attend(q_nope, q_rope, k_nope, k_rope, v):
    B, Tq, H, _ = q_nope.shape
    nb = Tq // Q_BLOCK
    qn = jnp.swapaxes(q_nope.reshape(B, nb, Q_BLOCK, H, MLA_NOPE), 0, 1)
    qr = jnp.swapaxes(q_rope.reshape(B, nb, Q_BLOCK, H, MLA_ROPE), 0, 1)

    def block(args):
        qn_i, qr_i = args
        s = jnp.einsum('bqhd,bkhd->bhqk', qn_i, k_nope) + jnp.einsum('bqhr,bkr->bhqk', qr_i, k_rope)
        prob = jax.nn.softmax(s.astype(F32) * MLA_SCALE, axis=-1).astype(v.dtype)
        return jnp.einsum('bhqk,bkhd->bqhd', prob, v)

    o = lax.map(block, (qn, qr))
    return jnp.swapaxes(o, 0, 1).reshape(B, Tq, H * MLA_V)


def _chunk_mlp(zu, zv, p):
    B, T, _ = zu.shape
    u = jax.nn.gelu(zu)
    vf = jax.nn.gelu(zv).reshape(B, T, GM_GROUPS, GM_GROUP_W).astype(F32)
    mu = jnp.mean(vf, -1, keepdims=True)
    var = jnp.mean(jnp.square(vf - mu), -1, keepdims=True)
    vn = ((vf - mu) * lax.rsqrt(var + 1e-5)).reshape(B, T, GM_W) * p['gm_norm_g'] + p['gm_norm_b']
    vc = vn.astype(zu.dtype).reshape(B, T // CHUNK, CHUNK, GM_GROUPS, GM_GROUP_W)
    s = jnp.einsum('gpq,bnqgc->bnpgc', p['gm_ws'], vc) + p['gm_bs'].T[:, :, None]
    return u * s.reshape(B, T, GM_W)


def _swiglu(h, w_in, w_out):
    gu = jnp.einsum('btd,df->btf', h, w_in)
    return jnp.einsum('btf,fd->btd', jax.nn.silu(gu[..., :D_FF]) * gu[..., D_FF:], w_out)


def _trunk_layer(x, mod, p, s0, rope, ctx):
    shift1, scale1, gate1, shift2, scale2, gate2 = jnp.split(mod[:, None, :], 6, axis=-1)
    h = x * (1 + scale1) + shift1
    z_rw, zq, zkv, zkr, zu, zv = _split_proj(h, p['w_in'], p['rw_conv'])
    y_rw, s_fin = _rwkv7_bidir(z_rw, s0, p)
    q_nope, q_rope, ckv, krope = _mla_qk(zq, zkv, zkr, p)
    k_nope, v = _mla_expand(ckv, p['mla_kv_up'])
    k_rope = krope
    if rope is not None:
        cos, sin = rope
        q_rope = _apply_rope(q_rope, cos[:, None, :], sin[:, None, :])
        k_rope = _apply_rope(krope, cos, sin)
    if ctx is not None:
        ckv_c, krope_c = ctx
        k_nope_c, v_c = _mla_expand(ckv_c, p['mla_kv_up'])
        k_nope = jnp.concatenate([k_nope, k_nope_c], 1)
        k_rope = jnp.concatenate([k_rope, krope_c], 1)
        v = jnp.concatenate([v, v_c], 1)
    y_mla = _mla_attend(q_nope, q_rope, k_nope, k_rope, v)
    y_gm = _chunk_mlp(zu, zv, p)
    mix = jnp.einsum('btm,md->btd', jnp.concatenate([y_rw, y_mla, y_gm], -1), p['w_out'])
    x = _layer_norm(ALPHA * x + gate1 * mix, p['ln1_g'], p['ln1_b'])
    ff = _swiglu(x * (1 + scale2) + shift2, p['ffn_w_in'], p['ffn_w_out'])
    x = _layer_norm(ALPHA * x + gate2 * ff, p['ln2_g'], p['ln2_b'])
    return x, ckv, krope, s_fin


def setup_inputs(seed: int = 0) -> dict:
    key = jax.random.key(seed)
    ks = iter(jax.random.split(key, 48))
    L = DEPTH

    def nrm(shape, std):
        return std * jax.random.normal(next(ks), shape, F32)

    def gain(shape):
        return 1.0 + nrm(shape, 0.05)

    return {
        'x_prompt': nrm((BATCH, SEQ, D_MODEL), 1.0),
        'x_sample': nrm((DEC_BATCH, DEC_SEQ, D_MODEL), 1.0),
        'cache_ckv': nrm((DEC_BATCH, L, PAST_LEN, MLA_KV_RANK), 1.0),
        'cache_krope': nrm((DEC_BATCH, L, PAST_LEN, MLA_ROPE), 1.0),
        'state_rwkv': nrm((DEC_BATCH, L, N_DIR, RW_HEADS, RW_HEAD, RW_HEAD), 0.5),
        'c': nrm((DEC_BATCH, D_MODEL), 1.0),
        'c_ctx': nrm((D_MODEL,), 1.0),
        'ada_w': nrm((L, D_MODEL, 6 * D_MODEL), 0.5 * D_MODEL ** -0.5),
        'ada_b': nrm((L, 6 * D_MODEL), 0.02),
        'w_in': nrm((L, D_MODEL, N_IN), D_MODEL ** -0.5),
        'rw_conv': jnp.array([0.25, 0.5, 0.25], F32)[None, :, None] + nrm((L, 3, RW_IN), 0.05),
        'rw_w0': jax.random.uniform(next(ks), (L, N_DIR, RW_W), F32, -6.0, -1.0),
        'rw_w2': nrm((L, N_DIR, RW_DECAY_RANK, RW_W), 0.1 * RW_DECAY_RANK ** -0.5),
        'rw_a0': nrm((L, N_DIR, RW_W), 0.1),
        'rw_a2': nrm((L, N_DIR, RW_A_RANK, RW_W), 0.1 * RW_A_RANK ** -0.5),
        'rw_g2': nrm((L, RW_GATE_RANK, RW_W), RW_GATE_RANK ** -0.5),
        'rw_kk': 0.85 + nrm((L, RW_W), 0.05),
        'rw_ka': gain((L, RW_W)),
        'rw_rk': nrm((L, RW_HEADS, RW_HEAD), 0.1),
        'rw_gn_g': gain((L, RW_W)),
        'rw_gn_b': nrm((L, RW_W), 0.02),
        'mla_q_norm': gain((L, MLA_Q_RANK)),
        'mla_q_up': nrm((L, MLA_Q_RANK, MLA_HEADS * (MLA_NOPE + MLA_ROPE)), MLA_Q_RANK ** -0.5),
        'mla_kv_norm': gain((L, MLA_KV_RANK)),
        'mla_kv_up': nrm((L, MLA_KV_RANK, MLA_HEADS * (MLA_NOPE + MLA_V)), MLA_KV_RANK ** -0.5),
        'gm_norm_g': gain((L, GM_W)),
        'gm_norm_b': nrm((L, GM_W), 0.02),
        'gm_ws': nrm((L, GM_GROUPS, CHUNK, CHUNK), CHUNK ** -0.5),
        'gm_bs': gain((L, GM_GROUPS, CHUNK)),
        'w_out': nrm((L, D_MIX, D_MODEL), BETA * D_MIX ** -0.5),
        'ln1_g': gain((L, D_MODEL)),
        'ln1_b': nrm((L, D_MODEL), 0.02),
        'ffn_w_in': nrm((L, D_MODEL, 2 * D_FF), D_MODEL ** -0.5),
        'ffn_w_out': nrm((L, D_FF, D_MODEL), BETA * D_FF ** -0.5),
        'ln2_g': gain((L, D_MODEL)),
        'ln2_b': nrm((L, D_MODEL), 0.02),
    }


def reference(x_prompt, x_sample, cache_ckv, cache_krope, state_rwkv, c, c_ctx, ada_w, ada_b, w_in, rw_conv,
              rw_w0, rw_w2, rw_a0, rw_a2, rw_g2, rw_kk, rw_ka, rw_rk, rw_gn_g, rw_gn_b, mla_q_norm, mla_q_up,
              mla_kv_norm, mla_kv_up, gm_norm_g, gm_norm_b, gm_ws, gm_bs, w_out, ln1_g, ln1_b, ffn_w_in,
              ffn_w_out, ln2_g, ln2_b):
    rope = _axial_rope_tables(x_sample.shape[1])
    xp = x_prompt
    xs = x_sample
    bp = x_prompt.shape[0]
    ckv_out, krope_out, state_out = [], [], []
    for l in range(DEPTH):
        p = {
            'w_in': w_in[l], 'rw_conv': rw_conv[l], 'rw_w0': rw_w0[l], 'rw_w2': rw_w2[l], 'rw_a0': rw_a0[l],
            'rw_a2': rw_a2[l], 'rw_g2': rw_g2[l], 'rw_kk': rw_kk[l], 'rw_ka': rw_ka[l], 'rw_rk': rw_rk[l],
            'rw_gn_g': rw_gn_g[l], 'rw_gn_b': rw_gn_b[l], 'mla_q_norm': mla_q_norm[l], 'mla_q_up': mla_q_up[l],
            'mla_kv_norm': mla_kv_norm[l], 'mla_kv_up': mla_kv_up[l], 'gm_norm_g': gm_norm_g[l],
            'gm_norm_b': gm_norm_b[l], 'gm_ws': gm_ws[l], 'gm_bs': gm_bs[l], 'w_out': w_out[l],
            'ln1_g': ln1_g[l], 'ln1_b': ln1_b[l], 'ffn_w_in': ffn_w_in[l], 'ffn_w_out': ffn_w_out[l],
            'ln2_g': ln2_g[l], 'ln2_b': ln2_b[l],
        }
        mod_ctx = _adaln(c_ctx[None, :], ada_w[l], ada_b[l])
        s0_ctx = jnp.zeros((N_DIR, bp, RW_HEADS, RW_HEAD, RW_HEAD), F32)
        xp, ckv, krope, s_fin = _trunk_layer(xp, mod_ctx, p, s0_ctx, None, None)
        ckv_out.append(ckv)
        krope_out.append(krope)
        state_out.append(jnp.moveaxis(s_fin, 0, 1).astype(xp.dtype))
        mod_lat = _adaln(c, ada_w[l], ada_b[l])
        s0_lat = jnp.moveaxis(state_rwkv[:, l], 1, 0)
        xs, _, _, _ = _trunk_layer(xs, mod_lat, p, s0_lat, rope, (cache_ckv[:, l], cache_krope[:, l]))
    new_cache_ckv = jnp.stack(ckv_out, axis=1)
    new_cache_krope = jnp.stack(krope_out, axis=1)
    new_state_rwkv = jnp.stack(state_out, axis=1)
    return (xp, xs, new_cache_ckv, new_cache_krope, new_state_rwkv)
```

```python
import contextlib
import os
import numpy as np
import concourse.bass as bass
import concourse.mybir as mybir
from concourse.bass_utils import run_bass_kernel_spmd

F32 = mybir.dt.float32
BF16 = mybir.dt.bfloat16
AF = mybir.ActivationFunctionType
ALU = mybir.AluOpType
AX = mybir.AxisListType

N_DMA_SEMS = 10
ENGS = ("pe", "act", "dve", "pool", "sp")

L = 4
D = 1024
NIN = 2080
DFF = 2816
TS = 4096
TP = 256
PAST = 512
ALPHA = (2 * L) ** 0.25
MLA_SCALE = 96 ** -0.5
DEBUG = False
NLAYERS = L


class Buf:
    __slots__ = ("name", "lw", "rd", "psum")

    def __init__(self, name="", psum=False):
        self.name = name
        self.lw = None
        self.rd = []
        self.psum = psum


class V:
    __slots__ = ("ap", "bufs")

    def __init__(self, ap, bufs):
        self.ap = ap
        self.bufs = bufs

    def __getitem__(self, idx):
        return V(self.ap[idx], self.bufs)

    def rearrange(self, pat_, **kw):
        return V(self.ap.rearrange(pat_, **kw), self.bufs)

    def bc(self, shape):
        return V(self.ap.to_broadcast(list(shape)), self.bufs)

    def us(self, ax):
        return V(self.ap.unsqueeze(ax), self.bufs)

    @property
    def shape(self):
        return self.ap.shape


class Instr:
    __slots__ = ("eng", "idx", "fn", "deps", "dma", "needed", "rank", "dsem", "dval", "slot", "epoch")

    def __init__(self, eng, fn, dma):
        self.eng = eng
        self.fn = fn
        self.dma = dma
        self.deps = []
        self.needed = False
        self.rank = 0
        self.dsem = None
        self.dval = 0
        self.slot = None


class Sched:
    def __init__(self, nc):
        self.nc = nc
        self.streams = {e: [] for e in ENGS}
        self.stack = contextlib.ExitStack()
        self.ntiles = 0
        self.dlast = {}
        self.dcount = {}
        self.ndma = {e: 0 for e in ENGS}
        self.pending = {e: None for e in ENGS}
        self.lastc = {e: None for e in ENGS}
        self.epoch = 0

    def sb(self, shape, dt, name=None):
        self.ntiles += 1
        name = name or f"t{self.ntiles}"
        t = self.stack.enter_context(self.nc.sbuf_tensor(name, list(shape), dt))
        return V(t[:], [Buf(name)])

    def ps(self, shape, dt, name=None):
        self.ntiles += 1
        name = name or f"p{self.ntiles}"
        t = self.stack.enter_context(self.nc.psum_tensor(name, list(shape), dt))
        return V(t[:], [Buf(name, psum=True)])

    def dram(self, name, shape, dt, kind="Internal"):
        t = self.nc.dram_tensor(name, list(shape), dt, kind=kind)
        return V(t.ap(), [Buf(name)])

    def barrier(self):
        bar = [I for I in self.lastc.values() if I is not None] + list(self.dlast.values())
        for e in ENGS:
            self.pending[e] = list(bar)

    def new_epoch(self):
        self.barrier()
        if not os.environ.get("NO_EPOCH"):
            self.epoch += 1

    def add(self, eng, fn, reads, writes, dma=False):
        I = Instr(eng, fn, dma)
        I.epoch = self.epoch
        st = self.streams[eng]
        I.idx = len(st)
        st.append(I)
        deps = {}

        def dep(P):
            if P is None or P is I:
                return
            if (not P.dma) and P.eng == "pe" and eng == "pe" and not dma:
                return
            deps[id(P)] = P

        if self.pending[eng] is not None:
            for P in self.pending[eng]:
                dep(P)
            self.pending[eng] = None
        for v in reads:
            for b in v.bufs:
                dep(b.lw)
                if b.psum:
                    for r in b.rd:
                        if r.eng != eng:
                            dep(r)
        for v in writes:
            for b in v.bufs:
                dep(b.lw)
                for r in b.rd:
                    dep(r)
        for v in reads:
            for b in v.bufs:
                b.rd.append(I)
        for v in writes:
            for b in v.bufs:
                b.lw = I
                b.rd = []
        if dma:
            k = self.ndma[eng] % N_DMA_SEMS
            self.ndma[eng] += 1
            key = (eng, k)
            self.dcount[key] = self.dcount.get(key, 0) + 1
            I.slot = key
            I.dval = 16 * self.dcount[key]
            prev = self.dlast.get(key)
            if prev is not None:
                deps[id(prev)] = prev
            self.dlast[key] = I
        else:
            self.lastc[eng] = I
        I.deps = list(deps.values())
        return I

    def mm(self, out, lhsT, rhs, start=True, stop=True):
        return self.add("pe", lambda e: e.matmul(out.ap, lhsT.ap, rhs.ap, start=start, stop=stop),
                        [lhsT, rhs], [out])

    def tr(self, out, in_, ident):
        return self.add("pe", lambda e: e.transpose(out.ap, in_.ap, ident.ap), [in_, ident], [out])

    def act(self, out, in_, func, bias=None, scale=None, accum=None):
        kw = {}
        rd = [in_]
        wr = [out]
        if bias is not None:
            if isinstance(bias, V):
                kw["bias"] = bias.ap
                rd.append(bias)
            else:
                kw["bias"] = bias
        if scale is not None:
            if isinstance(scale, V):
                kw["scale"] = scale.ap
                rd.append(scale)
            else:
                kw["scale"] = scale
        if accum is not None:
            kw["accum_out"] = accum.ap
            wr.append(accum)
        return self.add("act", lambda e: e.activation(out.ap, in_.ap, func, **kw), rd, wr)

    def tt(self, out, a, b, op, eng="dve"):
        return self.add(eng, lambda e: e.tensor_tensor(out.ap, a.ap, b.ap, op), [a, b], [out])

    def ts(self, out, a, s1, op0, s2=None, op1=None, eng="dve"):
        rd = [a]
        a1 = s1.ap if isinstance(s1, V) else s1
        a2 = s2.ap if isinstance(s2, V) else s2
        if isinstance(s1, V):
            rd.append(s1)
        if isinstance(s2, V):
            rd.append(s2)
        kw = {}
        if op1 is not None:
            kw["op1"] = op1
        return self.add(eng, lambda e: e.tensor_scalar(out.ap, a.ap, a1, a2, op0, **kw), rd, [out])

    def stt(self, out, a, s, b, op0, op1, eng="dve"):
        rd = [a, b]
        sa = s.ap if isinstance(s, V) else s
        if isinstance(s, V):
            rd.append(s)
        return self.add(eng, lambda e: e.scalar_tensor_tensor(out.ap, a.ap, sa, b.ap, op0, op1), rd, [out])

    def copy(self, out, in_, eng="dve"):
        if eng == "act":
            return self.add("act", lambda e: e.copy(out.ap, in_.ap), [in_], [out])
        return self.add(eng, lambda e: e.tensor_copy(out.ap, in_.ap), [in_], [out])

    def memset(self, out, val, eng="dve"):
        return self.add(eng, lambda e: e.memset(out.ap, val), [], [out])

    def reduce(self, out, in_, op, eng="dve"):
        return self.add(eng, lambda e: e.tensor_reduce(out.ap, in_.ap, AX.X, op), [in_], [out])

    def recip(self, out, in_):
        return self.add("dve", lambda e: e.reciprocal(out.ap, in_.ap), [in_], [out])

    def scan(self, out, d0, d1, init, op0, op1):
        return self.add("dve", lambda e: e.tensor_tensor_scan(out.ap, d0.ap, d1.ap, init, op0, op1), [d0, d1], [out])

    def dma(self, out, in_, q="sp", slow=False):
        if slow:
            return self.add(q, lambda e: e.dma_start(out.ap, in_.ap, allow_slow_non_contiguous=True), [in_], [out], dma=True)
        return self.add(q, lambda e: e.dma_start(out.ap, in_.ap), [in_], [out], dma=True)

    def emit(self):
        nc = self.nc
        for e in ENGS:
            for I in self.streams[e]:
                for P in I.deps:
                    P.needed = True
        for e in ENGS:
            r = 0
            ep = 0
            for I in self.streams[e]:
                if I.epoch != ep:
                    ep = I.epoch
                    r = 0
                if not I.dma and I.needed:
                    r += 1
                I.rank = r
        sems = {}
        for e in ENGS:
            for ep in sorted({I.epoch for I in self.streams[e] if not I.dma and I.needed}):
                sems[(e, ep)] = self.stack.enter_context(nc.semaphore(f"c_{e}{ep}"))
        dsems = {}
        for key in self.dlast:
            dsems[key] = self.stack.enter_context(nc.semaphore(f"d_{key[0]}{key[1]}"))
        for e in ENGS:
            for I in self.streams[e]:
                if I.dma:
                    I.dsem = dsems[I.slot]
        block = self.stack.enter_context(nc.Block())
        stats = {"wait": 0, "ins": 0}
        dlast = self.dlast
        lastc = self.lastc

        def run(ename, eng):
            seen = {}
            for I in self.streams[ename]:
                for P in I.deps:
                    if P.dma:
                        if seen.get(P.slot, 0) >= P.dval:
                            continue
                        seen[P.slot] = P.dval
                        eng.wait_ge(P.dsem, P.dval)
                    else:
                        kk = (P.eng, P.epoch)
                        if seen.get(kk, 0) >= P.rank:
                            continue
                        seen[kk] = P.rank
                        eng.wait_ge(sems[kk], P.rank)
                    stats["wait"] += 1
                ins = I.fn(eng)
                stats["ins"] += 1
                if I.dma:
                    ins.then_inc(I.dsem, 16)
                elif I.needed:
                    ins.then_inc(sems[(ename, I.epoch)], 1)
            if ename == "sp":
                for key, I in dlast.items():
                    eng.wait_ge(I.dsem, I.dval)

        @block.tensor
        def _(e):
            run("pe", e)

        @block.scalar
        def _(e):
            run("act", e)

        @block.vector
        def _(e):
            run("dve", e)

        @block.gpsimd
        def _(e):
            run("pool", e)

        @block.sync
        def _(e):
            run("sp", e)

        return stats


class Arena:
    def __init__(self, S, nwords):
        self.v = S.sb([128, nwords], F32, "arena")
        self.off = 0
        self.n = nwords

    def reset(self):
        self.off = 0

    def get(self, shape, dt):
        free = int(np.prod(shape[1:]))
        words = free if dt == F32 else (free + 1) // 2
        assert self.off + words <= self.n, ("arena overflow", self.off, words, self.n)
        ap = self.v.ap[0:shape[0], self.off:self.off + words]
        if dt != F32:
            ap = ap.bitcast(dt)[:, 0:free]
        if len(shape) == 3:
            ap = ap.rearrange("p (a b) -> p a b", a=shape[1])
        elif len(shape) == 4:
            ap = ap.rearrange("p (a b c) -> p a b c", a=shape[1], b=shape[2])
        self.off += words
        return V(ap, [Buf()])


class DT:
    def __init__(self, S, name, rows, cols, dt, kind="Internal"):
        t = S.nc.dram_tensor(name, [rows, cols], dt, kind=kind)
        self.ap = t.ap()
        self.nb = (cols + 127) // 128
        self.bufs = [Buf(f"{name}.{i}") for i in range(self.nb)]

    def v(self, r0, r1, c0, c1):
        return V(self.ap[r0:r1, c0:c1], self.bufs[c0 // 128:(c1 - 1) // 128 + 1])


class _Stop(Exception):
    pass


def build(nlayers=NLAYERS, debug=DEBUG, stage=99):
    try:
        return _build(nlayers, debug, stage)
    except _Stop as e:
        S = e.args[0]
        return S.nc, S, S.emit()


def _build(nlayers, debug, stage):
    nc = bass.Bass("TRN2", target_bir_lowering=False)
    S = Sched(nc)
    dbgkind = "ExternalOutput" if debug else "Internal"

    def inp(name, shape):
        return S.dram(name, shape, F32, kind="ExternalInput")

    xs_d = inp("xs", [TS, D])
    xp_d = inp("xp", [2 * TP, D])
    cckv_d = inp("cckv", [L, PAST, 128])
    ckr_d = inp("ckr", [L, PAST, 32])
    st_d = inp("st", [L, 2, 4, 64, 64])
    cs_d = inp("cs", [8, 128])
    cc_d = inp("cc", [8, 128])
    cos_d = inp("cos", [TS, 16])
    sin_d = inp("sin", [TS, 16])
    W = {}
    for name, shape in [("ada_w", [L, D, 6 * D]), ("ada_b", [L, 48, 128]), ("w_in", [L, D, NIN]),
                        ("rw_conv", [L, 27, 128]), ("rw_w0", [L, 4, 128]), ("rw_w2", [L, 128, 256]),
                        ("rw_a0", [L, 4, 128]), ("rw_a2", [L, 128, 256]), ("rw_g2", [L, 128, 256]),
                        ("rw_kk", [L, 2, 128]), ("rw_ka", [L, 2, 128]), ("rw_rk", [L, 2, 128]),
                        ("rw_gn_g", [L, 2, 128]), ("rw_gn_b", [L, 2, 128]), ("mla_q_norm", [L, 2, 128]),
                        ("mla_q_up", [L, 256, 768]), ("mla_kv_norm", [L, 128]), ("mla_kv_up", [L, 128, 1024]),
                        ("gm_norm_g", [L, 256]), ("gm_norm_b", [L, 256]), ("gm_ws", [L, 4, 128, 128]),
                        ("gm_bs", [L, 4, 128]), ("w_out", [L, D, D]), ("ln1_g", [L, 8, 128]),
                        ("ln1_b", [L, 8, 128]), ("ffn_w_in", [L, D, 2 * DFF]), ("ffn_w_out", [L, DFF, D]),
                        ("ln2_g", [L, 8, 128]), ("ln2_b", [L, 8, 128])]:
        W[name] = inp(name, shape)

    def outp(name, shape):
        return S.dram(name, shape, F32, kind="ExternalOutput")

    ys_o = outp("ys", [TS, D])
    yp_o = outp("yp", [2 * TP, D])
    ockv_o = outp("ockv", [2, L, TP, 128])
    okr_o = outp("okr", [2, L, TP, 32])
    ost_o = outp("ost", [2, L, 2, 4, 64, 64])

    class Seq:
        pass

    seqs = []
    for i, (T, samp) in enumerate([(TS, True), (TP, False), (TP, False)]):
        q = Seq()
        q.i = i
        q.T = T
        q.samp = samp
        q.Tk = T + (PAST if samp else 0)
        q.cond = 0 if samp else 1
        q.pi = i - 1
        q.XT = DT(S, f"XT{i}", D, T, F32, dbgkind)
        q.ZRW = DT(S, f"ZRW{i}", 1152, T + 2, F32, dbgkind)
        q.QT = DT(S, f"QT{i}", 768, T, BF16, dbgkind)
        q.CKVT = DT(S, f"CKVT{i}", 128, q.Tk, BF16, dbgkind)
        q.KRT = DT(S, f"KRT{i}", 32, q.Tk, BF16, dbgkind)
        q.OF = DT(S, f"OF{i}", 256, T, F32, dbgkind)
        q.MIXT = DT(S, f"MIXT{i}", D, T, BF16, dbgkind)
        seqs.append(q)
    FWI_ap = S.nc.dram_tensor("FWI", [L * D, 2 * DFF], BF16, kind="Internal").ap()
    FWO_ap = S.nc.dram_tensor("FWO", [L * DFF, D], BF16, kind="Internal").ap()
    FWI_b = [[Buf() for _ in range(8)] for _ in range(L)]
    FWO_b = [[Buf() for _ in range(22)] for _ in range(L)]

    identf = S.sb([128, 128], F32, "identf")
    ident = S.sb([128, 128], BF16, "ident")
    onesf = S.sb([128, 128], F32, "onesf")
    blk = S.sb([128, 128], F32, "blk")
    masks = {}
    S.memset(onesf, 1.0)
    S.memset(blk, 0.0)
    S.memset(blk[0:64, 0:64], 1.0)
    S.memset(blk[64:128, 64:128], 1.0)

    def mkmask(name, pattern, cm, op):
        m = S.sb([128, 128], F32, name)
        S.add("pool", lambda e: e.memset(m.ap, 1.0), [], [m])
        S.add("pool", lambda e: e.affine_select(m.ap, m.ap, pattern, op, 0.0, base=0, channel_multiplier=cm), [m], [m])
        return m

    S.add("pool", lambda e: e.memset(identf.ap, 1.0), [], [identf])
    S.add("pool", lambda e: e.affine_select(identf.ap, identf.ap, [[-1, 128]], ALU.is_equal, 0.0, base=0, channel_multiplier=1), [identf], [identf])
    S.copy(ident, identf)
    masks["ut_s"] = mkmask("ut_s", [[1, 128]], -1, ALU.is_gt)
    masks["ut_i"] = mkmask("ut_i", [[1, 128]], -1, ALU.is_ge)
    masks["lt_s"] = mkmask("lt_s", [[-1, 128]], 1, ALU.is_gt)
    masks["lt_i"] = mkmask("lt_i", [[-1, 128]], 1, ALU.is_ge)
    def blockdiag(name, bs):
        nb_ = 128 // bs
        E = S.sb([nb_, 128], F32, name + "_e")
        S.add("pool", lambda e: e.memset(E.ap, 1.0), [], [E])
        S.add("pool", lambda e: e.affine_select(E.ap, E.ap, [[1, 128]], ALU.is_ge, 0.0, base=0, channel_multiplier=-bs), [E], [E])
        S.add("pool", lambda e: e.affine_select(E.ap, E.ap, [[-1, 128]], ALU.is_ge, 0.0, base=bs - 1, channel_multiplier=bs), [E], [E])
        B = S.sb([128, 128], F32, name)
        return E, B

    EB = [blockdiag("b16", 16), blockdiag("b32", 32), blockdiag("b64", 64)]
    zpad = S.sb([128, 9, 1], F32, "zpad")
    S.memset(zpad, 0.0)
    for q in seqs:
        break
        S.dma(q.ZRW.v(0, 1152, 0, 1).rearrange("(c p) n -> p c n", p=128), zpad, slow=True)
        S.dma(q.ZRW.v(0, 1152, q.T + 1, q.T + 2).rearrange("(c p) n -> p c n", p=128), zpad, slow=True)

    WIN = S.sb([128, 8, NIN], BF16, "WIN")
    WOUT = S.sb([128, 8, D], BF16, "WOUT")
    QUP = S.sb([128, 2, 768], BF16, "QUP")
    KVUP = S.sb([128, 1024], BF16, "KVUP")
    W2z = [S.sb([128, 256], BF16, f"W2z{d}") for d in range(2)]
    A2z = [S.sb([128, 256], BF16, f"A2z{d}") for d in range(2)]
    HM = S.sb([128, 2], F32, "HM")
    S.memset(HM, 0.0)
    S.memset(HM[0:64, 0:1], 1.0)
    S.memset(HM[64:128, 1:2], 1.0)
    G2 = S.sb([128, 256], BF16, "G2")
    WST = S.sb([128, 4, 128], BF16, "WST")
    CV = S.sb([128, 128], F32, "CV")
    GBS = S.sb([128, 4], F32, "GBS")
    KVN = S.sb([128, 128], F32, "KVN")
    GMG = S.sb([128, 256], F32, "GMG")
    GMB = S.sb([128, 256], F32, "GMB")
    MOD = S.sb([128, 48, 2], F32, "MOD")
    ON1 = S.sb([128, 8, 2], F32, "ON1")
    ON2 = S.sb([128, 8, 2], F32, "ON2")
    OMKA = S.sb([128, 2], F32, "OMKA")
    CT = S.sb([128, 16], F32, "CT")
    CTB = S.sb([128, 8, 2], BF16, "CTB")
    EPS6 = S.sb([128, 1], F32, "EPS6")
    S.memset(EPS6, 1e-6)

    AR = Arena(S, 32000)
    PSA = [S.ps([128, 512], F32, f"psa{i}") for i in range(7)]
    PST = S.ps([128, 1024], BF16, "pst")

    def cvcol(r):
        return CV[:, r:r + 1]

    for i_, (E_, B_) in enumerate(EB):
        S.mm(PSA[i_][:, 0:128], E_, E_)
        S.copy(B_, PSA[i_][:, 0:128])
    B16 = EB[0][1]
    D32 = S.sb([128, 128], F32, "d32")
    D64 = S.sb([128, 128], F32, "d64")
    D128 = S.sb([128, 128], F32, "d128")
    S.tt(D32, EB[1][1], EB[0][1], ALU.subtract)
    S.tt(D64, EB[2][1], EB[1][1], ALU.subtract)
    S.tt(D128, onesf, EB[2][1], ALU.subtract)

    def areset():
        S.barrier()
        AR.reset()

    areset()
    for l in range(nlayers):
        if os.environ.get("NO_CONV"):
            break
        for r in range(8):
            S.dma(V(FWI_ap[l * D + r * 128:l * D + (r + 1) * 128, :], [FWI_b[l][r]]), W["ffn_w_in"][l, r * 128:(r + 1) * 128, :], q="pool")
        for r in range(22):
            S.dma(V(FWO_ap[l * DFF + r * 128:l * DFF + (r + 1) * 128, :], [FWO_b[l][r]]), W["ffn_w_out"][l, r * 128:(r + 1) * 128, :], q="pool")

    xsrc = [xs_d, xp_d[0:TP, :], xp_d[TP:2 * TP, :]]
    for q in seqs:
        xt_all = None
        for t in range(q.T // 128):
            if t % 8 == 0:
                areset()
            xt = AR.get([128, D], F32)
            S.dma(xt, xsrc[q.i][t * 128:(t + 1) * 128, :])
            st = AR.get([128, 8, 128], F32)
            for half in range(2):
                ps = PSA[half]
                for j in range(4):
                    dc = half * 4 + j
                    S.tr(ps[:, j * 128:(j + 1) * 128], xt[:, dc * 128:(dc + 1) * 128], identf)
                S.copy(st[:, half * 4:half * 4 + 4, :], ps.rearrange("p (a b) -> p a b", a=4), eng="act" if half else "dve")
            S.dma(q.XT.v(0, D, t * 128, (t + 1) * 128).rearrange("(c p) n -> p c n", p=128), st, q="pool")

    if stage <= 0:
        raise _Stop(S)
    for l in range(nlayers):
        S.new_epoch()
        areset()
        stg = AR.get([128, 128], F32)
        S.memset(stg, 0.0)
        r = 0
        for name, n in [("ada_b", 48), ("rw_conv", 27), ("rw_w0", 4), ("rw_a0", 4), ("rw_kk", 2), ("rw_ka", 2),
                        ("rw_rk", 2), ("rw_gn_g", 2), ("rw_gn_b", 2), ("mla_q_norm", 2), ("ln1_g", 8),
                        ("ln1_b", 8), ("ln2_g", 8), ("ln2_b", 8)]:
            S.dma(stg[r:r + n, :], W[name][l])
            r += n
        S.tr(PSA[0][:, 0:128], stg, identf)
        S.copy(CV, PSA[0][:, 0:128])
        C_ADAB, C_CONV, C_W0, C_A0, C_KK, C_KA, C_RK, C_GNG, C_GNB, C_QN, C_L1G, C_L1B, C_L2G, C_L2B = \
            0, 48, 75, 79, 83, 85, 87, 89, 91, 93, 95, 103, 111, 119
        S.ts(OMKA, CV[:, C_KA:C_KA + 2], -1.0, ALU.mult, 1.0, ALU.add)
        stg2 = AR.get([128, 128], F32)
        S.memset(stg2, 0.0)
        S.dma(stg2[0:4, :], W["gm_bs"][l])
        S.dma(stg2[4:12, :], cs_d)
        S.dma(stg2[12:20, :], cc_d)
        S.tr(PSA[1][:, 0:128], stg2, identf)
        S.copy(GBS, PSA[1][:, 0:4])
        S.act(CT, PSA[1][:, 4:20], AF.Silu)
        S.copy(CTB[:, :, 0], CT[:, 0:8])
        S.copy(CTB[:, :, 1], CT[:, 8:16])
        S.dma(KVN, V(W["mla_kv_norm"].ap[l].partition_broadcast(128), W["mla_kv_norm"].bufs))
        S.dma(GMG, V(W["gm_norm_g"].ap[l].partition_broadcast(128), W["gm_norm_g"].bufs))
        S.dma(GMB, V(W["gm_norm_b"].ap[l].partition_broadcast(128), W["gm_norm_b"].bufs))
        for kc in range(8):
            S.dma(WIN[:, kc, :], W["w_in"][l, kc * 128:(kc + 1) * 128, :], q="pool")
        for kc in range(8):
            S.dma(WOUT[:, kc, :], W["w_out"][l, kc * 128:(kc + 1) * 128, :], q="pool")
        for kc in range(2):
            S.dma(QUP[:, kc, :], W["mla_q_up"][l, kc * 128:(kc + 1) * 128, :], q="pool")
        S.dma(KVUP, W["mla_kv_up"][l], q="pool")
        for d_ in range(2):
            S.memset(W2z[d_], 0.0)
            S.memset(A2z[d_], 0.0)
            S.dma(W2z[d_][d_ * 64:(d_ + 1) * 64, :], W["rw_w2"][l, d_ * 64:(d_ + 1) * 64, :], q="pool")
            S.dma(A2z[d_][d_ * 64:(d_ + 1) * 64, :], W["rw_a2"][l, d_ * 64:(d_ + 1) * 64, :], q="pool")
        S.dma(G2, W["rw_g2"][l], q="pool")
        wsf = AR.get([128, 4, 128], F32)
        S.dma(wsf, W["gm_ws"][l].rearrange("g p q -> p g q"))
        wsb = AR.get([128, 4, 128], BF16)
        S.copy(wsb, wsf)
        for g in range(4):
            S.tr(PST[:, g * 128:(g + 1) * 128], wsb[:, g, :], ident)
        S.copy(WST, PST[:, 0:512].rearrange("p (g q) -> p g q", g=4))
        for nt in range(12):
            aw = AR.get([128, 8, 512], BF16)
            for kc in range(8):
                S.dma(aw[:, kc, :], W["ada_w"][l, kc * 128:(kc + 1) * 128, nt * 512:(nt + 1) * 512], q="pool")
            for j in range(4):
                ec = nt * 4 + j
                ps = PSA[2 + (ec % 2)]
                for kc in range(8):
                    S.mm(ps[:, 0:2], aw[:, kc, j * 128:(j + 1) * 128], CTB[:, kc, :], start=(kc == 0), stop=(kc == 7))
                S.ts(MOD[:, ec, :], ps[:, 0:2], cvcol(C_ADAB + ec), ALU.add)
        S.ts(ON1, MOD[:, 8:16, :], 1.0, ALU.add)
        S.ts(ON2, MOD[:, 32:40, :], 1.0, ALU.add)

        def modc(j, dc, c):
            return MOD[:, j * 8 + dc, c:c + 1]

        if stage <= 1:
            raise _Stop(S)
        for q in seqs:
            N = min(512, q.T)
            c = q.cond
            for bt in range(q.T // N):
                areset()
                t0 = bt * N
                xT = AR.get([128, 8, N], F32)
                S.dma(xT, q.XT.v(0, D, t0, t0 + N).rearrange("(c p) n -> p c n", p=128))
                hT = AR.get([128, 8, N], BF16)
                for dc in range(8):
                    S.act(hT[:, dc, :], xT[:, dc, :], AF.Identity, bias=modc(0, dc, c), scale=ON1[:, dc, c:c + 1])
                zst = AR.get([128, 9, N], F32)
                for ch in range(9):
                    ps = PSA[ch % 2]
                    for kc in range(8):
                        S.mm(ps[:, 0:N], WIN[:, kc, ch * 128:(ch + 1) * 128], hT[:, kc, :], start=(kc == 0), stop=(kc == 7))
                    S.copy(zst[:, ch, :], ps[:, 0:N], eng="act" if ch % 2 else "dve")
                S.dma(q.ZRW.v(0, 1152, 1 + t0, 1 + t0 + N).rearrange("(c p) n -> p c n", p=128), zst, q="pool")
                for sub in range(N // 128):
                    ta = t0 + sub * 128
                    hs = hT[:, :, sub * 128:(sub + 1) * 128]
                    psm = PSA[2]
                    psg = PSA[3]
                    for kc in range(8):
                        S.mm(psm[:, 0:416], hs[:, kc, :], WIN[:, kc, 1152:1568], start=(kc == 0), stop=(kc == 7))
                    for kc in range(8):
                        S.mm(psg[:, 0:512], hs[:, kc, :], WIN[:, kc, 1568:2080], start=(kc == 0), stop=(kc == 7))
                    zm = AR.get([128, 416], F32)
                    S.copy(zm, psm[:, 0:416], eng="act")
                    junk = AR.get([128, 256], F32)
                    ss = AR.get([128, 2], F32)
                    S.memset(ss, 0.0)
                    S.act(junk[:, 0:256], zm[:, 0:256], AF.Square, accum=ss[:, 0:1])
                    S.act(junk[:, 0:128], zm[:, 256:384], AF.Square, accum=ss[:, 1:2])
                    S.ts(ss[:, 0:1], ss[:, 0:1], 1.0 / 256, ALU.mult, 1e-6, ALU.add)
                    S.ts(ss[:, 1:2], ss[:, 1:2], 1.0 / 128, ALU.mult, 1e-6, ALU.add)
                    S.act(ss, ss, AF.Sqrt)
                    S.recip(ss, ss)
                    zqn = AR.get([128, 256], BF16)
                    S.ts(zqn, zm[:, 0:256], ss[:, 0:1], ALU.mult)
                    for cc in range(2):
                        S.tr(PST[:, cc * 128:(cc + 1) * 128], zqn[:, cc * 128:(cc + 1) * 128], ident)
                    zqT = AR.get([128, 2, 128], BF16)
                    for cc in range(2):
                        S.act(zqT[:, cc, :], PST[:, cc * 128:(cc + 1) * 128], AF.Identity, scale=cvcol(C_QN + cc))
                    qf = AR.get([128, 8, 96], F32)
                    for a in range(2):
                        ps = PSA[4 + a]
                        for kc in range(2):
                            S.mm(ps[:, 0:384], zqT[:, kc, :], QUP[:, kc, a * 384:(a + 1) * 384], start=(kc == 0), stop=(kc == 1))
                        S.copy(qf[:, a * 4:(a + 1) * 4, :], ps[:, 0:384].rearrange("p (h j) -> p h j", h=4), eng="act")
                    qb = AR.get([128, 8, 96], BF16)
                    krf = zm[:, 384:416]
                    krb = AR.get([128, 32], BF16)
                    if q.samp:
                        cs_t = AR.get([128, 2, 16], F32)
                        S.dma(cs_t[:, 0, :], cos_d[ta:ta + 128, :])
                        S.dma(cs_t[:, 1, :], sin_d[ta:ta + 128, :])
                        cosb = cs_t[:, 0:1, :].bc([128, 8, 16])
                        sinb = cs_t[:, 1:2, :].bc([128, 8, 16])
                        S.copy(qb[:, :, 0:64], qf[:, :, 0:64])
                        t1 = AR.get([128, 8, 16], F32)
                        t2 = AR.get([128, 8, 16], F32)
                        S.tt(t1, qf[:, :, 64:80], cosb, ALU.mult)
                        S.tt(t2, qf[:, :, 80:96], sinb, ALU.mult)
                        S.tt(qb[:, :, 64:80], t1, t2, ALU.subtract)
                        S.tt(t1, qf[:, :, 64:80], sinb, ALU.mult)
                        S.tt(t2, qf[:, :, 80:96], cosb, ALU.mult)
                        S.tt(qb[:, :, 80:96], t1, t2, ALU.add)
                        S.tt(t1[:, 0, :], krf[:, 0:16], cs_t[:, 0, :], ALU.mult)
                        S.tt(t2[:, 0, :], krf[:, 16:32], cs_t[:, 1, :], ALU.mult)
                        S.tt(krb[:, 0:16], t1[:, 0, :], t2[:, 0, :], ALU.subtract)
                        S.tt(t1[:, 0, :], krf[:, 0:16], cs_t[:, 1, :], ALU.mult)
                        S.tt(t2[:, 0, :], krf[:, 16:32], cs_t[:, 0, :], ALU.mult)
                        S.tt(krb[:, 16:32], t1[:, 0, :], t2[:, 0, :], ALU.add)
                    else:
                        S.copy(qb, qf)
                        S.copy(krb, krf)
                        S.dma(okr_o[q.pi, l, ta:ta + 128, :], krf, q="pool")
                    for h in range(8):
                        S.tr(PST[0:96, h * 128:(h + 1) * 128], qb[:, h, :], ident)
                    qst = AR.get([96, 8, 128], BF16)
                    S.copy(qst, PST[0:96, :].rearrange("p (h n) -> p h n", h=8), eng="act")
                    S.dma(q.QT.v(0, 768, ta, ta + 128).rearrange("(h j) n -> j h n", j=96), qst, q="pool")
                    ckv = AR.get([128, 128], F32)
                    S.stt(ckv, zm[:, 256:384], ss[:, 1:2], KVN, ALU.mult, ALU.mult)
                    if not q.samp:
                        S.dma(ockv_o[q.pi, l, ta:ta + 128, :], ckv, q="pool")
                    ckb = AR.get([128, 128], BF16)
                    S.copy(ckb, ckv)
                    S.tr(PST[:, 0:128], ckb, ident)
                    S.tr(PST[0:32, 128:256], krb, ident)
                    kst = AR.get([128, 256], BF16)
                    S.copy(kst[:, 0:128], PST[:, 0:128])
                    S.copy(kst[0:32, 128:256], PST[0:32, 128:256])
                    S.dma(q.CKVT.v(0, 128, ta, ta + 128), kst[:, 0:128], q="pool")
                    S.dma(q.KRT.v(0, 32, ta, ta + 128), kst[0:32, 128:256], q="pool")
                    g0 = AR.get([128, 512], F32)
                    S.copy(g0, psg[:, 0:512], eng="act")
                    g1 = AR.get([128, 512], F32)
                    S.tt(g1, g0, g0, ALU.mult)
                    S.ts(g1, g1, 0.044715, ALU.mult, 1.0, ALU.add)
                    S.tt(g1, g1, g0, ALU.mult)
                    S.act(g1, g1, AF.Sigmoid, scale=1.5957691216057308)
                    S.tt(g0, g0, g1, ALU.mult)
                    vf = g0[:, 256:512].rearrange("p (g c) -> p g c", g=4)
                    sm = AR.get([128, 4], F32)
                    sq = AR.get([128, 4], F32)
                    S.reduce(sm, vf, ALU.add)
                    S.tt(g1[:, 0:256], g0[:, 256:512], g0[:, 256:512], ALU.mult)
                    S.reduce(sq, g1[:, 0:256].rearrange("p (g c) -> p g c", g=4), ALU.add)
                    S.ts(sm, sm, 1.0 / 64, ALU.mult)
                    S.ts(sq, sq, 1.0 / 64, ALU.mult, 1e-5, ALU.add)
                    m2 = AR.get([128, 4], F32)
                    S.tt(m2, sm, sm, ALU.mult)
                    S.tt(sq, sq, m2, ALU.subtract)
                    S.act(sq, sq, AF.Sqrt)
                    S.recip(sq, sq)
                    vn = g1[:, 256:512].rearrange("p (g c) -> p g c", g=4)
                    S.tt(vn, vf, sm.us(2).bc([128, 4, 64]), ALU.subtract)
                    S.tt(vn, vn, sq.us(2).bc([128, 4, 64]), ALU.mult)
                    S.tt(g1[:, 256:512], g1[:, 256:512], GMG, ALU.mult)
                    vnb = AR.get([128, 256], BF16)
                    S.tt(vnb, g1[:, 256:512], GMB, ALU.add)
                    pss = PSA[4]
                    for g in range(4):
                        S.mm(pss[:, g * 128:g * 128 + 64], WST[:, g, :], vnb[:, g * 64:(g + 1) * 64])
                    sg = g1[:, 0:256].rearrange("p (g c) -> p g c", g=4)
                    S.tt(sg, pss[:, 0:512].rearrange("p (g c) -> p g c", g=4)[:, :, 0:64], GBS.us(2).bc([128, 4, 64]), ALU.add)
                    ygb = AR.get([128, 256], BF16)
                    S.tt(ygb, g0[:, 0:256], g1[:, 0:256], ALU.mult)
                    for cc in range(2):
                        S.tr(PST[:, 512 + cc * 128:512 + (cc + 1) * 128], ygb[:, cc * 128:(cc + 1) * 128], ident)
                    gst = AR.get([128, 2, 128], BF16)
                    S.copy(gst, PST[:, 512:768].rearrange("p (c n) -> p c n", c=2))
                    S.dma(q.MIXT.v(768, 1024, ta, ta + 128).rearrange("(c p) n -> p c n", p=128), gst, q="pool")
            if q.samp:
                areset()
                for kt in range(PAST // 128):
                    ck = AR.get([128, 128], F32)
                    kr = AR.get([128, 32], F32)
                    S.dma(ck, cckv_d[l, kt * 128:(kt + 1) * 128, :])
                    S.dma(kr, ckr_d[l, kt * 128:(kt + 1) * 128, :])
                    ckb = AR.get([128, 128], BF16)
                    krb = AR.get([128, 32], BF16)
                    S.copy(ckb, ck)
                    S.copy(krb, kr)
                    S.tr(PST[:, 0:128], ckb, ident)
                    S.tr(PST[0:32, 128:256], krb, ident)
                    kst = AR.get([128, 256], BF16)
                    S.copy(kst[:, 0:128], PST[:, 0:128])
                    S.copy(kst[0:32, 128:256], PST[0:32, 128:256])
                    S.dma(q.CKVT.v(0, 128, TS + kt * 128, TS + (kt + 1) * 128), kst[:, 0:128], q="pool")
                    S.dma(q.KRT.v(0, 32, TS + kt * 128, TS + (kt + 1) * 128), kst[0:32, 128:256], q="pool")

        if stage <= 2:
            raise _Stop(S)
        for q in seqs:
            nch = q.T // 128
            if os.environ.get("P2_PROMPTS") and q.samp:
                continue
            for d in range(2):
                areset()
                Af = [AR.get([128, 64], F32) for _ in range(4)]
                Ab = [AR.get([128, 64], BF16) for _ in range(4)]
                for h_ in range(4):
                    S.memset(Af[h_], 0.0)
                if q.samp:
                    sv = AR.get([64, 4, 64], F32)
                    S.dma(sv, st_d[l, d].rearrange("h v k -> v h k"))
                    for hp in range(2):
                        S.tr(PSA[0][:, hp * 64:(hp + 1) * 64], sv[:, 2 * hp:2 * hp + 2, :].rearrange("v h k -> v (h k)"), identf[0:64, 0:64])
                        for hh in range(2):
                            pr = slice(hh * 64, (hh + 1) * 64)
                            S.copy(Af[2 * hp + hh][pr, :], PSA[0][pr, hp * 64:(hp + 1) * 64])
                for h_ in range(4):
                    S.copy(Ab[h_], Af[h_])
                mark = AR.off
                order = range(nch) if d == 0 else range(nch - 1, -1, -1)
                m_s = masks["ut_s"] if d == 0 else masks["lt_s"]
                m_i = masks["ut_i"] if d == 0 else masks["lt_i"]
                m_sT = masks["lt_s"] if d == 0 else masks["ut_s"]
                for ci, cidx in enumerate(order):
                    S.barrier()
                    AR.off = mark
                    if os.environ.get("P2_CUT") == "0":
                        continue
                    ta = cidx * 128
                    zw = AR.get([128, 9, 130], F32)
                    lo = 0 if cidx > 0 else 1
                    hi = 130 if cidx < nch - 1 else 129
                    if lo:
                        S.memset(zw[:, :, 0:1], 0.0)
                    if hi == 129:
                        S.memset(zw[:, :, 129:130], 0.0)
                    S.dma(zw[:, :, lo:hi], q.ZRW.v(0, 1152, ta + lo, ta + hi).rearrange("(c p) n -> p c n", p=128))
                    zc = AR.get([128, 9, 128], F32)
                    for ch in range(9):
                        eng = "dve"
                        S.ts(zc[:, ch, :], zw[:, ch, 0:128], cvcol(C_CONV + ch), ALU.mult, eng=eng)
                        S.stt(zc[:, ch, :], zw[:, ch, 1:129], cvcol(C_CONV + 9 + ch), zc[:, ch, :], ALU.mult, ALU.add, eng=eng)
                        S.stt(zc[:, ch, :], zw[:, ch, 2:130], cvcol(C_CONV + 18 + ch), zc[:, ch, :], ALU.mult, ALU.add, eng=eng)
                    if os.environ.get("P2_CUT") == "1":
                        continue
                    rT = zc[:, 0:2, :]
                    kT = zc[:, 2:4, :]
                    vT = zc[:, 4:6, :]
                    txw = AR.get([128, 128], BF16)
                    S.act(txw, zc[:, 6, :], AF.Tanh)
                    xab = AR.get([128, 128], BF16)
                    S.copy(xab, zc[:, 7, :])
                    sxg = AR.get([128, 128], BF16)
                    S.act(sxg, zc[:, 8, :], AF.Sigmoid)
                    lw = AR.get([128, 2, 128], F32)
                    av = [AR.get([128, 2, 128], F32) for _ in range(2)]
                    gT = AR.get([128, 2, 128], F32)
                    dirs = [d] if d == 0 else [0, 1]
                    for cc in range(2):
                        ps = PSA[cc]
                        S.mm(ps[:, 0:128], W2z[d][:, cc * 128:(cc + 1) * 128], txw)
                        for dd in dirs:
                            S.mm(ps[:, 128 + dd * 128:256 + dd * 128], A2z[dd][:, cc * 128:(cc + 1) * 128], xab)
                        S.mm(ps[:, 384:512], G2[:, cc * 128:(cc + 1) * 128], sxg)
                        S.act(lw[:, cc, :], ps[:, 0:128], AF.Sigmoid, bias=cvcol(C_W0 + d * 2 + cc))
                        for dd in dirs:
                            S.act(av[dd][:, cc, :], ps[:, 128 + dd * 128:256 + dd * 128], AF.Sigmoid, bias=cvcol(C_A0 + dd * 2 + cc))
                        S.copy(gT[:, cc, :], ps[:, 384:512])
                    S.ts(lw, lw, -0.6065306597126334, ALU.mult)
                    if os.environ.get("P2_CUT") == "2":
                        continue
                    kap = AR.get([128, 2, 128], F32)
                    k2 = AR.get([128, 2, 128], F32)
                    for cc in range(2):
                        S.ts(kap[:, cc, :], kT[:, cc, :], cvcol(C_KK + cc), ALU.mult)
                    S.tt(k2, kap, kap, ALU.mult)
                    for cc in range(2):
                        S.mm(PSA[2][:, cc * 128:(cc + 1) * 128], blk, k2[:, cc, :])
                    S.ts(k2, PSA[2][:, 0:256].rearrange("p (c n) -> p c n", c=2), 1e-24, ALU.max)
                    S.act(k2, k2, AF.Sqrt)
                    S.recip(k2, k2)
                    S.tt(kap, kap, k2, ALU.mult)
                    kd = [None, None]
                    for dd in dirs:
                        kd[dd] = AR.get([128, 2, 128], F32)
                        for cc in range(2):
                            S.ts(kd[dd][:, cc, :], av[dd][:, cc, :], cvcol(C_KA + cc), ALU.mult, OMKA[:, cc:cc + 1], ALU.add)
                        S.tt(kd[dd], kd[dd], kT, ALU.mult)
                    bd = AR.get([128, 2, 128], F32)
                    S.tt(bd, kap, av[d], ALU.mult)
                    if os.environ.get("P2_CUT") == "3":
                        continue
                    cum = AR.get([128, 2, 128], F32)
                    for cc in range(2):
                        S.scan(cum[:, cc, :], onesf, lw[:, cc, :], 0.0, ALU.mult, ALU.add)
                    tot = cum[:, :, 127:128]
                    gi = AR.get([128, 2, 128], F32)
                    ge = AR.get([128, 2, 128], F32)
                    if d == 0:
                        S.copy(gi, cum, eng="act")
                        S.tt(ge, cum, lw, ALU.subtract)
                    else:
                        S.tt(ge, tot.bc([128, 2, 128]), cum, ALU.subtract)
                        S.tt(gi, ge, lw, ALU.add)
                    e_i = AR.get([128, 2, 128], F32)
                    e_e = AR.get([128, 2, 128], F32)
                    e_n = AR.get([128, 2, 128], F32)
                    e_c = AR.get([128, 2, 128], F32)
                    gC = AR.get([128, 2], F32)
                    S.act(e_i, gi, AF.Exp)
                    S.act(e_e, ge, AF.Exp)
                    S.act(e_n, gi, AF.Exp, scale=-1.0)
                    for cc in range(2):
                        S.act(e_c[:, cc, :], gi[:, cc, :], AF.Exp, scale=-1.0, bias=tot[:, cc, :])
                    S.act(gC, tot[:, :, 0], AF.Exp)
                    kaptf = AR.get([128, 2, 128], F32)
                    kapt = AR.get([128, 2, 128], BF16)
                    rt = AR.get([128, 2, 128], BF16)
                    khf = [[AR.get([128, 128], F32) for _ in range(2)] for _ in range(2)]
                    bhf = [[AR.get([128, 128], F32) for _ in range(2)] for _ in range(2)]
                    kh = [[AR.get([128, 128], BF16) for _ in range(2)] for _ in range(2)]
                    bh = [[AR.get([128, 128], BF16) for _ in range(2)] for _ in range(2)]
                    khpf = AR.get([128, 2, 128], F32)
                    bhpf = AR.get([128, 2, 128], F32)
                    S.tt(kaptf, kap, e_e, ALU.mult)
                    S.copy(kapt, kaptf, eng="act")
                    S.tt(rt, rT, e_i, ALU.mult)
                    for hp_ in range(2):
                        for hh_ in range(2):
                            S.stt(khf[hp_][hh_], kd[d][:, hp_, :], HM[:, hh_:hh_ + 1], e_n[:, hp_, :], ALU.mult, ALU.mult)
                            S.stt(bhf[hp_][hh_], bd[:, hp_, :], HM[:, hh_:hh_ + 1], e_n[:, hp_, :], ALU.mult, ALU.mult)
                            S.copy(kh[hp_][hh_], khf[hp_][hh_], eng="act")
                            S.copy(bh[hp_][hh_], bhf[hp_][hh_], eng="act")
                    S.tt(khpf, kd[d], e_c, ALU.mult)
                    S.tt(bhpf, bd, e_c, ALU.mult)
                    tmf = AR.get([128, 3, 256], F32)
                    for j, src in enumerate((vT, khpf, bhpf)):
                        for cc in range(2):
                            ix = j * 2 + cc
                            pst_ = PSA[0] if ix < 4 else PSA[1]
                            col = (ix % 4) * 128
                            S.tr(pst_[:, col:col + 128], src[:, cc, :], identf)
                    S.copy(tmf[:, 0:2, :], PSA[0][:, 0:512].rearrange("p (j c) -> p j c", j=2), eng="act")
                    S.copy(tmf[:, 2, :], PSA[1][:, 0:256], eng="act")
                    Vtf, Kpf, Bpf = tmf[:, 0, :], tmf[:, 1, :], tmf[:, 2, :]
                    Vt = AR.get([128, 256], BF16)
                    S.copy(Vt, Vtf)
                    if os.environ.get("P2_CUT"):
                        continue
                    oT = AR.get([128, 2, 128], F32)
                    for hp in range(2):
                        Lk = [None, None]
                        Pk = [None, None]
                        Pb = [None, None]
                        Y = [None, None]
                        for hh in range(2):
                            pr = slice(hh * 64, (hh + 1) * 64)
                            ps1 = PSA[0]
                            ps2 = PSA[1]
                            S.mm(ps1[:, 0:128], khf[hp][hh], kaptf[:, hp, :])
                            S.mm(ps1[:, 128:256], kh[hp][hh], rt[:, hp, :])
                            S.mm(ps1[:, 256:384], bh[hp][hh], rt[:, hp, :])
                            S.mm(ps2[:, 0:128], bhf[hp][hh], kaptf[:, hp, :])
                            S.mm(ps2[:, 128:256], kaptf[:, hp, :], bhf[hp][hh])
                            Lk[hh] = AR.get([128, 128], F32)
                            Pk[hh] = AR.get([128, 128], BF16)
                            Pb[hh] = AR.get([128, 128], BF16)
                            S.tt(Lk[hh], ps1[:, 0:128], m_s, ALU.mult)
                            S.tt(Pk[hh], ps1[:, 128:256], m_i, ALU.mult)
                            S.tt(Pb[hh], ps1[:, 256:384], m_i, ALU.mult)
                            Mneg = AR.get([128, 128], F32)
                            Lneg = AR.get([128, 128], F32)
                            S.stt(Mneg, ps2[:, 0:128], -1.0, m_s, ALU.mult, ALU.mult)
                            S.stt(Lneg, ps2[:, 128:256], -1.0, m_sT, ALU.mult, ALU.mult)
                            Zs = [AR.get([128, 128], F32) for _ in range(2)]
                            ZTs = [AR.get([128, 128], F32) for _ in range(2)]
                            Ys = [AR.get([128, 128], F32) for _ in range(2)]
                            Ts = [AR.get([128, 128], F32) for _ in range(2)]
                            Pa = AR.get([128, 128], F32)
                            Pb_ = AR.get([128, 128], F32)
                            Mo = AR.get([128, 128], F32)
                            Lo = AR.get([128, 128], F32)
                            Z, ZT, Yc, Tc = Zs[0], ZTs[0], Ys[0], Ts[0]
                            S.tt(Z, Mneg, B16, ALU.mult)
                            S.tt(ZT, Lneg, B16, ALU.mult)
                            S.tt(Yc, Z, identf, ALU.add)
                            S.tt(Tc, ZT, identf, ALU.add)
                            for lev in range(3):
                                pz = PSA[2]
                                py = PSA[3]
                                S.mm(pz[:, 0:128], ZT, Z)
                                S.mm(pz[:, 128:256], Z, ZT)
                                Zn, ZTn = Zs[(lev + 1) % 2], ZTs[(lev + 1) % 2]
                                S.copy(Zn, pz[:, 0:128], eng="act")
                                S.copy(ZTn, pz[:, 128:256], eng="act")
                                S.mm(py[:, 0:128], ZTn, Yc)
                                S.mm(py[:, 128:256], Zn, Tc)
                                Yn, Tn = Ys[(lev + 1) % 2], Ts[(lev + 1) % 2]
                                S.tt(Yn, py[:, 0:128], Yc, ALU.add)
                                S.tt(Tn, py[:, 128:256], Tc, ALU.add)
                                Z, ZT, Yc, Tc = Zn, ZTn, Yn, Tn
                            yi = 1
                            for di, Dm in enumerate((D32, D64, D128)):
                                lastd = (di == 2)
                                pz = PSA[2]
                                py = PSA[3]
                                S.tt(Lo, Lneg, Dm, ALU.mult)
                                S.mm(pz[:, 0:128], Lo, Yc)
                                S.copy(Pa, pz[:, 0:128], eng="act")
                                S.mm(py[:, 0:128], Tc, Pa)
                                if not lastd:
                                    S.tt(Mo, Mneg, Dm, ALU.mult)
                                    S.mm(pz[:, 128:256], Mo, Tc)
                                    S.copy(Pb_, pz[:, 128:256], eng="act")
                                    S.mm(py[:, 128:256], Yc, Pb_)
                                yi = 1 - yi
                                Yn, Tn = Ys[yi], Ts[yi]
                                S.tt(Yn, py[:, 0:128], Yc, ALU.add)
                                if not lastd:
                                    S.tt(Tn, py[:, 128:256], Tc, ALU.add)
                                Yc, Tc = Yn, Tn
                            Y[hh] = Yc
                        for hh in range(2):
                            pr = slice(hh * 64, (hh + 1) * 64)
                            hcol = slice((2 * hp + hh) * 64, (2 * hp + hh + 1) * 64)
                            pw = PSA[4]
                            hd = 2 * hp + hh
                            S.mm(pw[:, 0:64], kaptf[:, hp, :], Af[hd], start=True, stop=False)
                            S.mm(pw[:, 0:64], Lk[hh], Vtf[:, hcol], start=False, stop=True)
                            Wf = AR.get([128, 64], F32)
                            S.copy(Wf, pw[:, 0:64], eng="act")
                            S.mm(pw[:, 128:192], Y[hh], Wf)
                            Unf = AR.get([128, 64], F32)
                            S.ts(Unf, pw[:, 128:192], -1.0, ALU.mult)
                            Un = AR.get([128, 64], BF16)
                            S.copy(Un, Unf, eng="act")
                            po = PSA[5]
                            S.mm(po[0:64, 0:128], Ab[hd], rt[:, hp, :], start=True, stop=False)
                            S.mm(po[0:64, 0:128], Vt[:, hcol], Pk[hh], start=False, stop=False)
                            S.mm(po[0:64, 0:128], Un, Pb[hh], start=False, stop=True)
                            S.mm(po[0:64, 128:192], Kpf[:, hcol], Vtf[:, hcol], start=True, stop=False)
                            S.mm(po[0:64, 128:192], Bpf[:, hcol], Unf, start=False, stop=True)
                            S.copy(oT[pr, hp, :], po[0:64, 0:128], eng="act")
                            dA = AR.get([128, 64], F32)
                            S.copy(dA[pr, :], po[0:64, 128:192], eng="act")
                            S.stt(Af[hd][pr, :], Af[hd][pr, :], gC[pr, hp:hp + 1], dA[pr, :], ALU.mult, ALU.add)
                            S.copy(Ab[hd][pr, :], Af[hd][pr, :])
                    if d == 0:
                        S.dma(q.OF.v(0, 256, ta, ta + 128).rearrange("(c p) n -> p c n", p=128), oT, q="pool")
                    else:
                        of = AR.get([128, 2, 128], F32)
                        S.dma(of, q.OF.v(0, 256, ta, ta + 128).rearrange("(c p) n -> p c n", p=128))
                        y = AR.get([128, 2, 128], F32)
                        S.tt(y, oT, of, ALU.add)
                        y2 = AR.get([128, 2, 128], F32)
                        S.tt(y2, y, y, ALU.mult)
                        ksum = AR.get([128, 2, 128], F32)
                        S.tt(ksum, kd[0], kd[1], ALU.add)
                        S.tt(ksum, ksum, rT, ALU.mult)
                        for cc in range(2):
                            S.ts(ksum[:, cc, :], ksum[:, cc, :], cvcol(C_RK + cc), ALU.mult)
                        pg = PSA[0]
                        pg2 = PSA[1]
                        for cc in range(2):
                            S.mm(pg[:, cc * 128:(cc + 1) * 128], blk, y[:, cc, :])
                            S.mm(pg[:, 256 + cc * 128:256 + (cc + 1) * 128], blk, y2[:, cc, :])
                            S.mm(pg2[:, cc * 128:(cc + 1) * 128], blk, ksum[:, cc, :])
                        mu = AR.get([128, 2, 128], F32)
                        var = AR.get([128, 2, 128], F32)
                        S.ts(mu, pg[:, 0:256].rearrange("p (c n) -> p c n", c=2), 1.0 / 64, ALU.mult)
                        S.ts(var, pg[:, 256:512].rearrange("p (c n) -> p c n", c=2), 1.0 / 64, ALU.mult, 64e-5, ALU.add)
                        S.tt(y2, mu, mu, ALU.mult)
                        S.tt(var, var, y2, ALU.subtract)
                        S.act(var, var, AF.Sqrt)
                        S.recip(var, var)
                        S.tt(y, y, mu, ALU.subtract)
                        S.tt(y, y, var, ALU.mult)
                        for cc in range(2):
                            S.ts(y[:, cc, :], y[:, cc, :], cvcol(C_GNG + cc), ALU.mult, cvcol(C_GNB + cc), ALU.add)
                        S.tt(y2, pg2[:, 0:256].rearrange("p (c n) -> p c n", c=2), vT, ALU.mult)
                        S.tt(y, y, y2, ALU.add)
                        yb = AR.get([128, 2, 128], BF16)
                        S.tt(yb, y, gT, ALU.mult)
                        S.dma(q.MIXT.v(0, 256, ta, ta + 128).rearrange("(c p) n -> p c n", p=128), yb, q="pool")
                if not q.samp and not os.environ.get("NO_FINST"):
                    for hp in range(2):
                        apair = AR.get([128, 64], F32)
                        S.tt(apair, Af[2 * hp], Af[2 * hp + 1], ALU.add)
                        S.tr(PSA[0][0:64, hp * 128:(hp + 1) * 128], apair, identf)
                    so = AR.get([64, 4, 64], F32)
                    S.copy(so, PSA[0][0:64, 0:256].rearrange("v (h k) -> v h k", h=4))
                    S.dma(ost_o[q.pi, l, d].rearrange("h v k -> v h k"), so, q="pool")

        if stage <= 3:
            raise _Stop(S)
        S.new_epoch()
        for q in seqs:
            areset()
            Tk = q.Tk
            nkt = Tk // 128
            ckT = AR.get([128, Tk], BF16)
            S.dma(ckT, q.CKVT.v(0, 128, 0, Tk))
            Va = AR.get([128, nkt, 8, 65], BF16)
            S.memset(Va[:, :, :, 64:65], 1.0)
            for kt in range(nkt):
                ps = PSA[kt % 2]
                S.mm(ps[:, 0:512], ckT[:, kt * 128:(kt + 1) * 128],
                     KVUP.rearrange("p (h c) -> p h c", h=8)[:, :, 64:128])
                S.copy(Va[:, kt, :, 0:64], ps[:, 0:512].rearrange("p (h c) -> p h c", h=8), eng="act" if kt % 2 else "dve")
            NQ = min(512, q.T)
            mark = AR.off
            for h in range(8):
                S.barrier()
                AR.off = mark
                KhT = AR.get([96, Tk], BF16)
                S.dma(KhT[64:96, :], q.KRT.v(0, 32, 0, Tk))
                for k5 in range((Tk + 511) // 512):
                    n = min(512, Tk - k5 * 512)
                    ps = PSA[k5 % 2]
                    S.mm(ps[0:64, 0:n], KVUP[:, h * 128:h * 128 + 64], ckT[:, k5 * 512:k5 * 512 + n])
                    S.copy(KhT[0:64, k5 * 512:k5 * 512 + n], ps[0:64, 0:n], eng="act" if k5 % 2 else "dve")
                QhT = AR.get([96, q.T], BF16)
                S.dma(QhT, q.QT.v(h * 96, (h + 1) * 96, 0, q.T))
                PT = [AR.get([128, NQ], BF16) for _ in range(3)]
                nsub = NQ // 128
                obuf = [AR.get([128, nsub, 65], F32) for _ in range(2)]
                rc = AR.get([128, 4, 1], F32)
                ob = AR.get([128, 4, 64], BF16)
                ost = AR.get([64, 512], BF16)
                for qt in range(q.T // NQ):
                    for kt in range(nkt):
                        ps = PSA[kt % 2]
                        S.mm(ps[:, 0:NQ], KhT[:, kt * 128:(kt + 1) * 128], QhT[:, qt * NQ:(qt + 1) * NQ])
                        pt = PT[kt % 3]
                        S.act(pt, ps[:, 0:NQ], AF.Exp, scale=MLA_SCALE)
                        for sub in range(nsub):
                            S.mm(PSA[2 + sub][:, 0:65], pt[:, sub * 128:(sub + 1) * 128], Va[:, kt, h, :],
                                 start=(kt == 0), stop=(kt == nkt - 1))
                    o = obuf[qt % 2]
                    for sub in range(nsub):
                        S.copy(o[:, sub, :], PSA[2 + sub][:, 0:65], eng="act" if sub % 2 else "dve")
                    S.recip(rc[:, 0:nsub, :], o[:, :, 64:65])
                    S.tt(ob[:, 0:nsub, :], o[:, :, 0:64], rc[:, 0:nsub, :].bc([128, nsub, 64]), ALU.mult)
                    for sub in range(nsub):
                        S.tr(PST[0:64, sub * 128:(sub + 1) * 128], ob[:, sub, :], ident)
                    S.copy(ost[:, 0:NQ], PST[0:64, 0:NQ], eng="act")
                    S.dma(q.MIXT.v(256 + h * 64, 256 + (h + 1) * 64, qt * NQ, (qt + 1) * NQ), ost[:, 0:NQ], q="pool")

        if stage <= 4:
            raise _Stop(S)
        last = (l == L - 1)
        for q in seqs:
            N = min(512, q.T)
            c = q.cond
            for bt in range(q.T // N):
                areset()
                t0 = bt * N
                xT = AR.get([128, 8, N], F32)
                S.dma(xT, q.XT.v(0, D, t0, t0 + N).rearrange("(c p) n -> p c n", p=128))
                mx = AR.get([128, 8, N], BF16)
                S.dma(mx, q.MIXT.v(0, D, t0, t0 + N).rearrange("(c p) n -> p c n", p=128))
                u = AR.get([128, 8, N], F32)
                x1 = AR.get([128, 8, N], F32)
                h2 = AR.get([128, 8, N], BF16)
                sqb = AR.get([128, N], F32)
                stat = AR.get([128, 2, N], F32)
                actT = AR.get([128, 22, N], BF16)

                def layer_norm(src_ps_fn, xin, gate_j, gcol, bcol, xout):
                    pss = PSA[2]
                    psq = PSA[3]
                    for dc in range(8):
                        ps = src_ps_fn(dc)
                        S.act(u[:, dc, :], xin[:, dc, :], AF.Identity, scale=ALPHA)
                        S.stt(u[:, dc, :], ps[:, 0:N], modc(gate_j, dc, c), u[:, dc, :], ALU.mult, ALU.add)
                        S.act(sqb, u[:, dc, :], AF.Square)
                        S.mm(pss[:, 0:N], onesf, u[:, dc, :], start=(dc == 0), stop=(dc == 7))
                        S.mm(psq[:, 0:N], onesf, sqb, start=(dc == 0), stop=(dc == 7))
                    S.ts(stat[:, 0, :], pss[:, 0:N], 1.0 / D, ALU.mult)
                    S.ts(stat[:, 1, :], psq[:, 0:N], 1.0 / D, ALU.mult, 1e-5, ALU.add)
                    S.tt(sqb, stat[:, 0, :], stat[:, 0, :], ALU.mult)
                    S.tt(stat[:, 1, :], stat[:, 1, :], sqb, ALU.subtract)
                    S.act(stat[:, 1, :], stat[:, 1, :], AF.Sqrt)
                    S.recip(stat[:, 1, :], stat[:, 1, :])
                    for dc in range(8):
                        S.tt(u[:, dc, :], u[:, dc, :], stat[:, 0, :], ALU.subtract)
                        S.tt(u[:, dc, :], u[:, dc, :], stat[:, 1, :], ALU.mult)
                        S.act(xout[:, dc, :], u[:, dc, :], AF.Identity, bias=cvcol(bcol + dc), scale=cvcol(gcol + dc))

                def proj1(dc):
                    ps = PSA[dc % 2]
                    for kc in range(8):
                        S.mm(ps[:, 0:N], WOUT[:, kc, dc * 128:(dc + 1) * 128], mx[:, kc, :], start=(kc == 0), stop=(kc == 7))
                    return ps

                layer_norm(proj1, xT, 2, C_L1G, C_L1B, x1)
                for dc in range(8):
                    S.act(h2[:, dc, :], x1[:, dc, :], AF.Identity, bias=modc(3, dc, c), scale=ON2[:, dc, c:c + 1])
                wring = [AR.get([128, 8, 256], BF16) for _ in range(2)]
                sil = [AR.get([128, N], F32) for _ in range(2)]
                for j in range(22):
                    wb = wring[j % 2]
                    S.dma(wb[:, :, 0:128], V(FWI_ap[l * D:(l + 1) * D, j * 128:(j + 1) * 128].rearrange("(c p) n -> p c n", p=128), FWI_b[l]))
                    S.dma(wb[:, :, 128:256], V(FWI_ap[l * D:(l + 1) * D, DFF + j * 128:DFF + (j + 1) * 128].rearrange("(c p) n -> p c n", p=128), FWI_b[l]))
                    pg = PSA[4]
                    pu = PSA[5]
                    for kc in range(8):
                        S.mm(pg[:, 0:N], wb[:, kc, 0:128], h2[:, kc, :], start=(kc == 0), stop=(kc == 7))
                    for kc in range(8):
                        S.mm(pu[:, 0:N], wb[:, kc, 128:256], h2[:, kc, :], start=(kc == 0), stop=(kc == 7))
                    sl = sil[j % 2]
                    S.act(sl, pg[:, 0:N], AF.Silu)
                    S.tt(actT[:, j, :], sl, pu[:, 0:N], ALU.mult)
                oring = [AR.get([128, 22, 128], BF16) for _ in range(2)]

                def proj2(dc):
                    ob = oring[dc % 2]
                    S.dma(ob, V(FWO_ap[l * DFF:(l + 1) * DFF, dc * 128:(dc + 1) * 128].rearrange("(j p) n -> p j n", p=128), FWO_b[l]))
                    ps = PSA[dc % 2]
                    for j in range(22):
                        S.mm(ps[:, 0:N], ob[:, j, :], actT[:, j, :], start=(j == 0), stop=(j == 21))
                    return ps

                layer_norm(proj2, x1, 5, C_L2G, C_L2B, xT)
                if not last and l < nlayers - 1:
                    S.dma(q.XT.v(0, D, t0, t0 + N).rearrange("(c p) n -> p c n", p=128), xT, q="pool")
                else:
                    dst = ys_o if q.samp else yp_o[q.pi * TP:(q.pi + 1) * TP, :]
                    ytl = [AR.get([128, D], F32) for _ in range(2)]
                    for sub in range(N // 128):
                        yt = ytl[sub % 2]
                        for half in range(2):
                            ps = PSA[4 + half]
                            for jj in range(4):
                                dc = half * 4 + jj
                                S.tr(ps[:, jj * 128:(jj + 1) * 128], xT[:, dc, sub * 128:(sub + 1) * 128], identf)
                            S.copy(yt[:, half * 512:(half + 1) * 512], ps[:, 0:512], eng="act" if half else "dve")
                        S.dma(dst[t0 + sub * 128:t0 + (sub + 1) * 128, :], yt, q="pool")
    stats = S.emit()
    return nc, S, stats


def _rope_tables():
    rows = TS // 64
    rr, cc = np.meshgrid(np.arange(rows), np.arange(64), indexing="ij")
    rr = rr.reshape(-1).astype(np.float32)
    cc = cc.reshape(-1).astype(np.float32)
    inv = (np.float32(10000.0) ** (-np.arange(8, dtype=np.float32) / np.float32(8))).astype(np.float32)
    ang = np.concatenate([rr[:, None] * inv, cc[:, None] * inv], -1).astype(np.float32)
    return np.cos(ang).astype(np.float32), np.sin(ang).astype(np.float32)


_CACHE = {}


def make_in_maps(inputs):
    f = lambda a: np.ascontiguousarray(np.asarray(a, dtype=np.float32))
    cos, sin = _rope_tables()
    shared = {}
    resh = {"ada_b": (L, 48, 128), "rw_conv": (L, 27, 128), "rw_w0": (L, 4, 128), "rw_w2": (L, 128, 256),
            "rw_a0": (L, 4, 128), "rw_a2": (L, 128, 256), "rw_kk": (L, 2, 128), "rw_ka": (L, 2, 128),
            "rw_rk": (L, 2, 128), "rw_gn_g": (L, 2, 128), "rw_gn_b": (L, 2, 128), "mla_q_norm": (L, 2, 128),
            "ln1_g": (L, 8, 128), "ln1_b": (L, 8, 128), "ln2_g": (L, 8, 128), "ln2_b": (L, 8, 128)}
    for name in ["ada_w", "ada_b", "w_in", "rw_conv", "rw_w0", "rw_w2", "rw_a0", "rw_a2", "rw_g2", "rw_kk", "rw_ka",
                 "rw_rk", "rw_gn_g", "rw_gn_b", "mla_q_norm", "mla_q_up", "mla_kv_norm", "mla_kv_up", "gm_norm_g",
                 "gm_norm_b", "gm_ws", "gm_bs", "w_out", "ln1_g", "ln1_b", "ffn_w_in", "ffn_w_out", "ln2_g", "ln2_b"]:
        a = f(inputs[name])
        if name in resh:
            a = a.reshape(resh[name])
        shared[name] = a
    shared["cos"] = cos
    shared["sin"] = sin
    shared["cc"] = f(inputs["c_ctx"]).reshape(8, 128)
    xs = f(inputs["x_sample"])
    xp = f(inputs["x_prompt"])
    maps = []
    for core in range(8):
        b = core % 4
        m = dict(shared)
        m["xs"] = xs[b]
        m["xp"] = xp[2 * core:2 * core + 2].reshape(2 * TP, D)
        m["cckv"] = f(inputs["cache_ckv"])[b]
        m["ckr"] = f(inputs["cache_krope"])[b]
        m["st"] = f(inputs["state_rwkv"])[b]
        m["cs"] = f(inputs["c"])[b].reshape(8, 128)
        maps.append(m)
    return maps


def kernel(**inputs):
    if "nc" not in _CACHE:
        _CACHE["nc"] = build()[0]
    nc = _CACHE["nc"]
    maps = make_in_maps(inputs)
    res = run_bass_kernel_spmd(nc, maps, core_ids=list(range(8)))
    R = res.results
    yp = np.concatenate([R[c]["yp"].reshape(2, TP, D) for c in range(8)], 0).astype(np.float32)
    ys = np.stack([R[b]["ys"] for b in range(4)], 0).astype(np.float32)
    ockv = np.concatenate([R[c]["ockv"] for c in range(8)], 0).astype(np.float32)
    okr = np.concatenate([R[c]["okr"] for c in range(8)], 0).astype(np.float32)
    ost = np.concatenate([R[c]["ost"] for c in range(8)], 0).astype(np.float32)
    return (yp, ys, ockv, okr, ost)
```

```python
import contextlib
import os
import numpy as np
import concourse.bass as bass
import concourse.mybir as mybir
from concourse.bass_utils import run_bass_kernel_spmd

F32 = mybir.dt.float32
BF16 = mybir.dt.bfloat16
AF = mybir.ActivationFunctionType
ALU = mybir.AluOpType
AX = mybir.AxisListType

N_DMA_SEMS = 10
ENGS = ("pe", "act", "dve", "pool", "sp")

L = 4
D = 1024
NIN = 2080
DFF = 2816
TS = 4096
TP = 256
PAST = 512
ALPHA = (2 * L) ** 0.25
MLA_SCALE = 96 ** -0.5
DEBUG = False
NLAYERS = L


class Buf:
    __slots__ = ("name", "lw", "rd", "psum")

    def __init__(self, name="", psum=False):
        self.name = name
        self.lw = None
        self.rd = []
        self.psum = psum


class V:
    __slots__ = ("ap", "bufs")

    def __init__(self, ap, bufs):
        self.ap = ap
        self.bufs = bufs

    def __getitem__(self, idx):
        return V(self.ap[idx], self.bufs)

    def rearrange(self, pat_, **kw):
        return V(self.ap.rearrange(pat_, **kw), self.bufs)

    def bc(self, shape):
        return V(self.ap.to_broadcast(list(shape)), self.bufs)

    def us(self, ax):
        return V(self.ap.unsqueeze(ax), self.bufs)

    @property
    def shape(self):
        return self.ap.shape


class Instr:
    __slots__ = ("eng", "idx", "fn", "deps", "dma", "needed", "rank", "dsem", "dval", "slot", "epoch")

    def __init__(self, eng, fn, dma):
        self.eng = eng
        self.fn = fn
        self.dma = dma
        self.deps = []
        self.needed = False
        self.rank = 0
        self.dsem = None
        self.dval = 0
        self.slot = None


class Sched:
    def __init__(self, nc):
        self.nc = nc
        self.streams = {e: [] for e in ENGS}
        self.stack = contextlib.ExitStack()
        self.ntiles = 0
        self.dlast = {}
        self.dcount = {}
        self.ndma = {e: 0 for e in ENGS}
        self.pending = {e: None for e in ENGS}
        self.lastc = {e: None for e in ENGS}
        self.epoch = 0

    def sb(self, shape, dt, name=None):
        self.ntiles += 1
        name = name or f"t{self.ntiles}"
        t = self.stack.enter_context(self.nc.sbuf_tensor(name, list(shape), dt))
        return V(t[:], [Buf(name)])

    def ps(self, shape, dt, name=None):
        self.ntiles += 1
        name = name or f"p{self.ntiles}"
        t = self.stack.enter_context(self.nc.psum_tensor(name, list(shape), dt))
        return V(t[:], [Buf(name, psum=True)])

    def dram(self, name, shape, dt, kind="Internal"):
        t = self.nc.dram_tensor(name, list(shape), dt, kind=kind)
        return V(t.ap(), [Buf(name)])

    def barrier(self):
        bar = [I for I in self.lastc.values() if I is not None] + list(self.dlast.values())
        for e in ENGS:
            self.pending[e] = list(bar)

    def new_epoch(self):
        self.barrier()
        if not os.environ.get("NO_EPOCH"):
            self.epoch += 1

    def add(self, eng, fn, reads, writes, dma=False):
        I = Instr(eng, fn, dma)
        I.epoch = self.epoch
        st = self.streams[eng]
        I.idx = len(st)
        st.append(I)
        deps = {}

        def dep(P):
            if P is None or P is I:
                return
            if (not P.dma) and P.eng == "pe" and eng == "pe" and not dma:
                return
            deps[id(P)] = P

        if self.pending[eng] is not None:
            for P in self.pending[eng]:
                dep(P)
            self.pending[eng] = None
        for v in reads:
            for b in v.bufs:
                dep(b.lw)
                if b.psum:
                    for r in b.rd:
                        if r.eng != eng:
                            dep(r)
        for v in writes:
            for b in v.bufs:
                dep(b.lw)
                for r in b.rd:
                    dep(r)
        for v in reads:
            for b in v.bufs:
                b.rd.append(I)
        for v in writes:
            for b in v.bufs:
                b.lw = I
                b.rd = []
        if dma:
            k = self.ndma[eng] % N_DMA_SEMS
            self.ndma[eng] += 1
            key = (eng, k)
            self.dcount[key] = self.dcount.get(key, 0) + 1
            I.slot = key
            I.dval = 16 * self.dcount[key]
            prev = self.dlast.get(key)
            if prev is not None:
                deps[id(prev)] = prev
            self.dlast[key] = I
        else:
            self.lastc[eng] = I
        I.deps = list(deps.values())
        return I

    def mm(self, out, lhsT, rhs, start=True, stop=True):
        return self.add("pe", lambda e: e.matmul(out.ap, lhsT.ap, rhs.ap, start=start, stop=stop),
                        [lhsT, rhs], [out])

    def tr(self, out, in_, ident):
        return self.add("pe", lambda e: e.transpose(out.ap, in_.ap, ident.ap), [in_, ident], [out])

    def act(self, out, in_, func, bias=None, scale=None, accum=None):
        kw = {}
        rd = [in_]
        wr = [out]
        if bias is not None:
            if isinstance(bias, V):
                kw["bias"] = bias.ap
                rd.append(bias)
            else:
                kw["bias"] = bias
        if scale is not None:
            if isinstance(scale, V):
                kw["scale"] = scale.ap
                rd.append(scale)
            else:
                kw["scale"] = scale
        if accum is not None:
            kw["accum_out"] = accum.ap
            wr.append(accum)
        return self.add("act", lambda e: e.activation(out.ap, in_.ap, func, **kw), rd, wr)

    def tt(self, out, a, b, op, eng="dve"):
        return self.add(eng, lambda e: e.tensor_tensor(out.ap, a.ap, b.ap, op), [a, b], [out])

    def ts(self, out, a, s1, op0, s2=None, op1=None, eng="dve"):
        rd = [a]
        a1 = s1.ap if isinstance(s1, V) else s1
        a2 = s2.ap if isinstance(s2, V) else s2
        if isinstance(s1, V):
            rd.append(s1)
        if isinstance(s2, V):
            rd.append(s2)
        kw = {}
        if op1 is not None:
            kw["op1"] = op1
        return self.add(eng, lambda e: e.tensor_scalar(out.ap, a.ap, a1, a2, op0, **kw), rd, [out])

    def stt(self, out, a, s, b, op0, op1, eng="dve"):
        rd = [a, b]
        sa = s.ap if isinstance(s, V) else s
        if isinstance(s, V):
            rd.append(s)
        return self.add(eng, lambda e: e.scalar_tensor_tensor(out.ap, a.ap, sa, b.ap, op0, op1), rd, [out])

    def copy(self, out, in_, eng="dve"):
        if eng == "act":
            return self.add("act", lambda e: e.copy(out.ap, in_.ap), [in_], [out])
        return self.add(eng, lambda e: e.tensor_copy(out.ap, in_.ap), [in_], [out])

    def memset(self, out, val, eng="dve"):
        return self.add(eng, lambda e: e.memset(out.ap, val), [], [out])

    def reduce(self, out, in_, op, eng="dve"):
        return self.add(eng, lambda e: e.tensor_reduce(out.ap, in_.ap, AX.X, op), [in_], [out])

    def recip(self, out, in_):
        return self.add("dve", lambda e: e.reciprocal(out.ap, in_.ap), [in_], [out])

    def scan(self, out, d0, d1, init, op0, op1):
        return self.add("dve", lambda e: e.tensor_tensor_scan(out.ap, d0.ap, d1.ap, init, op0, op1), [d0, d1], [out])

    def dma(self, out, in_, q="sp", slow=False):
        if slow:
            return self.add(q, lambda e: e.dma_start(out.ap, in_.ap, allow_slow_non_contiguous=True), [in_], [out], dma=True)
        return self.add(q, lambda e: e.dma_start(out.ap, in_.ap), [in_], [out], dma=True)

    def emit(self):
        nc = self.nc
        for e in ENGS:
            for I in self.streams[e]:
                for P in I.deps:
                    P.needed = True
        for e in ENGS:
            r = 0
            ep = 0
            for I in self.streams[e]:
                if I.epoch != ep:
                    ep = I.epoch
                    r = 0
                if not I.dma and I.needed:
                    r += 1
                I.rank = r
        sems = {}
        for e in ENGS:
            for ep in sorted({I.epoch for I in self.streams[e] if not I.dma and I.needed}):
                sems[(e, ep)] = self.stack.enter_context(nc.semaphore(f"c_{e}{ep}"))
        dsems = {}
        for key in self.dlast:
            dsems[key] = self.stack.enter_context(nc.semaphore(f"d_{key[0]}{key[1]}"))
        for e in ENGS:
            for I in self.streams[e]:
                if I.dma:
                    I.dsem = dsems[I.slot]
        block = self.stack.enter_context(nc.Block())
        stats = {"wait": 0, "ins": 0}
        dlast = self.dlast
        lastc = self.lastc

        def run(ename, eng):
            seen = {}
            for I in self.streams[ename]:
                for P in I.deps:
                    if P.dma:
                        if seen.get(P.slot, 0) >= P.dval:
                            continue
                        seen[P.slot] = P.dval
                        eng.wait_ge(P.dsem, P.dval)
                    else:
                        kk = (P.eng, P.epoch)
                        if seen.get(kk, 0) >= P.rank:
                            continue
                        seen[kk] = P.rank
                        eng.wait_ge(sems[kk], P.rank)
                    stats["wait"] += 1
                ins = I.fn(eng)
                stats["ins"] += 1
                if I.dma:
                    ins.then_inc(I.dsem, 16)
                elif I.needed:
                    ins.then_inc(sems[(ename, I.epoch)], 1)
            if ename == "sp":
                for key, I in dlast.items():
                    eng.wait_ge(I.dsem, I.dval)

        @block.tensor
        def _(e):
            run("pe", e)

        @block.scalar
        def _(e):
            run("act", e)

        @block.vector
        def _(e):
            run("dve", e)

        @block.gpsimd
        def _(e):
            run("pool", e)

        @block.sync
        def _(e):
            run("sp", e)

        return stats


class Arena:
    def __init__(self, S, nwords):
        self.v = S.sb([128, nwords], F32, "arena")
        self.off = 0
        self.n = nwords

    def reset(self):
        self.off = 0

    def get(self, shape, dt):
        free = int(np.prod(shape[1:]))
        words = free if dt == F32 else (free + 1) // 2
        assert self.off + words <= self.n, ("arena overflow", self.off, words, self.n)
        ap = self.v.ap[0:shape[0], self.off:self.off + words]
        if dt != F32:
            ap = ap.bitcast(dt)[:, 0:free]
        if len(shape) == 3:
            ap = ap.rearrange("p (a b) -> p a b", a=shape[1])
        elif len(shape) == 4:
            ap = ap.rearrange("p (a b c) -> p a b c", a=shape[1], b=shape[2])
        self.off += words
        return V(ap, [Buf()])


class DT:
    def __init__(self, S, name, rows, cols, dt, kind="Internal"):
        t = S.nc.dram_tensor(name, [rows, cols], dt, kind=kind)
        self.ap = t.ap()
        self.nb = (cols + 127) // 128
        self.bufs = [Buf(f"{name}.{i}") for i in range(self.nb)]

    def v(self, r0, r1, c0, c1):
        return V(self.ap[r0:r1, c0:c1], self.bufs[c0 // 128:(c1 - 1) // 128 + 1])


class _Stop(Exception):
    pass


def build(nlayers=NLAYERS, debug=DEBUG, stage=99):
    try:
        return _build(nlayers, debug, stage)
    except _Stop as e:
        S = e.args[0]
        return S.nc, S, S.emit()


def _build(nlayers, debug, stage):
    nc = bass.Bass("TRN2", target_bir_lowering=False)
    S = Sched(nc)
    dbgkind = "ExternalOutput" if debug else "Internal"

    def inp(name, shape):
        return S.dram(name, shape, F32, kind="ExternalInput")

    xs_d = inp("xs", [TS, D])
    xp_d = inp("xp", [2 * TP, D])
    cckv_d = inp("cckv", [L, PAST, 128])
    ckr_d = inp("ckr", [L, PAST, 32])
    st_d = inp("st", [L, 2, 4, 64, 64])
    cs_d = inp("cs", [8, 128])
    cc_d = inp("cc", [8, 128])
    cos_d = inp("cos", [TS, 16])
    sin_d = inp("sin", [TS, 16])
    W = {}
    for name, shape in [("ada_w", [L, D, 6 * D]), ("ada_b", [L, 48, 128]), ("w_in", [L, D, NIN]),
                        ("rw_conv", [L, 27, 128]), ("rw_w0", [L, 4, 128]), ("rw_w2", [L, 128, 256]),
                        ("rw_a0", [L, 4, 128]), ("rw_a2", [L, 128, 256]), ("rw_g2", [L, 128, 256]),
                        ("rw_kk", [L, 2, 128]), ("rw_ka", [L, 2, 128]), ("rw_rk", [L, 2, 128]),
                        ("rw_gn_g", [L, 2, 128]), ("rw_gn_b", [L, 2, 128]), ("mla_q_norm", [L, 2, 128]),
                        ("mla_q_up", [L, 256, 768]), ("mla_kv_norm", [L, 128]), ("mla_kv_up", [L, 128, 1024]),
                        ("gm_norm_g", [L, 256]), ("gm_norm_b", [L, 256]), ("gm_ws", [L, 4, 128, 128]),
                        ("gm_bs", [L, 4, 128]), ("w_out", [L, D, D]), ("ln1_g", [L, 8, 128]),
                        ("ln1_b", [L, 8, 128]), ("ffn_w_in", [L, D, 2 * DFF]), ("ffn_w_out", [L, DFF, D]),
                        ("ln2_g", [L, 8, 128]), ("ln2_b", [L, 8, 128])]:
        W[name] = inp(name, shape)

    def outp(name, shape):
        return S.dram(name, shape, F32, kind="ExternalOutput")

    ys_o = outp("ys", [TS, D])
    yp_o = outp("yp", [2 * TP, D])
    ockv_o = outp("ockv", [2, L, TP, 128])
    okr_o = outp("okr", [2, L, TP, 32])
    ost_o = outp("ost", [2, L, 2, 4, 64, 64])

    class Seq:
        pass

    seqs = []
    for i, (T, samp) in enumerate([(TS, True), (TP, False), (TP, False)]):
        q = Seq()
        q.i = i
        q.T = T
        q.samp = samp
        q.Tk = T + (PAST if samp else 0)
        q.cond = 0 if samp else 1
        q.pi = i - 1
        q.XT = DT(S, f"XT{i}", D, T, F32, dbgkind)
        q.ZRW = DT(S, f"ZRW{i}", 1152, T + 2, F32, dbgkind)
        q.QT = DT(S, f"QT{i}", 768, T, BF16, dbgkind)
        q.CKVT = DT(S, f"CKVT{i}", 128, q.Tk, BF16, dbgkind)
        q.KRT = DT(S, f"KRT{i}", 32, q.Tk, BF16, dbgkind)
        q.OF = DT(S, f"OF{i}", 256, T, F32, dbgkind)
        q.MIXT = DT(S, f"MIXT{i}", D, T, BF16, dbgkind)
        seqs.append(q)
    FWI_ap = S.nc.dram_tensor("FWI", [L * D, 2 * DFF], BF16, kind="Internal").ap()
    FWO_ap = S.nc.dram_tensor("FWO", [L * DFF, D], BF16, kind="Internal").ap()
    FWI_b = [[Buf() for _ in range(8)] for _ in range(L)]
    FWO_b = [[Buf() for _ in range(22)] for _ in range(L)]

    identf = S.sb([128, 128], F32, "identf")
    ident = S.sb([128, 128], BF16, "ident")
    onesf = S.sb([128, 128], F32, "onesf")
    blk = S.sb([128, 128], F32, "blk")
    masks = {}
    S.memset(onesf, 1.0)
    S.memset(blk, 0.0)
    S.memset(blk[0:64, 0:64], 1.0)
    S.memset(blk[64:128, 64:128], 1.0)

    def mkmask(name, pattern, cm, op):
        m = S.sb([128, 128], F32, name)
        S.add("pool", lambda e: e.memset(m.ap, 1.0), [], [m])
        S.add("pool", lambda e: e.affine_select(m.ap, m.ap, pattern, op, 0.0, base=0, channel_multiplier=cm), [m], [m])
        return m

    S.add("pool", lambda e: e.memset(identf.ap, 1.0), [], [identf])
    S.add("pool", lambda e: e.affine_select(identf.ap, identf.ap, [[-1, 128]], ALU.is_equal, 0.0, base=0, channel_multiplier=1), [identf], [identf])
    S.copy(ident, identf)
    masks["ut_s"] = mkmask("ut_s", [[1, 128]], -1, ALU.is_gt)
    masks["ut_i"] = mkmask("ut_i", [[1, 128]], -1, ALU.is_ge)
    masks["lt_s"] = mkmask("lt_s", [[-1, 128]], 1, ALU.is_gt)
    masks["lt_i"] = mkmask("lt_i", [[-1, 128]], 1, ALU.is_ge)
    def blockdiag(name, bs):
        nb_ = 128 // bs
        E = S.sb([nb_, 128], F32, name + "_e")
        S.add("pool", lambda e: e.memset(E.ap, 1.0), [], [E])
        S.add("pool", lambda e: e.affine_select(E.ap, E.ap, [[1, 128]], ALU.is_ge, 0.0, base=0, channel_multiplier=-bs), [E], [E])
        S.add("pool", lambda e: e.affine_select(E.ap, E.ap, [[-1, 128]], ALU.is_ge, 0.0, base=bs - 1, channel_multiplier=bs), [E], [E])
        B = S.sb([128, 128], F32, name)
        return E, B

    EB = [blockdiag("b16", 16), blockdiag("b32", 32), blockdiag("b64", 64)]
    zpad = S.sb([128, 9, 1], F32, "zpad")
    S.memset(zpad, 0.0)
    for q in seqs:
        break
        S.dma(q.ZRW.v(0, 1152, 0, 1).rearrange("(c p) n -> p c n", p=128), zpad, slow=True)
        S.dma(q.ZRW.v(0, 1152, q.T + 1, q.T + 2).rearrange("(c p) n -> p c n", p=128), zpad, slow=True)

    WIN = S.sb([128, 8, NIN], BF16, "WIN")
    WOUT = S.sb([128, 8, D], BF16, "WOUT")
    QUP = S.sb([128, 2, 768], BF16, "QUP")
    KVUP = S.sb([128, 1024], BF16, "KVUP")
    W2z = [S.sb([128, 256], BF16, f"W2z{d}") for d in range(2)]
    A2z = [S.sb([128, 256], BF16, f"A2z{d}") for d in range(2)]
    HM = S.sb([128, 2], F32, "HM")
    S.memset(HM, 0.0)
    S.memset(HM[0:64, 0:1], 1.0)
    S.memset(HM[64:128, 1:2], 1.0)
    G2 = S.sb([128, 256], BF16, "G2")
    WST = S.sb([128, 4, 128], BF16, "WST")
    CV = S.sb([128, 128], F32, "CV")
    GBS = S.sb([128, 4], F32, "GBS")
    KVN = S.sb([128, 128], F32, "KVN")
    GMG = S.sb([128, 256], F32, "GMG")
    GMB = S.sb([128, 256], F32, "GMB")
    MOD = S.sb([128, 48, 2], F32, "MOD")
    ON1 = S.sb([128, 8, 2], F32, "ON1")
    ON2 = S.sb([128, 8, 2], F32, "ON2")
    OMKA = S.sb([128, 2], F32, "OMKA")
    CT = S.sb([128, 16], F32, "CT")
    CTB = S.sb([128, 8, 2], BF16, "CTB")
    EPS6 = S.sb([128, 1], F32, "EPS6")
    S.memset(EPS6, 1e-6)

    AR = Arena(S, 32000)
    PSA = [S.ps([128, 512], F32, f"psa{i}") for i in range(7)]
    PST = S.ps([128, 1024], BF16, "pst")

    def cvcol(r):
        return CV[:, r:r + 1]

    for i_, (E_, B_) in enumerate(EB):
        S.mm(PSA[i_][:, 0:128], E_, E_)
        S.copy(B_, PSA[i_][:, 0:128])
    B16 = EB[0][1]
    D32 = S.sb([128, 128], F32, "d32")
    D64 = S.sb([128, 128], F32, "d64")
    D128 = S.sb([128, 128], F32, "d128")
    S.tt(D32, EB[1][1], EB[0][1], ALU.subtract)
    S.tt(D64, EB[2][1], EB[1][1], ALU.subtract)
    S.tt(D128, onesf, EB[2][1], ALU.subtract)

    def areset():
        S.barrier()
        AR.reset()

    areset()
    for l in range(nlayers):
        if os.environ.get("NO_CONV"):
            break
        for r in range(8):
            S.dma(V(FWI_ap[l * D + r * 128:l * D + (r + 1) * 128, :], [FWI_b[l][r]]), W["ffn_w_in"][l, r * 128:(r + 1) * 128, :], q="pool")
        for r in range(22):
            S.dma(V(FWO_ap[l * DFF + r * 128:l * DFF + (r + 1) * 128, :], [FWO_b[l][r]]), W["ffn_w_out"][l, r * 128:(r + 1) * 128, :], q="pool")

    xsrc = [xs_d, xp_d[0:TP, :], xp_d[TP:2 * TP, :]]
    for q in seqs:
        xt_all = None
        for t in range(q.T // 128):
            if t % 8 == 0:
                areset()
            xt = AR.get([128, D], F32)
            S.dma(xt, xsrc[q.i][t * 128:(t + 1) * 128, :])
            st = AR.get([128, 8, 128], F32)
            for half in range(2):
                ps = PSA[half]
                for j in range(4):
                    dc = half * 4 + j
                    S.tr(ps[:, j * 128:(j + 1) * 128], xt[:, dc * 128:(dc + 1) * 128], identf)
                S.copy(st[:, half * 4:half * 4 + 4, :], ps.rearrange("p (a b) -> p a b", a=4), eng="act" if half else "dve")
            S.dma(q.XT.v(0, D, t * 128, (t + 1) * 128).rearrange("(c p) n -> p c n", p=128), st, q="pool")

    if stage <= 0:
        raise _Stop(S)
    for l in range(nlayers):
        S.new_epoch()
        areset()
        stg = AR.get([128, 128], F32)
        S.memset(stg, 0.0)
        r = 0
        for name, n in [("ada_b", 48), ("rw_conv", 27), ("rw_w0", 4), ("rw_a0", 4), ("rw_kk", 2), ("rw_ka", 2),
                        ("rw_rk", 2), ("rw_gn_g", 2), ("rw_gn_b", 2), ("mla_q_norm", 2), ("ln1_g", 8),
                        ("ln1_b", 8), ("ln2_g", 8), ("ln2_b", 8)]:
            S.dma(stg[r:r + n, :], W[name][l])
            r += n
        S.tr(PSA[0][:, 0:128], stg, identf)
        S.copy(CV, PSA[0][:, 0:128])
        C_ADAB, C_CONV, C_W0, C_A0, C_KK, C_KA, C_RK, C_GNG, C_GNB, C_QN, C_L1G, C_L1B, C_L2G, C_L2B = \
            0, 48, 75, 79, 83, 85, 87, 89, 91, 93, 95, 103, 111, 119
        S.ts(OMKA, CV[:, C_KA:C_KA + 2], -1.0, ALU.mult, 1.0, ALU.add)
        stg2 = AR.get([128, 128], F32)
        S.memset(stg2, 0.0)
        S.dma(stg2[0:4, :], W["gm_bs"][l])
        S.dma(stg2[4:12, :], cs_d)
        S.dma(stg2[12:20, :], cc_d)
        S.tr(PSA[1][:, 0:128], stg2, identf)
        S.copy(GBS, PSA[1][:, 0:4])
        S.act(CT, PSA[1][:, 4:20], AF.Silu)
        S.copy(CTB[:, :, 0], CT[:, 0:8])
        S.copy(CTB[:, :, 1], CT[:, 8:16])
        S.dma(KVN, V(W["mla_kv_norm"].ap[l].partition_broadcast(128), W["mla_kv_norm"].bufs))
        S.dma(GMG, V(W["gm_norm_g"].ap[l].partition_broadcast(128), W["gm_norm_g"].bufs))
        S.dma(GMB, V(W["gm_norm_b"].ap[l].partition_broadcast(128), W["gm_norm_b"].bufs))
        for kc in range(8):
            S.dma(WIN[:, kc, :], W["w_in"][l, kc * 128:(kc + 1) * 128, :], q="pool")
        for kc in range(8):
            S.dma(WOUT[:, kc, :], W["w_out"][l, kc * 128:(kc + 1) * 128, :], q="pool")
        for kc in range(2):
            S.dma(QUP[:, kc, :], W["mla_q_up"][l, kc * 128:(kc + 1) * 128, :], q="pool")
        S.dma(KVUP, W["mla_kv_up"][l], q="pool")
        for d_ in range(2):
            S.memset(W2z[d_], 0.0)
            S.memset(A2z[d_], 0.0)
            S.dma(W2z[d_][d_ * 64:(d_ + 1) * 64, :], W["rw_w2"][l, d_ * 64:(d_ + 1) * 64, :], q="pool")
            S.dma(A2z[d_][d_ * 64:(d_ + 1) * 64, :], W["rw_a2"][l, d_ * 64:(d_ + 1) * 64, :], q="pool")
        S.dma(G2, W["rw_g2"][l], q="pool")
        wsf = AR.get([128, 4, 128], F32)
        S.dma(wsf, W["gm_ws"][l].rearrange("g p q -> p g q"))
        wsb = AR.get([128, 4, 128], BF16)
        S.copy(wsb, wsf)
        for g in range(4):
            S.tr(PST[:, g * 128:(g + 1) * 128], wsb[:, g, :], ident)
        S.copy(WST, PST[:, 0:512].rearrange("p (g q) -> p g q", g=4))
        for nt in range(12):
            aw = AR.get([128, 8, 512], BF16)
            for kc in range(8):
                S.dma(aw[:, kc, :], W["ada_w"][l, kc * 128:(kc + 1) * 128, nt * 512:(nt + 1) * 512], q="pool")
            for j in range(4):
                ec = nt * 4 + j
                ps = PSA[2 + (ec % 2)]
                for kc in range(8):
                    S.mm(ps[:, 0:2], aw[:, kc, j * 128:(j + 1) * 128], CTB[:, kc, :], start=(kc == 0), stop=(kc == 7))
                S.ts(MOD[:, ec, :], ps[:, 0:2], cvcol(C_ADAB + ec), ALU.add)
        S.ts(ON1, MOD[:, 8:16, :], 1.0, ALU.add)
        S.ts(ON2, MOD[:, 32:40, :], 1.0, ALU.add)

        def modc(j, dc, c):
            return MOD[:, j * 8 + dc, c:c + 1]

        if stage <= 1:
            raise _Stop(S)
        for q in seqs:
            N = min(512, q.T)
            c = q.cond
            for bt in range(q.T // N):
                areset()
                t0 = bt * N
                xT = AR.get([128, 8, N], F32)
                S.dma(xT, q.XT.v(0, D, t0, t0 + N).rearrange("(c p) n -> p c n", p=128))
                hT = AR.get([128, 8, N], BF16)
                for dc in range(8):
                    S.act(hT[:, dc, :], xT[:, dc, :], AF.Identity, bias=modc(0, dc, c), scale=ON1[:, dc, c:c + 1])
                zst = AR.get([128, 9, N], F32)
                for ch in range(9):
                    ps = PSA[ch % 2]
                    for kc in range(8):
                        S.mm(ps[:, 0:N], WIN[:, kc, ch * 128:(ch + 1) * 128], hT[:, kc, :], start=(kc == 0), stop=(kc == 7))
                    S.copy(zst[:, ch, :], ps[:, 0:N], eng="act" if ch % 2 else "dve")
                S.dma(q.ZRW.v(0, 1152, 1 + t0, 1 + t0 + N).rearrange("(c p) n -> p c n", p=128), zst, q="pool")
                for sub in range(N // 128):
                    ta = t0 + sub * 128
                    hs = hT[:, :, sub * 128:(sub + 1) * 128]
                    psm = PSA[2]
                    psg = PSA[3]
                    for kc in range(8):
                        S.mm(psm[:, 0:416], hs[:, kc, :], WIN[:, kc, 1152:1568], start=(kc == 0), stop=(kc == 7))
                    for kc in range(8):
                        S.mm(psg[:, 0:512], hs[:, kc, :], WIN[:, kc, 1568:2080], start=(kc == 0), stop=(kc == 7))
                    zm = AR.get([128, 416], F32)
                    S.copy(zm, psm[:, 0:416], eng="act")
                    junk = AR.get([128, 256], F32)
                    ss = AR.get([128, 2], F32)
                    S.memset(ss, 0.0)
                    S.act(junk[:, 0:256], zm[:, 0:256], AF.Square, accum=ss[:, 0:1])
                    S.act(junk[:, 0:128], zm[:, 256:384], AF.Square, accum=ss[:, 1:2])
                    S.ts(ss[:, 0:1], ss[:, 0:1], 1.0 / 256, ALU.mult, 1e-6, ALU.add)
                    S.ts(ss[:, 1:2], ss[:, 1:2], 1.0 / 128, ALU.mult, 1e-6, ALU.add)
                    S.act(ss, ss, AF.Sqrt)
                    S.recip(ss, ss)
                    zqn = AR.get([128, 256], BF16)
                    S.ts(zqn, zm[:, 0:256], ss[:, 0:1], ALU.mult)
                    for cc in range(2):
                        S.tr(PST[:, cc * 128:(cc + 1) * 128], zqn[:, cc * 128:(cc + 1) * 128], ident)
                    zqT = AR.get([128, 2, 128], BF16)
                    for cc in range(2):
                        S.act(zqT[:, cc, :], PST[:, cc * 128:(cc + 1) * 128], AF.Identity, scale=cvcol(C_QN + cc))
                    qf = AR.get([128, 8, 96], F32)
                    for a in range(2):
                        ps = PSA[4 + a]
                        for kc in range(2):
                            S.mm(ps[:, 0:384], zqT[:, kc, :], QUP[:, kc, a * 384:(a + 1) * 384], start=(kc == 0), stop=(kc == 1))
                        S.copy(qf[:, a * 4:(a + 1) * 4, :], ps[:, 0:384].rearrange("p (h j) -> p h j", h=4), eng="act")
                    qb = AR.get([128, 8, 96], BF16)
                    krf = zm[:, 384:416]
                    krb = AR.get([128, 32], BF16)
                    if q.samp:
                        cs_t = AR.get([128, 2, 16], F32)
                        S.dma(cs_t[:, 0, :], cos_d[ta:ta + 128, :])
                        S.dma(cs_t[:, 1, :], sin_d[ta:ta + 128, :])
                        cosb = cs_t[:, 0:1, :].bc([128, 8, 16])
                        sinb = cs_t[:, 1:2, :].bc([128, 8, 16])
                        S.copy(qb[:, :, 0:64], qf[:, :, 0:64])
                        t1 = AR.get([128, 8, 16], F32)
                        t2 = AR.get([128, 8, 16], F32)
                        S.tt(t1, qf[:, :, 64:80], cosb, ALU.mult)
                        S.tt(t2, qf[:, :, 80:96], sinb, ALU.mult)
                        S.tt(qb[:, :, 64:80], t1, t2, ALU.subtract)
                        S.tt(t1, qf[:, :, 64:80], sinb, ALU.mult)
                        S.tt(t2, qf[:, :, 80:96], cosb, ALU.mult)
                        S.tt(qb[:, :, 80:96], t1, t2, ALU.add)
                        S.tt(t1[:, 0, :], krf[:, 0:16], cs_t[:, 0, :], ALU.mult)
                        S.tt(t2[:, 0, :], krf[:, 16:32], cs_t[:, 1, :], ALU.mult)
                        S.tt(krb[:, 0:16], t1[:, 0, :], t2[:, 0, :], ALU.subtract)
                        S.tt(t1[:, 0, :], krf[:, 0:16], cs_t[:, 1, :], ALU.mult)
                        S.tt(t2[:, 0, :], krf[:, 16:32], cs_t[:, 0, :], ALU.mult)
                        S.tt(krb[:, 16:32], t1[:, 0, :], t2[:, 0, :], ALU.add)
                    else:
                        S.copy(qb, qf)
                        S.copy(krb, krf)
                        S.dma(okr_o[q.pi, l, ta:ta + 128, :], krf, q="pool")
                    for h in range(8):
                        S.tr(PST[0:96, h * 128:(h + 1) * 128], qb[:, h, :], ident)
                    qst = AR.get([96, 8, 128], BF16)
                    S.copy(qst, PST[0:96, :].rearrange("p (h n) -> p h n", h=8), eng="act")
                    S.dma(q.QT.v(0, 768, ta, ta + 128).rearrange("(h j) n -> j h n", j=96), qst, q="pool")
                    ckv = AR.get([128, 128], F32)
                    S.stt(ckv, zm[:, 256:384], ss[:, 1:2], KVN, ALU.mult, ALU.mult)
                    if not q.samp:
                        S.dma(ockv_o[q.pi, l, ta:ta + 128, :], ckv, q="pool")
                    ckb = AR.get([128, 128], BF16)
                    S.copy(ckb, ckv)
                    S.tr(PST[:, 0:128], ckb, ident)
                    S.tr(PST[0:32, 128:256], krb, ident)
                    kst = AR.get([128, 256], BF16)
                    S.copy(kst[:, 0:128], PST[:, 0:128])
                    S.copy(kst[0:32, 128:256], PST[0:32, 128:256])
                    S.dma(q.CKVT.v(0, 128, ta, ta + 128), kst[:, 0:128], q="pool")
                    S.dma(q.KRT.v(0, 32, ta, ta + 128), kst[0:32, 128:256], q="pool")
                    g0 = AR.get([128, 512], F32)
                    S.copy(g0, psg[:, 0:512], eng="act")
                    g1 = AR.get([128, 512], F32)
                    S.tt(g1, g0, g0, ALU.mult)
                    S.ts(g1, g1, 0.044715, ALU.mult, 1.0, ALU.add)
                    S.tt(g1, g1, g0, ALU.mult)
                    S.act(g1, g1, AF.Sigmoid, scale=1.5957691216057308)
                    S.tt(g0, g0, g1, ALU.mult)
                    vf = g0[:, 256:512].rearrange("p (g c) -> p g c", g=4)
                    sm = AR.get([128, 4], F32)
                    sq = AR.get([128, 4], F32)
                    S.reduce(sm, vf, ALU.add)
                    S.tt(g1[:, 0:256], g0[:, 256:512], g0[:, 256:512], ALU.mult)
                    S.reduce(sq, g1[:, 0:256].rearrange("p (g c) -> p g c", g=4), ALU.add)
                    S.ts(sm, sm, 1.0 / 64, ALU.mult)
                    S.ts(sq, sq, 1.0 / 64, ALU.mult, 1e-5, ALU.add)
                    m2 = AR.get([128, 4], F32)
                    S.tt(m2, sm, sm, ALU.mult)
                    S.tt(sq, sq, m2, ALU.subtract)
                    S.act(sq, sq, AF.Sqrt)
                    S.recip(sq, sq)
                    vn = g1[:, 256:512].rearrange("p (g c) -> p g c", g=4)
                    S.tt(vn, vf, sm.us(2).bc([128, 4, 64]), ALU.subtract)
                    S.tt(vn, vn, sq.us(2).bc([128, 4, 64]), ALU.mult)
                    S.tt(g1[:, 256:512], g1[:, 256:512], GMG, ALU.mult)
                    vnb = AR.get([128, 256], BF16)
                    S.tt(vnb, g1[:, 256:512], GMB, ALU.add)
                    pss = PSA[4]
                    for g in range(4):
                        S.mm(pss[:, g * 128:g * 128 + 64], WST[:, g, :], vnb[:, g * 64:(g + 1) * 64])
                    sg = g1[:, 0:256].rearrange("p (g c) -> p g c", g=4)
                    S.tt(sg, pss[:, 0:512].rearrange("p (g c) -> p g c", g=4)[:, :, 0:64], GBS.us(2).bc([128, 4, 64]), ALU.add)
                    ygb = AR.get([128, 256], BF16)
                    S.tt(ygb, g0[:, 0:256], g1[:, 0:256], ALU.mult)
                    for cc in range(2):
                        S.tr(PST[:, 512 + cc * 128:512 + (cc + 1) * 128], ygb[:, cc * 128:(cc + 1) * 128], ident)
                    gst = AR.get([128, 2, 128], BF16)
                    S.copy(gst, PST[:, 512:768].rearrange("p (c n) -> p c n", c=2))
                    S.dma(q.MIXT.v(768, 1024, ta, ta + 128).rearrange("(c p) n -> p c n", p=128), gst, q="pool")
            if q.samp:
                areset()
                for kt in range(PAST // 128):
                    ck = AR.get([128, 128], F32)
                    kr = AR.get([128, 32], F32)
                    S.dma(ck, cckv_d[l, kt * 128:(kt + 1) * 128, :])
                    S.dma(kr, ckr_d[l, kt * 128:(kt + 1) * 128, :])
                    ckb = AR.get([128, 128], BF16)
                    krb = AR.get([128, 32], BF16)
                    S.copy(ckb, ck)
                    S.copy(krb, kr)
                    S.tr(PST[:, 0:128], ckb, ident)
                    S.tr(PST[0:32, 128:256], krb, ident)
                    kst = AR.get([128, 256], BF16)
                    S.copy(kst[:, 0:128], PST[:, 0:128])
                    S.copy(kst[0:32, 128:256], PST[0:32, 128:256])
                    S.dma(q.CKVT.v(0, 128, TS + kt * 128, TS + (kt + 1) * 128), kst[:, 0:128], q="pool")
                    S.dma(q.KRT.v(0, 32, TS + kt * 128, TS + (kt + 1) * 128), kst[0:32, 128:256], q="pool")

        if stage <= 2:
            raise _Stop(S)
        for q in seqs:
            nch = q.T // 128
            if os.environ.get("P2_PROMPTS") and q.samp:
                continue
            for d in range(2):
                areset()
                Af = [AR.get([128, 64], F32) for _ in range(4)]
                Ab = [AR.get([128, 64], BF16) for _ in range(4)]
                for h_ in range(4):
                    S.memset(Af[h_], 0.0)
                if q.samp:
                    sv = AR.get([64, 4, 64], F32)
                    S.dma(sv, st_d[l, d].rearrange("h v k -> v h k"))
                    for hp in range(2):
                        S.tr(PSA[0][:, hp * 64:(hp + 1) * 64], sv[:, 2 * hp:2 * hp + 2, :].rearrange("v h k -> v (h k)"), identf[0:64, 0:64])
                        for hh in range(2):
                            pr = slice(hh * 64, (hh + 1) * 64)
                            S.copy(Af[2 * hp + hh][pr, :], PSA[0][pr, hp * 64:(hp + 1) * 64])
                for h_ in range(4):
                    S.copy(Ab[h_], Af[h_])
                mark = AR.off
                order = range(nch) if d == 0 else range(nch - 1, -1, -1)
                m_s = masks["ut_s"] if d == 0 else masks["lt_s"]
                m_i = masks["ut_i"] if d == 0 else masks["lt_i"]
                m_sT = masks["lt_s"] if d == 0 else masks["ut_s"]
                for ci, cidx in enumerate(order):
                    S.barrier()
                    AR.off = mark
                    if os.environ.get("P2_CUT") == "0":
                        continue
                    ta = cidx * 128
                    zw = AR.get([128, 9, 130], F32)
                    lo = 0 if cidx > 0 else 1
                    hi = 130 if cidx < nch - 1 else 129
                    if lo:
                        S.memset(zw[:, :, 0:1], 0.0)
                    if hi == 129:
                        S.memset(zw[:, :, 129:130], 0.0)
                    S.dma(zw[:, :, lo:hi], q.ZRW.v(0, 1152, ta + lo, ta + hi).rearrange("(c p) n -> p c n", p=128))
                    zc = AR.get([128, 9, 128], F32)
                    for ch in range(9):
                        eng = "dve"
                        S.ts(zc[:, ch, :], zw[:, ch, 0:128], cvcol(C_CONV + ch), ALU.mult, eng=eng)
                        S.stt(zc[:, ch, :], zw[:, ch, 1:129], cvcol(C_CONV + 9 + ch), zc[:, ch, :], ALU.mult, ALU.add, eng=eng)
                        S.stt(zc[:, ch, :], zw[:, ch, 2:130], cvcol(C_CONV + 18 + ch), zc[:, ch, :], ALU.mult, ALU.add, eng=eng)
                    if os.environ.get("P2_CUT") == "1":
                        continue
                    rT = zc[:, 0:2, :]
                    kT = zc[:, 2:4, :]
                    vT = zc[:, 4:6, :]
                    txw = AR.get([128, 128], BF16)
                    S.act(txw, zc[:, 6, :], AF.Tanh)
                    xab = AR.get([128, 128], BF16)
                    S.copy(xab, zc[:, 7, :])
                    sxg = AR.get([128, 128], BF16)
                    S.act(sxg, zc[:, 8, :], AF.Sigmoid)
                    lw = AR.get([128, 2, 128], F32)
                    av = [AR.get([128, 2, 128], F32) for _ in range(2)]
                    gT = AR.get([128, 2, 128], F32)
                    dirs = [d] if d == 0 else [0, 1]
                    for cc in range(2):
                        ps = PSA[cc]
                        S.mm(ps[:, 0:128], W2z[d][:, cc * 128:(cc + 1) * 128], txw)
                        for dd in dirs:
                            S.mm(ps[:, 128 + dd * 128:256 + dd * 128], A2z[dd][:, cc * 128:(cc + 1) * 128], xab)
                        S.mm(ps[:, 384:512], G2[:, cc * 128:(cc + 1) * 128], sxg)
                        S.act(lw[:, cc, :], ps[:, 0:128], AF.Sigmoid, bias=cvcol(C_W0 + d * 2 + cc))
                        for dd in dirs:
                            S.act(av[dd][:, cc, :], ps[:, 128 + dd * 128:256 + dd * 128], AF.Sigmoid, bias=cvcol(C_A0 + dd * 2 + cc))
                        S.copy(gT[:, cc, :], ps[:, 384:512])
                    S.ts(lw, lw, -0.6065306597126334, ALU.mult)
                    if os.environ.get("P2_CUT") == "2":
                        continue
                    kap = AR.get([128, 2, 128], F32)
                    k2 = AR.get([128, 2, 128], F32)
                    for cc in range(2):
                        S.ts(kap[:, cc, :], kT[:, cc, :], cvcol(C_KK + cc), ALU.mult)
                    S.tt(k2, kap, kap, ALU.mult)
                    for cc in range(2):
                        S.mm(PSA[2][:, cc * 128:(cc + 1) * 128], blk, k2[:, cc, :])
                    S.ts(k2, PSA[2][:, 0:256].rearrange("p (c n) -> p c n", c=2), 1e-24, ALU.max)
                    S.act(k2, k2, AF.Sqrt)
                    S.recip(k2, k2)
                    S.tt(kap, kap, k2, ALU.mult)
                    kd = [None, None]
                    for dd in dirs:
                        kd[dd] = AR.get([128, 2, 128], F32)
                        for cc in range(2):
                            S.ts(kd[dd][:, cc, :], av[dd][:, cc, :], cvcol(C_KA + cc), ALU.mult, OMKA[:, cc:cc + 1], ALU.add)
                        S.tt(kd[dd], kd[dd], kT, ALU.mult)
                    bd = AR.get([128, 2, 128], F32)
                    S.tt(bd, kap, av[d], ALU.mult)
                    if os.environ.get("P2_CUT") == "3":
                        continue
                    cum = AR.get([128, 2, 128], F32)
                    for cc in range(2):
                        S.scan(cum[:, cc, :], onesf, lw[:, cc, :], 0.0, ALU.mult, ALU.add)
                    tot = cum[:, :, 127:128]
                    gi = AR.get([128, 2, 128], F32)
                    ge = AR.get([128, 2, 128], F32)
                    if d == 0:
                        S.copy(gi, cum, eng="act")
                        S.tt(ge, cum, lw, ALU.subtract)
                    else:
                        S.tt(ge, tot.bc([128, 2, 128]), cum, ALU.subtract)
                        S.tt(gi, ge, lw, ALU.add)
                    e_i = AR.get([128, 2, 128], F32)
                    e_e = AR.get([128, 2, 128], F32)
                    e_n = AR.get([128, 2, 128], F32)
                    e_c = AR.get([128, 2, 128], F32)
                    gC = AR.get([128, 2], F32)
                    S.act(e_i, gi, AF.Exp)
                    S.act(e_e, ge, AF.Exp)
                    S.act(e_n, gi, AF.Exp, scale=-1.0)
                    for cc in range(2):
                        S.act(e_c[:, cc, :], gi[:, cc, :], AF.Exp, scale=-1.0, bias=tot[:, cc, :])
                    S.act(gC, tot[:, :, 0], AF.Exp)
                    kaptf = AR.get([128, 2, 128], F32)
                    kapt = AR.get([128, 2, 128], BF16)
                    rt = AR.get([128, 2, 128], BF16)
                    khf = [[AR.get([128, 128], F32) for _ in range(2)] for _ in range(2)]
                    bhf = [[AR.get([128, 128], F32) for _ in range(2)] for _ in range(2)]
                    kh = [[AR.get([128, 128], BF16) for _ in range(2)] for _ in range(2)]
                    bh = [[AR.get([128, 128], BF16) for _ in range(2)] for _ in range(2)]
                    khpf = AR.get([128, 2, 128], F32)
                    bhpf = AR.get([128, 2, 128], F32)
                    S.tt(kaptf, kap, e_e, ALU.mult)
                    S.copy(kapt, kaptf, eng="act")
                    S.tt(rt, rT, e_i, ALU.mult)
                    for hp_ in range(2):
                        for hh_ in range(2):
                            S.stt(khf[hp_][hh_], kd[d][:, hp_, :], HM[:, hh_:hh_ + 1], e_n[:, hp_, :], ALU.mult, ALU.mult)
                            S.stt(bhf[hp_][hh_], bd[:, hp_, :], HM[:, hh_:hh_ + 1], e_n[:, hp_, :], ALU.mult, ALU.mult)
                            S.copy(kh[hp_][hh_], khf[hp_][hh_], eng="act")
                            S.copy(bh[hp_][hh_], bhf[hp_][hh_], eng="act")
                    S.tt(khpf, kd[d], e_c, ALU.mult)
                    S.tt(bhpf, bd, e_c, ALU.mult)
                    tmf = AR.get([128, 3, 256], F32)
                    for j, src in enumerate((vT, khpf, bhpf)):
                        for cc in range(2):
                            ix = j * 2 + cc
                            pst_ = PSA[0] if ix < 4 else PSA[1]
                            col = (ix % 4) * 128
                            S.tr(pst_[:, col:col + 128], src[:, cc, :], identf)
                    S.copy(tmf[:, 0:2, :], PSA[0][:, 0:512].rearrange("p (j c) -> p j c", j=2), eng="act")
                    S.copy(tmf[:, 2, :], PSA[1][:, 0:256], eng="act")
                    Vtf, Kpf, Bpf = tmf[:, 0, :], tmf[:, 1, :], tmf[:, 2, :]
                    Vt = AR.get([128, 256], BF16)
                    S.copy(Vt, Vtf)
                    if os.environ.get("P2_CUT"):
                        continue
                    oT = AR.get([128, 2, 128], F32)
                    HD = range(4)
                    hps = [hd_ // 2 for hd_ in HD]
                    hhs = [hd_ % 2 for hd_ in HD]
                    prs = [slice(hh_ * 64, (hh_ + 1) * 64) for hh_ in hhs]
                    hcols = [slice(hd_ * 64, (hd_ + 1) * 64) for hd_ in HD]
                    Lk, Pk, Pb, Mneg, Lneg = [], [], [], [], []
                    for hd in HD:
                        hp, hh = hps[hd], hhs[hd]
                        ps1 = PSA[4] if hd % 2 == 0 else PSA[6]
                        ps2 = PSA[5]
                        S.mm(ps1[:, 0:128], khf[hp][hh], kaptf[:, hp, :])
                        S.mm(ps1[:, 128:256], kh[hp][hh], rt[:, hp, :])
                        S.mm(ps1[:, 256:384], bh[hp][hh], rt[:, hp, :])
                        S.mm(ps2[:, 0:128], bhf[hp][hh], kaptf[:, hp, :])
                        S.mm(ps2[:, 128:256], kaptf[:, hp, :], bhf[hp][hh])
                        Lk.append(AR.get([128, 128], F32))
                        Pk.append(AR.get([128, 128], BF16))
                        Pb.append(AR.get([128, 128], BF16))
                        Mneg.append(AR.get([128, 128], F32))
                        Lneg.append(AR.get([128, 128], F32))
                        S.stt(Mneg[hd], ps2[:, 0:128], -1.0, m_s, ALU.mult, ALU.mult)
                        S.stt(Lneg[hd], ps2[:, 128:256], -1.0, m_sT, ALU.mult, ALU.mult)
                        S.tt(Lk[hd], ps1[:, 0:128], m_s, ALU.mult)
                        S.tt(Pk[hd], ps1[:, 128:256], m_i, ALU.mult)
                        S.tt(Pb[hd], ps1[:, 256:384], m_i, ALU.mult)
                    Zs = [[AR.get([128, 128], F32) for _ in range(2)] for _ in HD]
                    ZTs = [[AR.get([128, 128], F32) for _ in range(2)] for _ in HD]
                    Ys = [[AR.get([128, 128], F32) for _ in range(2)] for _ in HD]
                    Ts = [[AR.get([128, 128], F32) for _ in range(2)] for _ in HD]
                    Pa = [AR.get([128, 128], F32) for _ in HD]
                    Pb_ = [AR.get([128, 128], F32) for _ in HD]
                    Mo = [AR.get([128, 128], F32) for _ in HD]
                    Lo = [AR.get([128, 128], F32) for _ in HD]
                    PB = [PSA[hd_] for hd_ in HD]
                    Z = [Zs[h_][0] for h_ in HD]
                    ZT = [ZTs[h_][0] for h_ in HD]
                    Yc = [Ys[h_][0] for h_ in HD]
                    Tc = [Ts[h_][0] for h_ in HD]
                    for hd in HD:
                        S.tt(Z[hd], Mneg[hd], B16, ALU.mult)
                        S.tt(ZT[hd], Lneg[hd], B16, ALU.mult)
                        S.tt(Yc[hd], Z[hd], identf, ALU.add)
                        S.tt(Tc[hd], ZT[hd], identf, ALU.add)
                    for lev in range(3):
                        nx = (lev + 1) % 2
                        for hd in HD:
                            S.mm(PB[hd][:, 0:128], ZT[hd], Z[hd])
                            S.mm(PB[hd][:, 128:256], Z[hd], ZT[hd])
                        for hd in HD:
                            S.copy(Zs[hd][nx], PB[hd][:, 0:128], eng="act")
                            S.copy(ZTs[hd][nx], PB[hd][:, 128:256], eng="act")
                        for hd in HD:
                            S.mm(PB[hd][:, 256:384], ZTs[hd][nx], Yc[hd])
                            S.mm(PB[hd][:, 384:512], Zs[hd][nx], Tc[hd])
                        for hd in HD:
                            S.tt(Ys[hd][nx], PB[hd][:, 256:384], Yc[hd], ALU.add)
                            S.tt(Ts[hd][nx], PB[hd][:, 384:512], Tc[hd], ALU.add)
                        for hd in HD:
                            Z[hd], ZT[hd], Yc[hd], Tc[hd] = Zs[hd][nx], ZTs[hd][nx], Ys[hd][nx], Ts[hd][nx]
                    yi = 1
                    for di, Dm in enumerate((D32, D64, D128)):
                        lastd = (di == 2)
                        for hd in HD:
                            S.tt(Lo[hd], Lneg[hd], Dm, ALU.mult)
                            if not lastd:
                                S.tt(Mo[hd], Mneg[hd], Dm, ALU.mult)
                        for hd in HD:
                            S.mm(PB[hd][:, 0:128], Lo[hd], Yc[hd])
                            if not lastd:
                                S.mm(PB[hd][:, 128:256], Mo[hd], Tc[hd])
                        for hd in HD:
                            S.copy(Pa[hd], PB[hd][:, 0:128], eng="act")
                            if not lastd:
                                S.copy(Pb_[hd], PB[hd][:, 128:256], eng="act")
                        for hd in HD:
                            S.mm(PB[hd][:, 256:384], Tc[hd], Pa[hd])
                            if not lastd:
                                S.mm(PB[hd][:, 384:512], Yc[hd], Pb_[hd])
                        yi = 1 - yi
                        for hd in HD:
                            S.tt(Ys[hd][yi], PB[hd][:, 256:384], Yc[hd], ALU.add)
                            if not lastd:
                                S.tt(Ts[hd][yi], PB[hd][:, 384:512], Tc[hd], ALU.add)
                        for hd in HD:
                            Yc[hd], Tc[hd] = Ys[hd][yi], Ts[hd][yi]
                    Y = Yc
                    Wf = [AR.get([128, 64], F32) for _ in HD]
                    Unf = [AR.get([128, 64], F32) for _ in HD]
                    Un = [AR.get([128, 64], BF16) for _ in HD]
                    dA = [AR.get([128, 64], F32) for _ in HD]
                    for hd in HD:
                        S.mm(PB[hd][:, 0:64], kaptf[:, hps[hd], :], Af[hd], start=True, stop=False)
                        S.mm(PB[hd][:, 0:64], Lk[hd], Vtf[:, hcols[hd]], start=False, stop=True)
                    for hd in HD:
                        S.copy(Wf[hd], PB[hd][:, 0:64], eng="act")
                    for hd in HD:
                        S.mm(PB[hd][:, 128:192], Y[hd], Wf[hd])
                    for hd in HD:
                        S.ts(Unf[hd], PB[hd][:, 128:192], -1.0, ALU.mult)
                    for hd in HD:
                        S.copy(Un[hd], Unf[hd], eng="act")
                    for hd in HD:
                        hp = hps[hd]
                        S.mm(PB[hd][0:64, 256:384], Ab[hd], rt[:, hp, :], start=True, stop=False)
                        S.mm(PB[hd][0:64, 256:384], Vt[:, hcols[hd]], Pk[hd], start=False, stop=False)
                        S.mm(PB[hd][0:64, 256:384], Un[hd], Pb[hd], start=False, stop=True)
                        S.mm(PB[hd][0:64, 384:448], Kpf[:, hcols[hd]], Vtf[:, hcols[hd]], start=True, stop=False)
                        S.mm(PB[hd][0:64, 384:448], Bpf[:, hcols[hd]], Unf[hd], start=False, stop=True)
                    for hd in HD:
                        pr = prs[hd]
                        S.copy(oT[pr, hps[hd], :], PB[hd][0:64, 256:384], eng="act")
                        S.copy(dA[hd][pr, :], PB[hd][0:64, 384:448], eng="act")
                    for hd in HD:
                        pr = prs[hd]
                        S.stt(Af[hd][pr, :], Af[hd][pr, :], gC[pr, hps[hd]:hps[hd] + 1], dA[hd][pr, :], ALU.mult, ALU.add)
                        S.copy(Ab[hd][pr, :], Af[hd][pr, :])
                    if d == 0:
                        S.dma(q.OF.v(0, 256, ta, ta + 128).rearrange("(c p) n -> p c n", p=128), oT, q="pool")
                    else:
                        of = AR.get([128, 2, 128], F32)
                        S.dma(of, q.OF.v(0, 256, ta, ta + 128).rearrange("(c p) n -> p c n", p=128))
                        y = AR.get([128, 2, 128], F32)
                        S.tt(y, oT, of, ALU.add)
                        y2 = AR.get([128, 2, 128], F32)
                        S.tt(y2, y, y, ALU.mult)
                        ksum = AR.get([128, 2, 128], F32)
                        S.tt(ksum, kd[0], kd[1], ALU.add)
                        S.tt(ksum, ksum, rT, ALU.mult)
                        for cc in range(2):
                            S.ts(ksum[:, cc, :], ksum[:, cc, :], cvcol(C_RK + cc), ALU.mult)
                        pg = PSA[0]
                        pg2 = PSA[1]
                        for cc in range(2):
                            S.mm(pg[:, cc * 128:(cc + 1) * 128], blk, y[:, cc, :])
                            S.mm(pg[:, 256 + cc * 128:256 + (cc + 1) * 128], blk, y2[:, cc, :])
                            S.mm(pg2[:, cc * 128:(cc + 1) * 128], blk, ksum[:, cc, :])
                        mu = AR.get([128, 2, 128], F32)
                        var = AR.get([128, 2, 128], F32)
                        S.ts(mu, pg[:, 0:256].rearrange("p (c n) -> p c n", c=2), 1.0 / 64, ALU.mult)
                        S.ts(var, pg[:, 256:512].rearrange("p (c n) -> p c n", c=2), 1.0 / 64, ALU.mult, 64e-5, ALU.add)
                        S.tt(y2, mu, mu, ALU.mult)
                        S.tt(var, var, y2, ALU.subtract)
                        S.act(var, var, AF.Sqrt)
                        S.recip(var, var)
                        S.tt(y, y, mu, ALU.subtract)
                        S.tt(y, y, var, ALU.mult)
                        for cc in range(2):
                            S.ts(y[:, cc, :], y[:, cc, :], cvcol(C_GNG + cc), ALU.mult, cvcol(C_GNB + cc), ALU.add)
                        S.tt(y2, pg2[:, 0:256].rearrange("p (c n) -> p c n", c=2), vT, ALU.mult)
                        S.tt(y, y, y2, ALU.add)
                        yb = AR.get([128, 2, 128], BF16)
                        S.tt(yb, y, gT, ALU.mult)
                        S.dma(q.MIXT.v(0, 256, ta, ta + 128).rearrange("(c p) n -> p c n", p=128), yb, q="pool")
                if not q.samp and not os.environ.get("NO_FINST"):
                    for hp in range(2):
                        apair = AR.get([128, 64], F32)
                        S.tt(apair, Af[2 * hp], Af[2 * hp + 1], ALU.add)
                        S.tr(PSA[0][0:64, hp * 128:(hp + 1) * 128], apair, identf)
                    so = AR.get([64, 4, 64], F32)
                    S.copy(so, PSA[0][0:64, 0:256].rearrange("v (h k) -> v h k", h=4))
                    S.dma(ost_o[q.pi, l, d].rearrange("h v k -> v h k"), so, q="pool")

        if stage <= 3:
            raise _Stop(S)
        S.new_epoch()
        for q in seqs:
            areset()
            Tk = q.Tk
            nkt = Tk // 128
            ckT = AR.get([128, Tk], BF16)
            S.dma(ckT, q.CKVT.v(0, 128, 0, Tk))
            Va = AR.get([128, nkt, 8, 65], BF16)
            S.memset(Va[:, :, :, 64:65], 1.0)
            for kt in range(nkt):
                ps = PSA[kt % 2]
                S.mm(ps[:, 0:512], ckT[:, kt * 128:(kt + 1) * 128],
                     KVUP.rearrange("p (h c) -> p h c", h=8)[:, :, 64:128])
                S.copy(Va[:, kt, :, 0:64], ps[:, 0:512].rearrange("p (h c) -> p h c", h=8), eng="act" if kt % 2 else "dve")
            NQ = min(512, q.T)
            mark = AR.off
            for h in range(8):
                S.barrier()
                AR.off = mark
                KhT = AR.get([96, Tk], BF16)
                S.dma(KhT[64:96, :], q.KRT.v(0, 32, 0, Tk))
                for k5 in range((Tk + 511) // 512):
                    n = min(512, Tk - k5 * 512)
                    ps = PSA[k5 % 2]
                    S.mm(ps[0:64, 0:n], KVUP[:, h * 128:h * 128 + 64], ckT[:, k5 * 512:k5 * 512 + n])
                    S.copy(KhT[0:64, k5 * 512:k5 * 512 + n], ps[0:64, 0:n], eng="act" if k5 % 2 else "dve")
                QhT = AR.get([96, q.T], BF16)
                S.dma(QhT, q.QT.v(h * 96, (h + 1) * 96, 0, q.T))
                PT = [AR.get([128, NQ], BF16) for _ in range(3)]
                nsub = NQ // 128
                obuf = [AR.get([128, nsub, 65], F32) for _ in range(2)]
                rc = AR.get([128, 4, 1], F32)
                ob = AR.get([128, 4, 64], BF16)
                ost = AR.get([64, 512], BF16)
                items = [(qt_, kt_) for qt_ in range(q.T // NQ) for kt_ in range(nkt)]
                SB = [PSA[0], PSA[1], PSA[6]]

                def score(ix):
                    qt_, kt_ = items[ix]
                    S.mm(SB[ix % 3][:, 0:NQ], KhT[:, kt_ * 128:(kt_ + 1) * 128], QhT[:, qt_ * NQ:(qt_ + 1) * NQ])

                for ix0 in range(min(2, len(items))):
                    score(ix0)
                for ix, (qt, kt) in enumerate(items):
                    if ix + 2 < len(items):
                        score(ix + 2)
                    pt = PT[ix % 3]
                    S.act(pt, SB[ix % 3][:, 0:NQ], AF.Exp, scale=MLA_SCALE)
                    for sub in range(nsub):
                        S.mm(PSA[2 + sub][:, 0:65], pt[:, sub * 128:(sub + 1) * 128], Va[:, kt, h, :],
                             start=(kt == 0), stop=(kt == nkt - 1))
                    if kt != nkt - 1:
                        continue
                    o = obuf[qt % 2]
                    for sub in range(nsub):
                        S.copy(o[:, sub, :], PSA[2 + sub][:, 0:65], eng="act" if sub % 2 else "dve")
                    S.recip(rc[:, 0:nsub, :], o[:, :, 64:65])
                    S.tt(ob[:, 0:nsub, :], o[:, :, 0:64], rc[:, 0:nsub, :].bc([128, nsub, 64]), ALU.mult)
                    for sub in range(nsub):
                        S.tr(PST[0:64, sub * 128:(sub + 1) * 128], ob[:, sub, :], ident)
                    S.copy(ost[:, 0:NQ], PST[0:64, 0:NQ], eng="act")
                    S.dma(q.MIXT.v(256 + h * 64, 256 + (h + 1) * 64, qt * NQ, (qt + 1) * NQ), ost[:, 0:NQ], q="pool")

        if stage <= 4:
            raise _Stop(S)
        last = (l == L - 1)
        for q in seqs:
            N = min(512, q.T)
            c = q.cond
            for bt in range(q.T // N):
                areset()
                t0 = bt * N
                xT = AR.get([128, 8, N], F32)
                S.dma(xT, q.XT.v(0, D, t0, t0 + N).rearrange("(c p) n -> p c n", p=128))
                mx = AR.get([128, 8, N], BF16)
                S.dma(mx, q.MIXT.v(0, D, t0, t0 + N).rearrange("(c p) n -> p c n", p=128))
                u = AR.get([128, 8, N], F32)
                x1 = AR.get([128, 8, N], F32)
                h2 = AR.get([128, 8, N], BF16)
                sqb = AR.get([128, N], F32)
                stat = AR.get([128, 2, N], F32)
                actT = AR.get([128, 22, N], BF16)

                def layer_norm(src_ps_fn, xin, gate_j, gcol, bcol, xout):
                    pss = PSA[2]
                    psq = PSA[3]
                    for dc in range(8):
                        ps = src_ps_fn(dc)
                        S.act(u[:, dc, :], xin[:, dc, :], AF.Identity, scale=ALPHA)
                        S.stt(u[:, dc, :], ps[:, 0:N], modc(gate_j, dc, c), u[:, dc, :], ALU.mult, ALU.add)
                        S.act(sqb, u[:, dc, :], AF.Square)
                        S.mm(pss[:, 0:N], onesf, u[:, dc, :], start=(dc == 0), stop=(dc == 7))
                        S.mm(psq[:, 0:N], onesf, sqb, start=(dc == 0), stop=(dc == 7))
                    S.ts(stat[:, 0, :], pss[:, 0:N], 1.0 / D, ALU.mult)
                    S.ts(stat[:, 1, :], psq[:, 0:N], 1.0 / D, ALU.mult, 1e-5, ALU.add)
                    S.tt(sqb, stat[:, 0, :], stat[:, 0, :], ALU.mult)
                    S.tt(stat[:, 1, :], stat[:, 1, :], sqb, ALU.subtract)
                    S.act(stat[:, 1, :], stat[:, 1, :], AF.Sqrt)
                    S.recip(stat[:, 1, :], stat[:, 1, :])
                    for dc in range(8):
                        S.tt(u[:, dc, :], u[:, dc, :], stat[:, 0, :], ALU.subtract)
                        S.tt(u[:, dc, :], u[:, dc, :], stat[:, 1, :], ALU.mult)
                        S.act(xout[:, dc, :], u[:, dc, :], AF.Identity, bias=cvcol(bcol + dc), scale=cvcol(gcol + dc))

                def proj1(dc):
                    ps = PSA[dc % 2]
                    for kc in range(8):
                        S.mm(ps[:, 0:N], WOUT[:, kc, dc * 128:(dc + 1) * 128], mx[:, kc, :], start=(kc == 0), stop=(kc == 7))
                    return ps

                layer_norm(proj1, xT, 2, C_L1G, C_L1B, x1)
                for dc in range(8):
                    S.act(h2[:, dc, :], x1[:, dc, :], AF.Identity, bias=modc(3, dc, c), scale=ON2[:, dc, c:c + 1])
                wring = [AR.get([128, 8, 256], BF16) for _ in range(2)]
                sil = [AR.get([128, N], F32) for _ in range(2)]
                for j in range(22):
                    wb = wring[j % 2]
                    S.dma(wb[:, :, 0:128], V(FWI_ap[l * D:(l + 1) * D, j * 128:(j + 1) * 128].rearrange("(c p) n -> p c n", p=128), FWI_b[l]))
                    S.dma(wb[:, :, 128:256], V(FWI_ap[l * D:(l + 1) * D, DFF + j * 128:DFF + (j + 1) * 128].rearrange("(c p) n -> p c n", p=128), FWI_b[l]))
                    pg = PSA[4] if j % 2 == 0 else PSA[2]
                    pu = PSA[5] if j % 2 == 0 else PSA[3]
                    for kc in range(8):
                        S.mm(pg[:, 0:N], wb[:, kc, 0:128], h2[:, kc, :], start=(kc == 0), stop=(kc == 7))
                    for kc in range(8):
                        S.mm(pu[:, 0:N], wb[:, kc, 128:256], h2[:, kc, :], start=(kc == 0), stop=(kc == 7))
                    sl = sil[j % 2]
                    S.act(sl, pg[:, 0:N], AF.Silu)
                    S.tt(actT[:, j, :], sl, pu[:, 0:N], ALU.mult)
                oring = [AR.get([128, 22, 128], BF16) for _ in range(2)]

                def proj2(dc):
                    ob = oring[dc % 2]
                    S.dma(ob, V(FWO_ap[l * DFF:(l + 1) * DFF, dc * 128:(dc + 1) * 128].rearrange("(j p) n -> p j n", p=128), FWO_b[l]))
                    ps = PSA[dc % 2]
                    for j in range(22):
                        S.mm(ps[:, 0:N], ob[:, j, :], actT[:, j, :], start=(j == 0), stop=(j == 21))
                    return ps

                layer_norm(proj2, x1, 5, C_L2G, C_L2B, xT)
                if not last and l < nlayers - 1:
                    S.dma(q.XT.v(0, D, t0, t0 + N).rearrange("(c p) n -> p c n", p=128), xT, q="pool")
                else:
                    dst = ys_o if q.samp else yp_o[q.pi * TP:(q.pi + 1) * TP, :]
                    ytl = [AR.get([128, D], F32) for _ in range(2)]
                    for sub in range(N // 128):
                        yt = ytl[sub % 2]
                        for half in range(2):
                            ps = PSA[4 + half]
                            for jj in range(4):
                                dc = half * 4 + jj
                                S.tr(ps[:, jj * 128:(jj + 1) * 128], xT[:, dc, sub * 128:(sub + 1) * 128], identf)
                            S.copy(yt[:, half * 512:(half + 1) * 512], ps[:, 0:512], eng="act" if half else "dve")
                        S.dma(dst[t0 + sub * 128:t0 + (sub + 1) * 128, :], yt, q="pool")
    stats = S.emit()
    return nc, S, stats


def _rope_tables():
    rows = TS // 64
    rr, cc = np.meshgrid(np.arange(rows), np.arange(64), indexing="ij")
    rr = rr.reshape(-1).astype(np.float32)
    cc = cc.reshape(-1).astype(np.float32)
    inv = (np.float32(10000.0) ** (-np.arange(8, dtype=np.float32) / np.float32(8))).astype(np.float32)
    ang = np.concatenate([rr[:, None] * inv, cc[:, None] * inv], -1).astype(np.float32)
    return np.cos(ang).astype(np.float32), np.sin(ang).astype(np.float32)


_CACHE = {}


def make_in_maps(inputs):
    f = lambda a: np.ascontiguousarray(np.asarray(a, dtype=np.float32))
    cos, sin = _rope_tables()
    shared = {}
    resh = {"ada_b": (L, 48, 128), "rw_conv": (L, 27, 128), "rw_w0": (L, 4, 128), "rw_w2": (L, 128, 256),
            "rw_a0": (L, 4, 128), "rw_a2": (L, 128, 256), "rw_kk": (L, 2, 128), "rw_ka": (L, 2, 128),
            "rw_rk": (L, 2, 128), "rw_gn_g": (L, 2, 128), "rw_gn_b": (L, 2, 128), "mla_q_norm": (L, 2, 128),
            "ln1_g": (L, 8, 128), "ln1_b": (L, 8, 128), "ln2_g": (L, 8, 128), "ln2_b": (L, 8, 128)}
    for name in ["ada_w", "ada_b", "w_in", "rw_conv", "rw_w0", "rw_w2", "rw_a0", "rw_a2", "rw_g2", "rw_kk", "rw_ka",
                 "rw_rk", "rw_gn_g", "rw_gn_b", "mla_q_norm", "mla_q_up", "mla_kv_norm", "mla_kv_up", "gm_norm_g",
                 "gm_norm_b", "gm_ws", "gm_bs", "w_out", "ln1_g", "ln1_b", "ffn_w_in", "ffn_w_out", "ln2_g", "ln2_b"]:
        a = f(inputs[name])
        if name in resh:
            a = a.reshape(resh[name])
        shared[name] = a
    shared["cos"] = cos
    shared["sin"] = sin
    shared["cc"] = f(inputs["c_ctx"]).reshape(8, 128)
    xs = f(inputs["x_sample"])
    xp = f(inputs["x_prompt"])
    maps = []
    for core in range(8):
        b = core % 4
        m = dict(shared)
        m["xs"] = xs[b]
        m["xp"] = xp[2 * core:2 * core + 2].reshape(2 * TP, D)
        m["cckv"] = f(inputs["cache_ckv"])[b]
        m["ckr"] = f(inputs["cache_krope"])[b]
        m["st"] = f(inputs["state_rwkv"])[b]
        m["cs"] = f(inputs["c"])[b].reshape(8, 128)
        maps.append(m)
    return maps


def kernel(**inputs):
    if "nc" not in _CACHE:
        _CACHE["nc"] = build()[0]
    nc = _CACHE["nc"]
    maps = make_in_maps(inputs)
    res = run_bass_kernel_spmd(nc, maps, core_ids=list(range(8)))
    R = res.results
    yp = np.concatenate([R[c]["yp"].reshape(2, TP, D) for c in range(8)], 0).astype(np.float32)
    ys = np.stack([R[b]["ys"] for b in range(4)], 0).astype(np.float32)
    ockv = np.concatenate([R[c]["ockv"] for c in range(8)], 0).astype(np.float32)
    okr = np.concatenate([R[c]["okr"] for c in range(8)], 0).astype(np.float32)
    ost = np.concatenate([R[c]["ost"] for c in range(8)], 0).astype(np.float32)
    return (yp, ys, ockv, okr, ost)
```

```python
import contextlib
import os
import numpy as np
import concourse.bass as bass
import concourse.mybir as mybir
from concourse.bass_utils import run_bass_kernel_spmd

F32 = mybir.dt.float32
BF16 = mybir.dt.bfloat16
AF = mybir.ActivationFunctionType
ALU = mybir.AluOpType
AX = mybir.AxisListType

N_DMA_SEMS = 10
ENGS = ("pe", "act", "dve", "pool", "sp")

L = 4
D = 1024
NIN = 2080
DFF = 2816
TS = 4096
TP = 256
PAST = 512
ALPHA = (2 * L) ** 0.25
MLA_SCALE = 96 ** -0.5
DEBUG = False
NLAYERS = L


class Buf:
    __slots__ = ("name", "lw", "rd", "psum")

    def __init__(self, name="", psum=False):
        self.name = name
        self.lw = None
        self.rd = []
        self.psum = psum


class V:
    __slots__ = ("ap", "bufs")

    def __init__(self, ap, bufs):
        self.ap = ap
        self.bufs = bufs

    def __getitem__(self, idx):
        return V(self.ap[idx], self.bufs)

    def rearrange(self, pat_, **kw):
        return V(self.ap.rearrange(pat_, **kw), self.bufs)

    def bc(self, shape):
        return V(self.ap.to_broadcast(list(shape)), self.bufs)

    def us(self, ax):
        return V(self.ap.unsqueeze(ax), self.bufs)

    @property
    def shape(self):
        return self.ap.shape


class Instr:
    __slots__ = ("eng", "idx", "fn", "deps", "dma", "needed", "rank", "dsem", "dval", "slot", "epoch")

    def __init__(self, eng, fn, dma):
        self.eng = eng
        self.fn = fn
        self.dma = dma
        self.deps = []
        self.needed = False
        self.rank = 0
        self.dsem = None
        self.dval = 0
        self.slot = None


class Sched:
    def __init__(self, nc):
        self.nc = nc
        self.streams = {e: [] for e in ENGS}
        self.stack = contextlib.ExitStack()
        self.ntiles = 0
        self.dlast = {}
        self.dcount = {}
        self.ndma = {e: 0 for e in ENGS}
        self.pending = {e: None for e in ENGS}
        self.lastc = {e: None for e in ENGS}
        self.epoch = 0

    def sb(self, shape, dt, name=None):
        self.ntiles += 1
        name = name or f"t{self.ntiles}"
        t = self.stack.enter_context(self.nc.sbuf_tensor(name, list(shape), dt))
        return V(t[:], [Buf(name)])

    def ps(self, shape, dt, name=None):
        self.ntiles += 1
        name = name or f"p{self.ntiles}"
        t = self.stack.enter_context(self.nc.psum_tensor(name, list(shape), dt))
        return V(t[:], [Buf(name, psum=True)])

    def dram(self, name, shape, dt, kind="Internal"):
        t = self.nc.dram_tensor(name, list(shape), dt, kind=kind)
        return V(t.ap(), [Buf(name)])

    def barrier(self):
        bar = [I for I in self.lastc.values() if I is not None] + list(self.dlast.values())
        for e in ENGS:
            self.pending[e] = list(bar)

    def new_epoch(self):
        self.barrier()
        if not os.environ.get("NO_EPOCH"):
            self.epoch += 1

    def add(self, eng, fn, reads, writes, dma=False):
        I = Instr(eng, fn, dma)
        I.epoch = self.epoch
        st = self.streams[eng]
        I.idx = len(st)
        st.append(I)
        deps = {}

        def dep(P):
            if P is None or P is I:
                return
            if (not P.dma) and P.eng == "pe" and eng == "pe" and not dma:
                return
            deps[id(P)] = P

        if self.pending[eng] is not None:
            for P in self.pending[eng]:
                dep(P)
            self.pending[eng] = None
        for v in reads:
            for b in v.bufs:
                dep(b.lw)
                if b.psum:
                    for r in b.rd:
                        if r.eng != eng:
                            dep(r)
        for v in writes:
            for b in v.bufs:
                dep(b.lw)
                for r in b.rd:
                    dep(r)
        for v in reads:
            for b in v.bufs:
                b.rd.append(I)
        for v in writes:
            for b in v.bufs:
                b.lw = I
                b.rd = []
        if dma:
            k = self.ndma[eng] % N_DMA_SEMS
            self.ndma[eng] += 1
            key = (eng, k)
            self.dcount[key] = self.dcount.get(key, 0) + 1
            I.slot = key
            I.dval = 16 * self.dcount[key]
            prev = self.dlast.get(key)
            if prev is not None:
                deps[id(prev)] = prev
            self.dlast[key] = I
        else:
            self.lastc[eng] = I
        I.deps = list(deps.values())
        return I

    def mm(self, out, lhsT, rhs, start=True, stop=True):
        return self.add("pe", lambda e: e.matmul(out.ap, lhsT.ap, rhs.ap, start=start, stop=stop),
                        [lhsT, rhs], [out])

    def tr(self, out, in_, ident):
        return self.add("pe", lambda e: e.transpose(out.ap, in_.ap, ident.ap), [in_, ident], [out])

    def act(self, out, in_, func, bias=None, scale=None, accum=None):
        kw = {}
        rd = [in_]
        wr = [out]
        if bias is not None:
            if isinstance(bias, V):
                kw["bias"] = bias.ap
                rd.append(bias)
            else:
                kw["bias"] = bias
        if scale is not None:
            if isinstance(scale, V):
                kw["scale"] = scale.ap
                rd.append(scale)
            else:
                kw["scale"] = scale
        if accum is not None:
            kw["accum_out"] = accum.ap
            wr.append(accum)
        return self.add("act", lambda e: e.activation(out.ap, in_.ap, func, **kw), rd, wr)

    def tt(self, out, a, b, op, eng="dve"):
        return self.add(eng, lambda e: e.tensor_tensor(out.ap, a.ap, b.ap, op), [a, b], [out])

    def ts(self, out, a, s1, op0, s2=None, op1=None, eng="dve"):
        rd = [a]
        a1 = s1.ap if isinstance(s1, V) else s1
        a2 = s2.ap if isinstance(s2, V) else s2
        if isinstance(s1, V):
            rd.append(s1)
        if isinstance(s2, V):
            rd.append(s2)
        kw = {}
        if op1 is not None:
            kw["op1"] = op1
        return self.add(eng, lambda e: e.tensor_scalar(out.ap, a.ap, a1, a2, op0, **kw), rd, [out])

    def stt(self, out, a, s, b, op0, op1, eng="dve"):
        rd = [a, b]
        sa = s.ap if isinstance(s, V) else s
        if isinstance(s, V):
            rd.append(s)
        return self.add(eng, lambda e: e.scalar_tensor_tensor(out.ap, a.ap, sa, b.ap, op0, op1), rd, [out])

    def copy(self, out, in_, eng="dve"):
        if eng == "act":
            return self.add("act", lambda e: e.copy(out.ap, in_.ap), [in_], [out])
        return self.add(eng, lambda e: e.tensor_copy(out.ap, in_.ap), [in_], [out])

    def memset(self, out, val, eng="dve"):
        return self.add(eng, lambda e: e.memset(out.ap, val), [], [out])

    def reduce(self, out, in_, op, eng="dve"):
        return self.add(eng, lambda e: e.tensor_reduce(out.ap, in_.ap, AX.X, op), [in_], [out])

    def recip(self, out, in_):
        return self.add("dve", lambda e: e.reciprocal(out.ap, in_.ap), [in_], [out])

    def scan(self, out, d0, d1, init, op0, op1):
        return self.add("dve", lambda e: e.tensor_tensor_scan(out.ap, d0.ap, d1.ap, init, op0, op1), [d0, d1], [out])

    def dma(self, out, in_, q="sp", slow=False):
        if slow:
            return self.add(q, lambda e: e.dma_start(out.ap, in_.ap, allow_slow_non_contiguous=True), [in_], [out], dma=True)
        return self.add(q, lambda e: e.dma_start(out.ap, in_.ap), [in_], [out], dma=True)

    def emit(self):
        nc = self.nc
        for e in ENGS:
            for I in self.streams[e]:
                for P in I.deps:
                    P.needed = True
        for e in ENGS:
            r = 0
            ep = 0
            for I in self.streams[e]:
                if I.epoch != ep:
                    ep = I.epoch
                    r = 0
                if not I.dma and I.needed:
                    r += 1
                I.rank = r
        sems = {}
        for e in ENGS:
            for ep in sorted({I.epoch for I in self.streams[e] if not I.dma and I.needed}):
                sems[(e, ep)] = self.stack.enter_context(nc.semaphore(f"c_{e}{ep}"))
        dsems = {}
        for key in self.dlast:
            dsems[key] = self.stack.enter_context(nc.semaphore(f"d_{key[0]}{key[1]}"))
        for e in ENGS:
            for I in self.streams[e]:
                if I.dma:
                    I.dsem = dsems[I.slot]
        block = self.stack.enter_context(nc.Block())
        stats = {"wait": 0, "ins": 0}
        dlast = self.dlast
        lastc = self.lastc

        def run(ename, eng):
            seen = {}
            for I in self.streams[ename]:
                for P in I.deps:
                    if P.dma:
                        if seen.get(P.slot, 0) >= P.dval:
                            continue
                        seen[P.slot] = P.dval
                        eng.wait_ge(P.dsem, P.dval)
                    else:
                        kk = (P.eng, P.epoch)
                        if seen.get(kk, 0) >= P.rank:
                            continue
                        seen[kk] = P.rank
                        eng.wait_ge(sems[kk], P.rank)
                    stats["wait"] += 1
                ins = I.fn(eng)
                stats["ins"] += 1
                if I.dma:
                    ins.then_inc(I.dsem, 16)
                elif I.needed:
                    ins.then_inc(sems[(ename, I.epoch)], 1)
            if ename == "sp":
                for key, I in dlast.items():
                    eng.wait_ge(I.dsem, I.dval)

        @block.tensor
        def _(e):
            run("pe", e)

        @block.scalar
        def _(e):
            run("act", e)

        @block.vector
        def _(e):
            run("dve", e)

        @block.gpsimd
        def _(e):
            run("pool", e)

        @block.sync
        def _(e):
            run("sp", e)

        return stats


class Arena:
    def __init__(self, S, nwords):
        self.v = S.sb([128, nwords], F32, "arena")
        self.off = 0
        self.n = nwords

    def reset(self):
        self.off = 0
        self.mode = None

    def start_record(self):
        self.rec = []
        self.mode = "rec"

    def start_replay(self):
        self.pos = 0
        self.mode = "rep"

    def stop(self):
        self.mode = None

    def get(self, shape, dt):
        if getattr(self, "mode", None) == "rep":
            v_, sh_, d_ = self.rec[self.pos]
            self.pos += 1
            assert sh_ == tuple(shape) and d_ == dt, ("arena replay mismatch", sh_, shape)
            return v_
        v_ = self._alloc(shape, dt)
        if getattr(self, "mode", None) == "rec":
            self.rec.append((v_, tuple(shape), dt))
        return v_

    def _alloc(self, shape, dt):
        free = int(np.prod(shape[1:]))
        words = free if dt == F32 else (free + 1) // 2
        assert self.off + words <= self.n, ("arena overflow", self.off, words, self.n)
        ap = self.v.ap[0:shape[0], self.off:self.off + words]
        if dt != F32:
            ap = ap.bitcast(dt)[:, 0:free]
        if len(shape) == 3:
            ap = ap.rearrange("p (a b) -> p a b", a=shape[1])
        elif len(shape) == 4:
            ap = ap.rearrange("p (a b c) -> p a b c", a=shape[1], b=shape[2])
        self.off += words
        return V(ap, [Buf()])


class DT:
    def __init__(self, S, name, rows, cols, dt, kind="Internal"):
        t = S.nc.dram_tensor(name, [rows, cols], dt, kind=kind)
        self.ap = t.ap()
        self.nb = (cols + 127) // 128
        self.bufs = [Buf(f"{name}.{i}") for i in range(self.nb)]

    def v(self, r0, r1, c0, c1):
        return V(self.ap[r0:r1, c0:c1], self.bufs[c0 // 128:(c1 - 1) // 128 + 1])


class _Stop(Exception):
    pass


def build(nlayers=NLAYERS, debug=DEBUG, stage=99):
    try:
        return _build(nlayers, debug, stage)
    except _Stop as e:
        S = e.args[0]
        return S.nc, S, S.emit()


def _build(nlayers, debug, stage):
    nc = bass.Bass("TRN2", target_bir_lowering=False)
    S = Sched(nc)
    dbgkind = "ExternalOutput" if debug else "Internal"

    def inp(name, shape):
        return S.dram(name, shape, F32, kind="ExternalInput")

    xs_d = inp("xs", [TS, D])
    xp_d = inp("xp", [2 * TP, D])
    cckv_d = inp("cckv", [L, PAST, 128])
    ckr_d = inp("ckr", [L, PAST, 32])
    st_d = inp("st", [L, 2, 4, 64, 64])
    cs_d = inp("cs", [8, 128])
    cc_d = inp("cc", [8, 128])
    cos_d = inp("cos", [TS, 16])
    sin_d = inp("sin", [TS, 16])
    W = {}
    for name, shape in [("ada_w", [L, D, 6 * D]), ("ada_b", [L, 48, 128]), ("w_in", [L, D, NIN]),
                        ("rw_conv", [L, 27, 128]), ("rw_w0", [L, 4, 128]), ("rw_w2", [L, 128, 256]),
                        ("rw_a0", [L, 4, 128]), ("rw_a2", [L, 128, 256]), ("rw_g2", [L, 128, 256]),
                        ("rw_kk", [L, 2, 128]), ("rw_ka", [L, 2, 128]), ("rw_rk", [L, 2, 128]),
                        ("rw_gn_g", [L, 2, 128]), ("rw_gn_b", [L, 2, 128]), ("mla_q_norm", [L, 2, 128]),
                        ("mla_q_up", [L, 256, 768]), ("mla_kv_norm", [L, 128]), ("mla_kv_up", [L, 128, 1024]),
                        ("gm_norm_g", [L, 256]), ("gm_norm_b", [L, 256]), ("gm_ws", [L, 4, 128, 128]),
                        ("gm_bs", [L, 4, 128]), ("w_out", [L, D, D]), ("ln1_g", [L, 8, 128]),
                        ("ln1_b", [L, 8, 128]), ("ffn_w_in", [L, D, 2 * DFF]), ("ffn_w_out", [L, DFF, D]),
                        ("ln2_g", [L, 8, 128]), ("ln2_b", [L, 8, 128])]:
        W[name] = inp(name, shape)

    def outp(name, shape):
        return S.dram(name, shape, F32, kind="ExternalOutput")

    ys_o = outp("ys", [TS, D])
    yp_o = outp("yp", [2 * TP, D])
    ockv_o = outp("ockv", [2, L, TP, 128])
    okr_o = outp("okr", [2, L, TP, 32])
    ost_o = outp("ost", [2, L, 2, 4, 64, 64])

    class Seq:
        pass

    seqs = []
    for i, (T, samp) in enumerate([(TS, True), (TP, False), (TP, False)]):
        q = Seq()
        q.i = i
        q.T = T
        q.samp = samp
        q.Tk = T + (PAST if samp else 0)
        q.cond = 0 if samp else 1
        q.pi = i - 1
        q.XT = DT(S, f"XT{i}", D, T, F32, dbgkind)
        q.ZRW = DT(S, f"ZRW{i}", 1152, T + 2, F32, dbgkind)
        q.QT = DT(S, f"QT{i}", 768, T, BF16, dbgkind)
        q.CKVT = DT(S, f"CKVT{i}", 128, q.Tk, BF16, dbgkind)
        q.KRT = DT(S, f"KRT{i}", 32, q.Tk, BF16, dbgkind)
        q.OF = DT(S, f"OF{i}", 256, T, F32, dbgkind)
        q.MIXT = DT(S, f"MIXT{i}", D, T, BF16, dbgkind)
        seqs.append(q)
    FWI_ap = S.nc.dram_tensor("FWI", [L * D, 2 * DFF], BF16, kind="Internal").ap()
    FWO_ap = S.nc.dram_tensor("FWO", [L * DFF, D], BF16, kind="Internal").ap()
    FWI_b = [[Buf() for _ in range(8)] for _ in range(L)]
    FWO_b = [[Buf() for _ in range(22)] for _ in range(L)]

    identf = S.sb([128, 128], F32, "identf")
    ident = S.sb([128, 128], BF16, "ident")
    onesf = S.sb([128, 128], F32, "onesf")
    blk = S.sb([128, 128], F32, "blk")
    masks = {}
    S.memset(onesf, 1.0)
    S.memset(blk, 0.0)
    S.memset(blk[0:64, 0:64], 1.0)
    S.memset(blk[64:128, 64:128], 1.0)

    def mkmask(name, pattern, cm, op):
        m = S.sb([128, 128], F32, name)
        S.add("pool", lambda e: e.memset(m.ap, 1.0), [], [m])
        S.add("pool", lambda e: e.affine_select(m.ap, m.ap, pattern, op, 0.0, base=0, channel_multiplier=cm), [m], [m])
        return m

    S.add("pool", lambda e: e.memset(identf.ap, 1.0), [], [identf])
    S.add("pool", lambda e: e.affine_select(identf.ap, identf.ap, [[-1, 128]], ALU.is_equal, 0.0, base=0, channel_multiplier=1), [identf], [identf])
    S.copy(ident, identf)
    masks["ut_s"] = mkmask("ut_s", [[1, 128]], -1, ALU.is_gt)
    masks["ut_i"] = mkmask("ut_i", [[1, 128]], -1, ALU.is_ge)
    masks["lt_s"] = mkmask("lt_s", [[-1, 128]], 1, ALU.is_gt)
    masks["lt_i"] = mkmask("lt_i", [[-1, 128]], 1, ALU.is_ge)
    def blockdiag(name, bs):
        nb_ = 128 // bs
        E = S.sb([nb_, 128], F32, name + "_e")
        S.add("pool", lambda e: e.memset(E.ap, 1.0), [], [E])
        S.add("pool", lambda e: e.affine_select(E.ap, E.ap, [[1, 128]], ALU.is_ge, 0.0, base=0, channel_multiplier=-bs), [E], [E])
        S.add("pool", lambda e: e.affine_select(E.ap, E.ap, [[-1, 128]], ALU.is_ge, 0.0, base=bs - 1, channel_multiplier=bs), [E], [E])
        B = S.sb([128, 128], F32, name)
        return E, B

    EB = [blockdiag("b16", 16), blockdiag("b32", 32), blockdiag("b64", 64)]
    zpad = S.sb([128, 9, 1], F32, "zpad")
    S.memset(zpad, 0.0)
    for q in seqs:
        break
        S.dma(q.ZRW.v(0, 1152, 0, 1).rearrange("(c p) n -> p c n", p=128), zpad, slow=True)
        S.dma(q.ZRW.v(0, 1152, q.T + 1, q.T + 2).rearrange("(c p) n -> p c n", p=128), zpad, slow=True)

    WIN = S.sb([128, 8, NIN], BF16, "WIN")
    WOUT = S.sb([128, 8, D], BF16, "WOUT")
    QUP = S.sb([128, 2, 768], BF16, "QUP")
    KVUP = S.sb([128, 1024], BF16, "KVUP")
    W2z = [S.sb([128, 256], BF16, f"W2z{d}") for d in range(2)]
    A2z = [S.sb([128, 256], BF16, f"A2z{d}") for d in range(2)]
    HM = S.sb([128, 2], F32, "HM")
    S.memset(HM, 0.0)
    S.memset(HM[0:64, 0:1], 1.0)
    S.memset(HM[64:128, 1:2], 1.0)
    G2 = S.sb([128, 256], BF16, "G2")
    WST = S.sb([128, 4, 128], BF16, "WST")
    CV = S.sb([128, 128], F32, "CV")
    GBS = S.sb([128, 4], F32, "GBS")
    KVN = S.sb([128, 128], F32, "KVN")
    GMG = S.sb([128, 256], F32, "GMG")
    GMB = S.sb([128, 256], F32, "GMB")
    MOD = S.sb([128, 48, 2], F32, "MOD")
    ON1 = S.sb([128, 8, 2], F32, "ON1")
    ON2 = S.sb([128, 8, 2], F32, "ON2")
    OMKA = S.sb([128, 2], F32, "OMKA")
    CT = S.sb([128, 16], F32, "CT")
    CTB = S.sb([128, 8, 2], BF16, "CTB")
    EPS6 = S.sb([128, 1], F32, "EPS6")
    S.memset(EPS6, 1e-6)

    AR = Arena(S, 32000)
    PSA = [S.ps([128, 512], F32, f"psa{i}") for i in range(7)]
    PST = S.ps([128, 1024], BF16, "pst")

    def cvcol(r):
        return CV[:, r:r + 1]

    for i_, (E_, B_) in enumerate(EB):
        S.mm(PSA[i_][:, 0:128], E_, E_)
        S.copy(B_, PSA[i_][:, 0:128])
    B16 = EB[0][1]
    D32 = S.sb([128, 128], F32, "d32")
    D64 = S.sb([128, 128], F32, "d64")
    D128 = S.sb([128, 128], F32, "d128")
    S.tt(D32, EB[1][1], EB[0][1], ALU.subtract)
    S.tt(D64, EB[2][1], EB[1][1], ALU.subtract)
    S.tt(D128, onesf, EB[2][1], ALU.subtract)

    def areset():
        S.barrier()
        AR.reset()

    areset()
    for l in range(nlayers):
        if os.environ.get("NO_CONV"):
            break
        for r in range(8):
            S.dma(V(FWI_ap[l * D + r * 128:l * D + (r + 1) * 128, :], [FWI_b[l][r]]), W["ffn_w_in"][l, r * 128:(r + 1) * 128, :], q="pool")
        for r in range(22):
            S.dma(V(FWO_ap[l * DFF + r * 128:l * DFF + (r + 1) * 128, :], [FWO_b[l][r]]), W["ffn_w_out"][l, r * 128:(r + 1) * 128, :], q="pool")

    xsrc = [xs_d, xp_d[0:TP, :], xp_d[TP:2 * TP, :]]
    for q in seqs:
        xt_all = None
        for t in range(q.T // 128):
            if t % 8 == 0:
                areset()
            xt = AR.get([128, D], F32)
            S.dma(xt, xsrc[q.i][t * 128:(t + 1) * 128, :])
            st = AR.get([128, 8, 128], F32)
            for half in range(2):
                ps = PSA[half]
                for j in range(4):
                    dc = half * 4 + j
                    S.tr(ps[:, j * 128:(j + 1) * 128], xt[:, dc * 128:(dc + 1) * 128], identf)
                S.copy(st[:, half * 4:half * 4 + 4, :], ps.rearrange("p (a b) -> p a b", a=4), eng="act" if half else "dve")
            S.dma(q.XT.v(0, D, t * 128, (t + 1) * 128).rearrange("(c p) n -> p c n", p=128), st, q="pool")

    if stage <= 0:
        raise _Stop(S)
    for l in range(nlayers):
        S.new_epoch()
        areset()
        stg = AR.get([128, 128], F32)
        S.memset(stg, 0.0)
        r = 0
        for name, n in [("ada_b", 48), ("rw_conv", 27), ("rw_w0", 4), ("rw_a0", 4), ("rw_kk", 2), ("rw_ka", 2),
                        ("rw_rk", 2), ("rw_gn_g", 2), ("rw_gn_b", 2), ("mla_q_norm", 2), ("ln1_g", 8),
                        ("ln1_b", 8), ("ln2_g", 8), ("ln2_b", 8)]:
            S.dma(stg[r:r + n, :], W[name][l])
            r += n
        S.tr(PSA[0][:, 0:128], stg, identf)
        S.copy(CV, PSA[0][:, 0:128])
        C_ADAB, C_CONV, C_W0, C_A0, C_KK, C_KA, C_RK, C_GNG, C_GNB, C_QN, C_L1G, C_L1B, C_L2G, C_L2B = \
            0, 48, 75, 79, 83, 85, 87, 89, 91, 93, 95, 103, 111, 119
        S.ts(OMKA, CV[:, C_KA:C_KA + 2], -1.0, ALU.mult, 1.0, ALU.add)
        stg2 = AR.get([128, 128], F32)
        S.memset(stg2, 0.0)
        S.dma(stg2[0:4, :], W["gm_bs"][l])
        S.dma(stg2[4:12, :], cs_d)
        S.dma(stg2[12:20, :], cc_d)
        S.tr(PSA[1][:, 0:128], stg2, identf)
        S.copy(GBS, PSA[1][:, 0:4])
        S.act(CT, PSA[1][:, 4:20], AF.Silu)
        S.copy(CTB[:, :, 0], CT[:, 0:8])
        S.copy(CTB[:, :, 1], CT[:, 8:16])
        S.dma(KVN, V(W["mla_kv_norm"].ap[l].partition_broadcast(128), W["mla_kv_norm"].bufs))
        S.dma(GMG, V(W["gm_norm_g"].ap[l].partition_broadcast(128), W["gm_norm_g"].bufs))
        S.dma(GMB, V(W["gm_norm_b"].ap[l].partition_broadcast(128), W["gm_norm_b"].bufs))
        for kc in range(8):
            S.dma(WIN[:, kc, :], W["w_in"][l, kc * 128:(kc + 1) * 128, :], q="pool")
        for kc in range(8):
            S.dma(WOUT[:, kc, :], W["w_out"][l, kc * 128:(kc + 1) * 128, :], q="pool")
        for kc in range(2):
            S.dma(QUP[:, kc, :], W["mla_q_up"][l, kc * 128:(kc + 1) * 128, :], q="pool")
        S.dma(KVUP, W["mla_kv_up"][l], q="pool")
        for d_ in range(2):
            S.memset(W2z[d_], 0.0)
            S.memset(A2z[d_], 0.0)
            S.dma(W2z[d_][d_ * 64:(d_ + 1) * 64, :], W["rw_w2"][l, d_ * 64:(d_ + 1) * 64, :], q="pool")
            S.dma(A2z[d_][d_ * 64:(d_ + 1) * 64, :], W["rw_a2"][l, d_ * 64:(d_ + 1) * 64, :], q="pool")
        S.dma(G2, W["rw_g2"][l], q="pool")
        wsf = AR.get([128, 4, 128], F32)
        S.dma(wsf, W["gm_ws"][l].rearrange("g p q -> p g q"))
        wsb = AR.get([128, 4, 128], BF16)
        S.copy(wsb, wsf)
        for g in range(4):
            S.tr(PST[:, g * 128:(g + 1) * 128], wsb[:, g, :], ident)
        S.copy(WST, PST[:, 0:512].rearrange("p (g q) -> p g q", g=4))
        for nt in range(12):
            aw = AR.get([128, 8, 512], BF16)
            for kc in range(8):
                S.dma(aw[:, kc, :], W["ada_w"][l, kc * 128:(kc + 1) * 128, nt * 512:(nt + 1) * 512], q="pool")
            for j in range(4):
                ec = nt * 4 + j
                ps = PSA[2 + (ec % 2)]
                for kc in range(8):
                    S.mm(ps[:, 0:2], aw[:, kc, j * 128:(j + 1) * 128], CTB[:, kc, :], start=(kc == 0), stop=(kc == 7))
                S.ts(MOD[:, ec, :], ps[:, 0:2], cvcol(C_ADAB + ec), ALU.add)
        S.ts(ON1, MOD[:, 8:16, :], 1.0, ALU.add)
        S.ts(ON2, MOD[:, 32:40, :], 1.0, ALU.add)

        def modc(j, dc, c):
            return MOD[:, j * 8 + dc, c:c + 1]

        if stage <= 1:
            raise _Stop(S)
        for q in seqs:
            N = min(512, q.T)
            c = q.cond
            for bt in range(q.T // N):
                areset()
                t0 = bt * N
                xT = AR.get([128, 8, N], F32)
                S.dma(xT, q.XT.v(0, D, t0, t0 + N).rearrange("(c p) n -> p c n", p=128))
                hT = AR.get([128, 8, N], BF16)
                for dc in range(8):
                    S.act(hT[:, dc, :], xT[:, dc, :], AF.Identity, bias=modc(0, dc, c), scale=ON1[:, dc, c:c + 1])
                zst = AR.get([128, 9, N], F32)
                for ch in range(9):
                    ps = PSA[ch % 2]
                    for kc in range(8):
                        S.mm(ps[:, 0:N], WIN[:, kc, ch * 128:(ch + 1) * 128], hT[:, kc, :], start=(kc == 0), stop=(kc == 7))
                    S.copy(zst[:, ch, :], ps[:, 0:N], eng="act" if ch % 2 else "dve")
                S.dma(q.ZRW.v(0, 1152, 1 + t0, 1 + t0 + N).rearrange("(c p) n -> p c n", p=128), zst, q="pool")
                for sub in range(N // 128):
                    ta = t0 + sub * 128
                    hs = hT[:, :, sub * 128:(sub + 1) * 128]
                    psm = PSA[2]
                    psg = PSA[3]
                    for kc in range(8):
                        S.mm(psm[:, 0:416], hs[:, kc, :], WIN[:, kc, 1152:1568], start=(kc == 0), stop=(kc == 7))
                    for kc in range(8):
                        S.mm(psg[:, 0:512], hs[:, kc, :], WIN[:, kc, 1568:2080], start=(kc == 0), stop=(kc == 7))
                    zm = AR.get([128, 416], F32)
                    S.copy(zm, psm[:, 0:416], eng="act")
                    junk = AR.get([128, 256], F32)
                    ss = AR.get([128, 2], F32)
                    S.memset(ss, 0.0)
                    S.act(junk[:, 0:256], zm[:, 0:256], AF.Square, accum=ss[:, 0:1])
                    S.act(junk[:, 0:128], zm[:, 256:384], AF.Square, accum=ss[:, 1:2])
                    S.ts(ss[:, 0:1], ss[:, 0:1], 1.0 / 256, ALU.mult, 1e-6, ALU.add)
                    S.ts(ss[:, 1:2], ss[:, 1:2], 1.0 / 128, ALU.mult, 1e-6, ALU.add)
                    S.act(ss, ss, AF.Sqrt)
                    S.recip(ss, ss)
                    zqn = AR.get([128, 256], BF16)
                    S.ts(zqn, zm[:, 0:256], ss[:, 0:1], ALU.mult)
                    for cc in range(2):
                        S.tr(PST[:, cc * 128:(cc + 1) * 128], zqn[:, cc * 128:(cc + 1) * 128], ident)
                    zqT = AR.get([128, 2, 128], BF16)
                    for cc in range(2):
                        S.act(zqT[:, cc, :], PST[:, cc * 128:(cc + 1) * 128], AF.Identity, scale=cvcol(C_QN + cc))
                    qf = AR.get([128, 8, 96], F32)
                    for a in range(2):
                        ps = PSA[4 + a]
                        for kc in range(2):
                            S.mm(ps[:, 0:384], zqT[:, kc, :], QUP[:, kc, a * 384:(a + 1) * 384], start=(kc == 0), stop=(kc == 1))
                        S.copy(qf[:, a * 4:(a + 1) * 4, :], ps[:, 0:384].rearrange("p (h j) -> p h j", h=4), eng="act")
                    qb = AR.get([128, 8, 96], BF16)
                    krf = zm[:, 384:416]
                    krb = AR.get([128, 32], BF16)
                    if q.samp:
                        cs_t = AR.get([128, 2, 16], F32)
                        S.dma(cs_t[:, 0, :], cos_d[ta:ta + 128, :])
                        S.dma(cs_t[:, 1, :], sin_d[ta:ta + 128, :])
                        cosb = cs_t[:, 0:1, :].bc([128, 8, 16])
                        sinb = cs_t[:, 1:2, :].bc([128, 8, 16])
                        S.copy(qb[:, :, 0:64], qf[:, :, 0:64])
                        t1 = AR.get([128, 8, 16], F32)
                        t2 = AR.get([128, 8, 16], F32)
                        S.tt(t1, qf[:, :, 64:80], cosb, ALU.mult)
                        S.tt(t2, qf[:, :, 80:96], sinb, ALU.mult)
                        S.tt(qb[:, :, 64:80], t1, t2, ALU.subtract)
                        S.tt(t1, qf[:, :, 64:80], sinb, ALU.mult)
                        S.tt(t2, qf[:, :, 80:96], cosb, ALU.mult)
                        S.tt(qb[:, :, 80:96], t1, t2, ALU.add)
                        S.tt(t1[:, 0, :], krf[:, 0:16], cs_t[:, 0, :], ALU.mult)
                        S.tt(t2[:, 0, :], krf[:, 16:32], cs_t[:, 1, :], ALU.mult)
                        S.tt(krb[:, 0:16], t1[:, 0, :], t2[:, 0, :], ALU.subtract)
                        S.tt(t1[:, 0, :], krf[:, 0:16], cs_t[:, 1, :], ALU.mult)
                        S.tt(t2[:, 0, :], krf[:, 16:32], cs_t[:, 0, :], ALU.mult)
                        S.tt(krb[:, 16:32], t1[:, 0, :], t2[:, 0, :], ALU.add)
                    else:
                        S.copy(qb, qf)
                        S.copy(krb, krf)
                        S.dma(okr_o[q.pi, l, ta:ta + 128, :], krf, q="pool")
                    for h in range(8):
                        S.tr(PST[0:96, h * 128:(h + 1) * 128], qb[:, h, :], ident)
                    qst = AR.get([96, 8, 128], BF16)
                    S.copy(qst, PST[0:96, :].rearrange("p (h n) -> p h n", h=8), eng="act")
                    S.dma(q.QT.v(0, 768, ta, ta + 128).rearrange("(h j) n -> j h n", j=96), qst, q="pool")
                    ckv = AR.get([128, 128], F32)
                    S.stt(ckv, zm[:, 256:384], ss[:, 1:2], KVN, ALU.mult, ALU.mult)
                    if not q.samp:
                        S.dma(ockv_o[q.pi, l, ta:ta + 128, :], ckv, q="pool")
                    ckb = AR.get([128, 128], BF16)
                    S.copy(ckb, ckv)
                    S.tr(PST[:, 0:128], ckb, ident)
                    S.tr(PST[0:32, 128:256], krb, ident)
                    kst = AR.get([128, 256], BF16)
                    S.copy(kst[:, 0:128], PST[:, 0:128])
                    S.copy(kst[0:32, 128:256], PST[0:32, 128:256])
                    S.dma(q.CKVT.v(0, 128, ta, ta + 128), kst[:, 0:128], q="pool")
                    S.dma(q.KRT.v(0, 32, ta, ta + 128), kst[0:32, 128:256], q="pool")
                    g0 = AR.get([128, 512], F32)
                    S.copy(g0, psg[:, 0:512], eng="act")
                    g1 = AR.get([128, 512], F32)
                    S.tt(g1, g0, g0, ALU.mult)
                    S.ts(g1, g1, 0.044715, ALU.mult, 1.0, ALU.add)
                    S.tt(g1, g1, g0, ALU.mult)
                    S.act(g1, g1, AF.Sigmoid, scale=1.5957691216057308)
                    S.tt(g0, g0, g1, ALU.mult)
                    vf = g0[:, 256:512].rearrange("p (g c) -> p g c", g=4)
                    sm = AR.get([128, 4], F32)
                    sq = AR.get([128, 4], F32)
                    S.reduce(sm, vf, ALU.add)
                    S.tt(g1[:, 0:256], g0[:, 256:512], g0[:, 256:512], ALU.mult)
                    S.reduce(sq, g1[:, 0:256].rearrange("p (g c) -> p g c", g=4), ALU.add)
                    S.ts(sm, sm, 1.0 / 64, ALU.mult)
                    S.ts(sq, sq, 1.0 / 64, ALU.mult, 1e-5, ALU.add)
                    m2 = AR.get([128, 4], F32)
                    S.tt(m2, sm, sm, ALU.mult)
                    S.tt(sq, sq, m2, ALU.subtract)
                    S.act(sq, sq, AF.Sqrt)
                    S.recip(sq, sq)
                    vn = g1[:, 256:512].rearrange("p (g c) -> p g c", g=4)
                    S.tt(vn, vf, sm.us(2).bc([128, 4, 64]), ALU.subtract)
                    S.tt(vn, vn, sq.us(2).bc([128, 4, 64]), ALU.mult)
                    S.tt(g1[:, 256:512], g1[:, 256:512], GMG, ALU.mult)
                    vnb = AR.get([128, 256], BF16)
                    S.tt(vnb, g1[:, 256:512], GMB, ALU.add)
                    pss = PSA[4]
                    for g in range(4):
                        S.mm(pss[:, g * 128:g * 128 + 64], WST[:, g, :], vnb[:, g * 64:(g + 1) * 64])
                    sg = g1[:, 0:256].rearrange("p (g c) -> p g c", g=4)
                    S.tt(sg, pss[:, 0:512].rearrange("p (g c) -> p g c", g=4)[:, :, 0:64], GBS.us(2).bc([128, 4, 64]), ALU.add)
                    ygb = AR.get([128, 256], BF16)
                    S.tt(ygb, g0[:, 0:256], g1[:, 0:256], ALU.mult)
                    for cc in range(2):
                        S.tr(PST[:, 512 + cc * 128:512 + (cc + 1) * 128], ygb[:, cc * 128:(cc + 1) * 128], ident)
                    gst = AR.get([128, 2, 128], BF16)
                    S.copy(gst, PST[:, 512:768].rearrange("p (c n) -> p c n", c=2))
                    S.dma(q.MIXT.v(768, 1024, ta, ta + 128).rearrange("(c p) n -> p c n", p=128), gst, q="pool")
            if q.samp:
                areset()
                for kt in range(PAST // 128):
                    ck = AR.get([128, 128], F32)
                    kr = AR.get([128, 32], F32)
                    S.dma(ck, cckv_d[l, kt * 128:(kt + 1) * 128, :])
                    S.dma(kr, ckr_d[l, kt * 128:(kt + 1) * 128, :])
                    ckb = AR.get([128, 128], BF16)
                    krb = AR.get([128, 32], BF16)
                    S.copy(ckb, ck)
                    S.copy(krb, kr)
                    S.tr(PST[:, 0:128], ckb, ident)
                    S.tr(PST[0:32, 128:256], krb, ident)
                    kst = AR.get([128, 256], BF16)
                    S.copy(kst[:, 0:128], PST[:, 0:128])
                    S.copy(kst[0:32, 128:256], PST[0:32, 128:256])
                    S.dma(q.CKVT.v(0, 128, TS + kt * 128, TS + (kt + 1) * 128), kst[:, 0:128], q="pool")
                    S.dma(q.KRT.v(0, 32, TS + kt * 128, TS + (kt + 1) * 128), kst[0:32, 128:256], q="pool")

        if stage <= 2:
            raise _Stop(S)
        for q in seqs:
            nch = q.T // 128
            if os.environ.get("P2_PROMPTS") and q.samp:
                continue
            for d in range(2):
                areset()
                Af = [AR.get([128, 64], F32) for _ in range(4)]
                Ab = [AR.get([128, 64], BF16) for _ in range(4)]
                for h_ in range(4):
                    S.memset(Af[h_], 0.0)
                if q.samp:
                    sv = AR.get([64, 4, 64], F32)
                    S.dma(sv, st_d[l, d].rearrange("h v k -> v h k"))
                    for hp in range(2):
                        S.tr(PSA[0][:, hp * 64:(hp + 1) * 64], sv[:, 2 * hp:2 * hp + 2, :].rearrange("v h k -> v (h k)"), identf[0:64, 0:64])
                        for hh in range(2):
                            pr = slice(hh * 64, (hh + 1) * 64)
                            S.copy(Af[2 * hp + hh][pr, :], PSA[0][pr, hp * 64:(hp + 1) * 64])
                for h_ in range(4):
                    S.copy(Ab[h_], Af[h_])
                mark = AR.off
                order = range(nch) if d == 0 else range(nch - 1, -1, -1)
                m_s = masks["ut_s"] if d == 0 else masks["lt_s"]
                m_i = masks["ut_i"] if d == 0 else masks["lt_i"]
                m_sT = masks["lt_s"] if d == 0 else masks["ut_s"]
                for ci, cidx in enumerate(order):
                    if ci == 0:
                        AR.start_record()
                    else:
                        AR.start_replay()
                    if os.environ.get("P2_CUT") == "0":
                        continue
                    ta = cidx * 128
                    zw = AR.get([128, 9, 130], F32)
                    lo = 0 if cidx > 0 else 1
                    hi = 130 if cidx < nch - 1 else 129
                    if lo:
                        S.memset(zw[:, :, 0:1], 0.0)
                    if hi == 129:
                        S.memset(zw[:, :, 129:130], 0.0)
                    S.dma(zw[:, :, lo:hi], q.ZRW.v(0, 1152, ta + lo, ta + hi).rearrange("(c p) n -> p c n", p=128))
                    zc = AR.get([128, 9, 128], F32)
                    for ch in range(9):
                        eng = "dve"
                        S.ts(zc[:, ch, :], zw[:, ch, 0:128], cvcol(C_CONV + ch), ALU.mult, eng=eng)
                        S.stt(zc[:, ch, :], zw[:, ch, 1:129], cvcol(C_CONV + 9 + ch), zc[:, ch, :], ALU.mult, ALU.add, eng=eng)
                        S.stt(zc[:, ch, :], zw[:, ch, 2:130], cvcol(C_CONV + 18 + ch), zc[:, ch, :], ALU.mult, ALU.add, eng=eng)
                    if os.environ.get("P2_CUT") == "1":
                        continue
                    rT = zc[:, 0:2, :]
                    kT = zc[:, 2:4, :]
                    vT = zc[:, 4:6, :]
                    txw = AR.get([128, 128], BF16)
                    S.act(txw, zc[:, 6, :], AF.Tanh)
                    xab = AR.get([128, 128], BF16)
                    S.copy(xab, zc[:, 7, :])
                    sxg = AR.get([128, 128], BF16)
                    S.act(sxg, zc[:, 8, :], AF.Sigmoid)
                    lw = AR.get([128, 2, 128], F32)
                    av = [AR.get([128, 2, 128], F32) for _ in range(2)]
                    gT = AR.get([128, 2, 128], F32)
                    dirs = [d] if d == 0 else [0, 1]
                    for cc in range(2):
                        ps = PSA[cc]
                        S.mm(ps[:, 0:128], W2z[d][:, cc * 128:(cc + 1) * 128], txw)
                        for dd in dirs:
                            S.mm(ps[:, 128 + dd * 128:256 + dd * 128], A2z[dd][:, cc * 128:(cc + 1) * 128], xab)
                        S.mm(ps[:, 384:512], G2[:, cc * 128:(cc + 1) * 128], sxg)
                        S.act(lw[:, cc, :], ps[:, 0:128], AF.Sigmoid, bias=cvcol(C_W0 + d * 2 + cc))
                        for dd in dirs:
                            S.act(av[dd][:, cc, :], ps[:, 128 + dd * 128:256 + dd * 128], AF.Sigmoid, bias=cvcol(C_A0 + dd * 2 + cc))
                        S.copy(gT[:, cc, :], ps[:, 384:512])
                    S.ts(lw, lw, -0.6065306597126334, ALU.mult)
                    if os.environ.get("P2_CUT") == "2":
                        continue
                    kap = AR.get([128, 2, 128], F32)
                    k2 = AR.get([128, 2, 128], F32)
                    for cc in range(2):
                        S.ts(kap[:, cc, :], kT[:, cc, :], cvcol(C_KK + cc), ALU.mult)
                    S.tt(k2, kap, kap, ALU.mult)
                    for cc in range(2):
                        S.mm(PSA[2][:, cc * 128:(cc + 1) * 128], blk, k2[:, cc, :])
                    S.ts(k2, PSA[2][:, 0:256].rearrange("p (c n) -> p c n", c=2), 1e-24, ALU.max)
                    S.act(k2, k2, AF.Sqrt)
                    S.recip(k2, k2)
                    S.tt(kap, kap, k2, ALU.mult)
                    kd = [None, None]
                    for dd in dirs:
                        kd[dd] = AR.get([128, 2, 128], F32)
                        for cc in range(2):
                            S.ts(kd[dd][:, cc, :], av[dd][:, cc, :], cvcol(C_KA + cc), ALU.mult, OMKA[:, cc:cc + 1], ALU.add)
                        S.tt(kd[dd], kd[dd], kT, ALU.mult)
                    bd = AR.get([128, 2, 128], F32)
                    S.tt(bd, kap, av[d], ALU.mult)
                    if os.environ.get("P2_CUT") == "3":
                        continue
                    cum = AR.get([128, 2, 128], F32)
                    for cc in range(2):
                        S.scan(cum[:, cc, :], onesf, lw[:, cc, :], 0.0, ALU.mult, ALU.add)
                    tot = cum[:, :, 127:128]
                    gi = AR.get([128, 2, 128], F32)
                    ge = AR.get([128, 2, 128], F32)
                    if d == 0:
                        S.copy(gi, cum, eng="act")
                        S.tt(ge, cum, lw, ALU.subtract)
                    else:
                        S.tt(ge, tot.bc([128, 2, 128]), cum, ALU.subtract)
                        S.tt(gi, ge, lw, ALU.add)
                    e_i = AR.get([128, 2, 128], F32)
                    e_e = AR.get([128, 2, 128], F32)
                    e_n = AR.get([128, 2, 128], F32)
                    e_c = AR.get([128, 2, 128], F32)
                    gC = AR.get([128, 2], F32)
                    S.act(e_i, gi, AF.Exp)
                    S.act(e_e, ge, AF.Exp)
                    S.act(e_n, gi, AF.Exp, scale=-1.0)
                    for cc in range(2):
                        S.act(e_c[:, cc, :], gi[:, cc, :], AF.Exp, scale=-1.0, bias=tot[:, cc, :])
                    S.act(gC, tot[:, :, 0], AF.Exp)
                    kaptf = AR.get([128, 2, 128], F32)
                    kapt = AR.get([128, 2, 128], BF16)
                    rt = AR.get([128, 2, 128], BF16)
                    khf = [[AR.get([128, 128], F32) for _ in range(2)] for _ in range(2)]
                    bhf = [[AR.get([128, 128], F32) for _ in range(2)] for _ in range(2)]
                    kh = [[AR.get([128, 128], BF16) for _ in range(2)] for _ in range(2)]
                    bh = [[AR.get([128, 128], BF16) for _ in range(2)] for _ in range(2)]
                    khpf = AR.get([128, 2, 128], F32)
                    bhpf = AR.get([128, 2, 128], F32)
                    S.tt(kaptf, kap, e_e, ALU.mult)
                    S.copy(kapt, kaptf, eng="act")
                    S.tt(rt, rT, e_i, ALU.mult)
                    for hp_ in range(2):
                        for hh_ in range(2):
                            S.stt(khf[hp_][hh_], kd[d][:, hp_, :], HM[:, hh_:hh_ + 1], e_n[:, hp_, :], ALU.mult, ALU.mult)
                            S.stt(bhf[hp_][hh_], bd[:, hp_, :], HM[:, hh_:hh_ + 1], e_n[:, hp_, :], ALU.mult, ALU.mult)
                            S.copy(kh[hp_][hh_], khf[hp_][hh_], eng="act")
                            S.copy(bh[hp_][hh_], bhf[hp_][hh_], eng="act")
                    S.tt(khpf, kd[d], e_c, ALU.mult)
                    S.tt(bhpf, bd, e_c, ALU.mult)
                    tmf = AR.get([128, 3, 256], F32)
                    for j, src in enumerate((vT, khpf, bhpf)):
                        for cc in range(2):
                            ix = j * 2 + cc
                            pst_ = PSA[0] if ix < 4 else PSA[1]
                            col = (ix % 4) * 128
                            S.tr(pst_[:, col:col + 128], src[:, cc, :], identf)
                    S.copy(tmf[:, 0:2, :], PSA[0][:, 0:512].rearrange("p (j c) -> p j c", j=2), eng="act")
                    S.copy(tmf[:, 2, :], PSA[1][:, 0:256], eng="act")
                    Vtf, Kpf, Bpf = tmf[:, 0, :], tmf[:, 1, :], tmf[:, 2, :]
                    Vt = AR.get([128, 256], BF16)
                    S.copy(Vt, Vtf)
                    if os.environ.get("P2_CUT"):
                        continue
                    oT = AR.get([128, 2, 128], F32)
                    HD = range(4)
                    hps = [hd_ // 2 for hd_ in HD]
                    hhs = [hd_ % 2 for hd_ in HD]
                    prs = [slice(hh_ * 64, (hh_ + 1) * 64) for hh_ in hhs]
                    hcols = [slice(hd_ * 64, (hd_ + 1) * 64) for hd_ in HD]
                    Lk, Pk, Pb, Mneg, Lneg = [], [], [], [], []
                    for hd in HD:
                        hp, hh = hps[hd], hhs[hd]
                        ps1 = PSA[4] if hd % 2 == 0 else PSA[6]
                        ps2 = PSA[5]
                        S.mm(ps1[:, 0:128], khf[hp][hh], kaptf[:, hp, :])
                        S.mm(ps1[:, 128:256], kh[hp][hh], rt[:, hp, :])
                        S.mm(ps1[:, 256:384], bh[hp][hh], rt[:, hp, :])
                        S.mm(ps2[:, 0:128], bhf[hp][hh], kaptf[:, hp, :])
                        S.mm(ps2[:, 128:256], kaptf[:, hp, :], bhf[hp][hh])
                        Lk.append(AR.get([128, 128], F32))
                        Pk.append(AR.get([128, 128], BF16))
                        Pb.append(AR.get([128, 128], BF16))
                        Mneg.append(AR.get([128, 128], F32))
                        Lneg.append(AR.get([128, 128], F32))
                        S.stt(Mneg[hd], ps2[:, 0:128], -1.0, m_s, ALU.mult, ALU.mult)
                        S.stt(Lneg[hd], ps2[:, 128:256], -1.0, m_sT, ALU.mult, ALU.mult)
                        S.tt(Lk[hd], ps1[:, 0:128], m_s, ALU.mult)
                        S.tt(Pk[hd], ps1[:, 128:256], m_i, ALU.mult)
                        S.tt(Pb[hd], ps1[:, 256:384], m_i, ALU.mult)
                    Zs = [[AR.get([128, 128], F32) for _ in range(2)] for _ in HD]
                    ZTs = [[AR.get([128, 128], F32) for _ in range(2)] for _ in HD]
                    Ys = [[AR.get([128, 128], F32) for _ in range(2)] for _ in HD]
                    Ts = [[AR.get([128, 128], F32) for _ in range(2)] for _ in HD]
                    Pa = [AR.get([128, 128], F32) for _ in HD]
                    Pb_ = [AR.get([128, 128], F32) for _ in HD]
                    Mo = [AR.get([128, 128], F32) for _ in HD]
                    Lo = [AR.get([128, 128], F32) for _ in HD]
                    PB = [PSA[hd_] for hd_ in HD]
                    Z = [Zs[h_][0] for h_ in HD]
                    ZT = [ZTs[h_][0] for h_ in HD]
                    Yc = [Ys[h_][0] for h_ in HD]
                    Tc = [Ts[h_][0] for h_ in HD]
                    for hd in HD:
                        S.tt(Z[hd], Mneg[hd], B16, ALU.mult)
                        S.tt(ZT[hd], Lneg[hd], B16, ALU.mult)
                        S.tt(Yc[hd], Z[hd], identf, ALU.add)
                        S.tt(Tc[hd], ZT[hd], identf, ALU.add)
                    for lev in range(3):
                        nx = (lev + 1) % 2
                        for hd in HD:
                            S.mm(PB[hd][:, 0:128], ZT[hd], Z[hd])
                            S.mm(PB[hd][:, 128:256], Z[hd], ZT[hd])
                        for hd in HD:
                            S.copy(Zs[hd][nx], PB[hd][:, 0:128], eng="act")
                            S.copy(ZTs[hd][nx], PB[hd][:, 128:256], eng="act")
                        for hd in HD:
                            S.mm(PB[hd][:, 256:384], ZTs[hd][nx], Yc[hd])
                            S.mm(PB[hd][:, 384:512], Zs[hd][nx], Tc[hd])
                        for hd in HD:
                            S.tt(Ys[hd][nx], PB[hd][:, 256:384], Yc[hd], ALU.add)
                            S.tt(Ts[hd][nx], PB[hd][:, 384:512], Tc[hd], ALU.add)
                        for hd in HD:
                            Z[hd], ZT[hd], Yc[hd], Tc[hd] = Zs[hd][nx], ZTs[hd][nx], Ys[hd][nx], Ts[hd][nx]
                    yi = 1
                    for di, Dm in enumerate((D32, D64, D128)):
                        lastd = (di == 2)
                        for hd in HD:
                            S.tt(Lo[hd], Lneg[hd], Dm, ALU.mult)
                            if not lastd:
                                S.tt(Mo[hd], Mneg[hd], Dm, ALU.mult)
                        for hd in HD:
                            S.mm(PB[hd][:, 0:128], Lo[hd], Yc[hd])
                            if not lastd:
                                S.mm(PB[hd][:, 128:256], Mo[hd], Tc[hd])
                        for hd in HD:
                            S.copy(Pa[hd], PB[hd][:, 0:128], eng="act")
                            if not lastd:
                                S.copy(Pb_[hd], PB[hd][:, 128:256], eng="act")
                        for hd in HD:
                            S.mm(PB[hd][:, 256:384], Tc[hd], Pa[hd])
                            if not lastd:
                                S.mm(PB[hd][:, 384:512], Yc[hd], Pb_[hd])
                        yi = 1 - yi
                        for hd in HD:
                            S.tt(Ys[hd][yi], PB[hd][:, 256:384], Yc[hd], ALU.add)
                            if not lastd:
                                S.tt(Ts[hd][yi], PB[hd][:, 384:512], Tc[hd], ALU.add)
                        for hd in HD:
                            Yc[hd], Tc[hd] = Ys[hd][yi], Ts[hd][yi]
                    Y = Yc
                    Wf = [AR.get([128, 64], F32) for _ in HD]
                    Unf = [AR.get([128, 64], F32) for _ in HD]
                    Un = [AR.get([128, 64], BF16) for _ in HD]
                    dA = [AR.get([128, 64], F32) for _ in HD]
                    for hd in HD:
                        S.mm(PB[hd][:, 0:64], kaptf[:, hps[hd], :], Af[hd], start=True, stop=False)
                        S.mm(PB[hd][:, 0:64], Lk[hd], Vtf[:, hcols[hd]], start=False, stop=True)
                    for hd in HD:
                        S.copy(Wf[hd], PB[hd][:, 0:64], eng="act")
                    for hd in HD:
                        S.mm(PB[hd][:, 128:192], Y[hd], Wf[hd])
                    for hd in HD:
                        S.ts(Unf[hd], PB[hd][:, 128:192], -1.0, ALU.mult)
                    for hd in HD:
                        S.copy(Un[hd], Unf[hd], eng="act")
                    for hd in HD:
                        hp = hps[hd]
                        S.mm(PB[hd][0:64, 256:384], Ab[hd], rt[:, hp, :], start=True, stop=False)
                        S.mm(PB[hd][0:64, 256:384], Vt[:, hcols[hd]], Pk[hd], start=False, stop=False)
                        S.mm(PB[hd][0:64, 256:384], Un[hd], Pb[hd], start=False, stop=True)
                        S.mm(PB[hd][0:64, 384:448], Kpf[:, hcols[hd]], Vtf[:, hcols[hd]], start=True, stop=False)
                        S.mm(PB[hd][0:64, 384:448], Bpf[:, hcols[hd]], Unf[hd], start=False, stop=True)
                    for hd in HD:
                        pr = prs[hd]
                        S.copy(oT[pr, hps[hd], :], PB[hd][0:64, 256:384], eng="act")
                        S.copy(dA[hd][pr, :], PB[hd][0:64, 384:448], eng="act")
                    for hd in HD:
                        pr = prs[hd]
                        S.stt(Af[hd][pr, :], Af[hd][pr, :], gC[pr, hps[hd]:hps[hd] + 1], dA[hd][pr, :], ALU.mult, ALU.add)
                        S.copy(Ab[hd][pr, :], Af[hd][pr, :])
                    if d == 0:
                        S.dma(q.OF.v(0, 256, ta, ta + 128).rearrange("(c p) n -> p c n", p=128), oT, q="pool")
                    else:
                        of = AR.get([128, 2, 128], F32)
                        S.dma(of, q.OF.v(0, 256, ta, ta + 128).rearrange("(c p) n -> p c n", p=128))
                        y = AR.get([128, 2, 128], F32)
                        S.tt(y, oT, of, ALU.add)
                        y2 = AR.get([128, 2, 128], F32)
                        S.tt(y2, y, y, ALU.mult)
                        ksum = AR.get([128, 2, 128], F32)
                        S.tt(ksum, kd[0], kd[1], ALU.add)
                        S.tt(ksum, ksum, rT, ALU.mult)
                        for cc in range(2):
                            S.ts(ksum[:, cc, :], ksum[:, cc, :], cvcol(C_RK + cc), ALU.mult)
                        pg = PSA[0]
                        pg2 = PSA[1]
                        for cc in range(2):
                            S.mm(pg[:, cc * 128:(cc + 1) * 128], blk, y[:, cc, :])
                            S.mm(pg[:, 256 + cc * 128:256 + (cc + 1) * 128], blk, y2[:, cc, :])
                            S.mm(pg2[:, cc * 128:(cc + 1) * 128], blk, ksum[:, cc, :])
                        mu = AR.get([128, 2, 128], F32)
                        var = AR.get([128, 2, 128], F32)
                        S.ts(mu, pg[:, 0:256].rearrange("p (c n) -> p c n", c=2), 1.0 / 64, ALU.mult)
                        S.ts(var, pg[:, 256:512].rearrange("p (c n) -> p c n", c=2), 1.0 / 64, ALU.mult, 64e-5, ALU.add)
                        S.tt(y2, mu, mu, ALU.mult)
                        S.tt(var, var, y2, ALU.subtract)
                        S.act(var, var, AF.Sqrt)
                        S.recip(var, var)
                        S.tt(y, y, mu, ALU.subtract)
                        S.tt(y, y, var, ALU.mult)
                        for cc in range(2):
                            S.ts(y[:, cc, :], y[:, cc, :], cvcol(C_GNG + cc), ALU.mult, cvcol(C_GNB + cc), ALU.add)
                        S.tt(y2, pg2[:, 0:256].rearrange("p (c n) -> p c n", c=2), vT, ALU.mult)
                        S.tt(y, y, y2, ALU.add)
                        yb = AR.get([128, 2, 128], BF16)
                        S.tt(yb, y, gT, ALU.mult)
                        S.dma(q.MIXT.v(0, 256, ta, ta + 128).rearrange("(c p) n -> p c n", p=128), yb, q="pool")
                AR.stop()
                if not q.samp and not os.environ.get("NO_FINST"):
                    for hp in range(2):
                        apair = AR.get([128, 64], F32)
                        S.tt(apair, Af[2 * hp], Af[2 * hp + 1], ALU.add)
                        S.tr(PSA[0][0:64, hp * 128:(hp + 1) * 128], apair, identf)
                    so = AR.get([64, 4, 64], F32)
                    S.copy(so, PSA[0][0:64, 0:256].rearrange("v (h k) -> v h k", h=4))
                    S.dma(ost_o[q.pi, l, d].rearrange("h v k -> v h k"), so, q="pool")

        if stage <= 3:
            raise _Stop(S)
        S.new_epoch()
        for q in seqs:
            areset()
            Tk = q.Tk
            nkt = Tk // 128
            ckT = AR.get([128, Tk], BF16)
            S.dma(ckT, q.CKVT.v(0, 128, 0, Tk))
            Va = AR.get([128, nkt, 8, 65], BF16)
            S.memset(Va[:, :, :, 64:65], 1.0)
            for kt in range(nkt):
                ps = PSA[kt % 2]
                S.mm(ps[:, 0:512], ckT[:, kt * 128:(kt + 1) * 128],
                     KVUP.rearrange("p (h c) -> p h c", h=8)[:, :, 64:128])
                S.copy(Va[:, kt, :, 0:64], ps[:, 0:512].rearrange("p (h c) -> p h c", h=8), eng="act" if kt % 2 else "dve")
            NQ = min(512, q.T)
            mark = AR.off
            for h in range(8):
                S.barrier()
                AR.off = mark
                KhT = AR.get([96, Tk], BF16)
                S.dma(KhT[64:96, :], q.KRT.v(0, 32, 0, Tk))
                for k5 in range((Tk + 511) // 512):
                    n = min(512, Tk - k5 * 512)
                    ps = PSA[k5 % 2]
                    S.mm(ps[0:64, 0:n], KVUP[:, h * 128:h * 128 + 64], ckT[:, k5 * 512:k5 * 512 + n])
                    S.copy(KhT[0:64, k5 * 512:k5 * 512 + n], ps[0:64, 0:n], eng="act" if k5 % 2 else "dve")
                QhT = AR.get([96, q.T], BF16)
                S.dma(QhT, q.QT.v(h * 96, (h + 1) * 96, 0, q.T))
                PT = [AR.get([128, NQ], BF16) for _ in range(3)]
                nsub = NQ // 128
                obuf = [AR.get([128, nsub, 65], F32) for _ in range(2)]
                rc = AR.get([128, 4, 1], F32)
                ob = AR.get([128, 4, 64], BF16)
                ost = AR.get([64, 512], BF16)
                items = [(qt_, kt_) for qt_ in range(q.T // NQ) for kt_ in range(nkt)]
                SB = [PSA[0], PSA[1], PSA[6]]

                def score(ix):
                    qt_, kt_ = items[ix]
                    S.mm(SB[ix % 3][:, 0:NQ], KhT[:, kt_ * 128:(kt_ + 1) * 128], QhT[:, qt_ * NQ:(qt_ + 1) * NQ])

                for ix0 in range(min(2, len(items))):
                    score(ix0)
                for ix, (qt, kt) in enumerate(items):
                    if ix + 2 < len(items):
                        score(ix + 2)
                    pt = PT[ix % 3]
                    S.act(pt, SB[ix % 3][:, 0:NQ], AF.Exp, scale=MLA_SCALE)
                    for sub in range(nsub):
                        S.mm(PSA[2 + sub][:, 0:65], pt[:, sub * 128:(sub + 1) * 128], Va[:, kt, h, :],
                             start=(kt == 0), stop=(kt == nkt - 1))
                    if kt != nkt - 1:
                        continue
                    o = obuf[qt % 2]
                    for sub in range(nsub):
                        S.copy(o[:, sub, :], PSA[2 + sub][:, 0:65], eng="act" if sub % 2 else "dve")
                    S.recip(rc[:, 0:nsub, :], o[:, :, 64:65])
                    S.tt(ob[:, 0:nsub, :], o[:, :, 0:64], rc[:, 0:nsub, :].bc([128, nsub, 64]), ALU.mult)
                    for sub in range(nsub):
                        S.tr(PST[0:64, sub * 128:(sub + 1) * 128], ob[:, sub, :], ident)
                    S.copy(ost[:, 0:NQ], PST[0:64, 0:NQ], eng="act")
                    S.dma(q.MIXT.v(256 + h * 64, 256 + (h + 1) * 64, qt * NQ, (qt + 1) * NQ), ost[:, 0:NQ], q="pool")

        if stage <= 4:
            raise _Stop(S)
        last = (l == L - 1)
        for q in seqs:
            N = min(512, q.T)
            c = q.cond
            for bt in range(q.T // N):
                areset()
                t0 = bt * N
                xT = AR.get([128, 8, N], F32)
                S.dma(xT, q.XT.v(0, D, t0, t0 + N).rearrange("(c p) n -> p c n", p=128))
                mx = AR.get([128, 8, N], BF16)
                S.dma(mx, q.MIXT.v(0, D, t0, t0 + N).rearrange("(c p) n -> p c n", p=128))
                u = AR.get([128, 8, N], F32)
                x1 = AR.get([128, 8, N], F32)
                h2 = AR.get([128, 8, N], BF16)
                sqb = AR.get([128, N], F32)
                stat = AR.get([128, 2, N], F32)
                actT = AR.get([128, 22, N], BF16)

                def layer_norm(src_ps_fn, xin, gate_j, gcol, bcol, xout):
                    pss = PSA[2]
                    psq = PSA[3]
                    for dc in range(8):
                        ps = src_ps_fn(dc)
                        S.act(u[:, dc, :], xin[:, dc, :], AF.Identity, scale=ALPHA)
                        S.stt(u[:, dc, :], ps[:, 0:N], modc(gate_j, dc, c), u[:, dc, :], ALU.mult, ALU.add)
                        S.act(sqb, u[:, dc, :], AF.Square)
                        S.mm(pss[:, 0:N], onesf, u[:, dc, :], start=(dc == 0), stop=(dc == 7))
                        S.mm(psq[:, 0:N], onesf, sqb, start=(dc == 0), stop=(dc == 7))
                    S.ts(stat[:, 0, :], pss[:, 0:N], 1.0 / D, ALU.mult)
                    S.ts(stat[:, 1, :], psq[:, 0:N], 1.0 / D, ALU.mult, 1e-5, ALU.add)
                    S.tt(sqb, stat[:, 0, :], stat[:, 0, :], ALU.mult)
                    S.tt(stat[:, 1, :], stat[:, 1, :], sqb, ALU.subtract)
                    S.act(stat[:, 1, :], stat[:, 1, :], AF.Sqrt)
                    S.recip(stat[:, 1, :], stat[:, 1, :])
                    for dc in range(8):
                        S.tt(u[:, dc, :], u[:, dc, :], stat[:, 0, :], ALU.subtract)
                        S.tt(u[:, dc, :], u[:, dc, :], stat[:, 1, :], ALU.mult)
                        S.act(xout[:, dc, :], u[:, dc, :], AF.Identity, bias=cvcol(bcol + dc), scale=cvcol(gcol + dc))

                def proj1(dc):
                    ps = PSA[dc % 2]
                    for kc in range(8):
                        S.mm(ps[:, 0:N], WOUT[:, kc, dc * 128:(dc + 1) * 128], mx[:, kc, :], start=(kc == 0), stop=(kc == 7))
                    return ps

                layer_norm(proj1, xT, 2, C_L1G, C_L1B, x1)
                for dc in range(8):
                    S.act(h2[:, dc, :], x1[:, dc, :], AF.Identity, bias=modc(3, dc, c), scale=ON2[:, dc, c:c + 1])
                wring = [AR.get([128, 8, 256], BF16) for _ in range(2)]
                sil = [AR.get([128, N], F32) for _ in range(2)]
                for j in range(22):
                    wb = wring[j % 2]
                    S.dma(wb[:, :, 0:128], V(FWI_ap[l * D:(l + 1) * D, j * 128:(j + 1) * 128].rearrange("(c p) n -> p c n", p=128), FWI_b[l]))
                    S.dma(wb[:, :, 128:256], V(FWI_ap[l * D:(l + 1) * D, DFF + j * 128:DFF + (j + 1) * 128].rearrange("(c p) n -> p c n", p=128), FWI_b[l]))
                    pg = PSA[4] if j % 2 == 0 else PSA[2]
                    pu = PSA[5] if j % 2 == 0 else PSA[3]
                    for kc in range(8):
                        S.mm(pg[:, 0:N], wb[:, kc, 0:128], h2[:, kc, :], start=(kc == 0), stop=(kc == 7))
                    for kc in range(8):
                        S.mm(pu[:, 0:N], wb[:, kc, 128:256], h2[:, kc, :], start=(kc == 0), stop=(kc == 7))
                    sl = sil[j % 2]
                    S.act(sl, pg[:, 0:N], AF.Silu)
                    S.tt(actT[:, j, :], sl, pu[:, 0:N], ALU.mult)
                oring = [AR.get([128, 22, 128], BF16) for _ in range(2)]

                def proj2(dc):
                    ob = oring[dc % 2]
                    S.dma(ob, V(FWO_ap[l * DFF:(l + 1) * DFF, dc * 128:(dc + 1) * 128].rearrange("(j p) n -> p j n", p=128), FWO_b[l]))
                    ps = PSA[dc % 2]
                    for j in range(22):
                        S.mm(ps[:, 0:N], ob[:, j, :], actT[:, j, :], start=(j == 0), stop=(j == 21))
                    return ps

                layer_norm(proj2, x1, 5, C_L2G, C_L2B, xT)
                if not last and l < nlayers - 1:
                    S.dma(q.XT.v(0, D, t0, t0 + N).rearrange("(c p) n -> p c n", p=128), xT, q="pool")
                else:
                    dst = ys_o if q.samp else yp_o[q.pi * TP:(q.pi + 1) * TP, :]
                    ytl = [AR.get([128, D], F32) for _ in range(2)]
                    for sub in range(N // 128):
                        yt = ytl[sub % 2]
                        for half in range(2):
                            ps = PSA[4 + half]
                            for jj in range(4):
                                dc = half * 4 + jj
                                S.tr(ps[:, jj * 128:(jj + 1) * 128], xT[:, dc, sub * 128:(sub + 1) * 128], identf)
                            S.copy(yt[:, half * 512:(half + 1) * 512], ps[:, 0:512], eng="act" if half else "dve")
                        S.dma(dst[t0 + sub * 128:t0 + (sub + 1) * 128, :], yt, q="pool")
    stats = S.emit()
    return nc, S, stats


def _rope_tables():
    rows = TS // 64
    rr, cc = np.meshgrid(np.arange(rows), np.arange(64), indexing="ij")
    rr = rr.reshape(-1).astype(np.float32)
    cc = cc.reshape(-1).astype(np.float32)
    inv = (np.float32(10000.0) ** (-np.arange(8, dtype=np.float32) / np.float32(8))).astype(np.float32)
    ang = np.concatenate([rr[:, None] * inv, cc[:, None] * inv], -1).astype(np.float32)
    return np.cos(ang).astype(np.float32), np.sin(ang).astype(np.float32)


_CACHE = {}


def make_in_maps(inputs):
    f = lambda a: np.ascontiguousarray(np.asarray(a, dtype=np.float32))
    cos, sin = _rope_tables()
    shared = {}
    resh = {"ada_b": (L, 48, 128), "rw_conv": (L, 27, 128), "rw_w0": (L, 4, 128), "rw_w2": (L, 128, 256),
            "rw_a0": (L, 4, 128), "rw_a2": (L, 128, 256), "rw_kk": (L, 2, 128), "rw_ka": (L, 2, 128),
            "rw_rk": (L, 2, 128), "rw_gn_g": (L, 2, 128), "rw_gn_b": (L, 2, 128), "mla_q_norm": (L, 2, 128),
            "ln1_g": (L, 8, 128), "ln1_b": (L, 8, 128), "ln2_g": (L, 8, 128), "ln2_b": (L, 8, 128)}
    for name in ["ada_w", "ada_b", "w_in", "rw_conv", "rw_w0", "rw_w2", "rw_a0", "rw_a2", "rw_g2", "rw_kk", "rw_ka",
                 "rw_rk", "rw_gn_g", "rw_gn_b", "mla_q_norm", "mla_q_up", "mla_kv_norm", "mla_kv_up", "gm_norm_g",
                 "gm_norm_b", "gm_ws", "gm_bs", "w_out", "ln1_g", "ln1_b", "ffn_w_in", "ffn_w_out", "ln2_g", "ln2_b"]:
        a = f(inputs[name])
        if name in resh:
            a = a.reshape(resh[name])
        shared[name] = a
    shared["cos"] = cos
    shared["sin"] = sin
    shared["cc"] = f(inputs["c_ctx"]).reshape(8, 128)
    xs = f(inputs["x_sample"])
    xp = f(inputs["x_prompt"])
    maps = []
    for core in range(8):
        b = core % 4
        m = dict(shared)
        m["xs"] = xs[b]
        m["xp"] = xp[2 * core:2 * core + 2].reshape(2 * TP, D)
        m["cckv"] = f(inputs["cache_ckv"])[b]
        m["ckr"] = f(inputs["cache_krope"])[b]
        m["st"] = f(inputs["state_rwkv"])[b]
        m["cs"] = f(inputs["c"])[b].reshape(8, 128)
        maps.append(m)
    return maps


def kernel(**inputs):
    if "nc" not in _CACHE:
        _CACHE["nc"] = build()[0]
    nc = _CACHE["nc"]
    maps = make_in_maps(inputs)
    res = run_bass_kernel_spmd(nc, maps, core_ids=list(range(8)))
    R = res.results
    yp = np.concatenate([R[c]["yp"].reshape(2, TP, D) for c in range(8)], 0).astype(np.float32)
    ys = np.stack([R[b]["ys"] for b in range(4)], 0).astype(np.float32)
    ockv = np.concatenate([R[c]["ockv"] for c in range(8)], 0).astype(np.float32)
    okr = np.concatenate([R[c]["okr"] for c in range(8)], 0).astype(np.float32)
    ost = np.concatenate([R[c]["ost"] for c in range(8)], 0).astype(np.float32)
    return (yp, ys, ockv, okr, ost)
```
